# Optimizing a Trainium2 kernel written in Bass

```python
import math
import jax, jax.numpy as jnp
from jax import lax
import numpy as np

D_MODEL = 1024
BATCH = 8
SEQ = 2048
DEPTH = 2
DEC_BATCH = 128
DEC_SEQ = 1
PAST_LEN = 2048
PAGE_SIZE = 128

HEAD_DIM = D_MODEL // 16
HEADS_PER_GROUP = 4
ATTN_GROUPS = ((128, 1), (512, 4), (2048, 16))
N_GROUPS = len(ATTN_GROUPS)
N_HEADS = N_GROUPS * HEADS_PER_GROUP
ATTN_WIDTH = N_HEADS * HEAD_DIM
POOL_WINDOWS = (2, 4, 8, 16)
POOL_WIDTH = D_MODEL // 4
POOL_GROUP_DIM = POOL_WIDTH // len(POOL_WINDOWS)
POOL_STATE = max(POOL_WINDOWS) - 1
D_FF = ((8 * D_MODEL // 3 + 127) // 128) * 128
N_MOD = 9
IN_WIDTH = 3 * ATTN_WIDTH + POOL_WIDTH + 2 * D_MODEL
EPS = 1e-6

kernel_name = "hybrid_dilated_pool_decoder_step"


def alibi_slopes():
    h = np.arange(1, N_HEADS + 1, dtype=np.float32)
    return jnp.asarray(np.power(2.0, -8.0 * h / N_HEADS).astype(np.float32))


def rms_norm(x, g):
    xf = x.astype(jnp.float32)
    y = xf * lax.rsqrt(jnp.mean(xf * xf, axis=-1, keepdims=True) + EPS)
    return (y * g.astype(jnp.float32)).astype(x.dtype)


def adaln_mod(c, w, b):
    m = jax.nn.silu(c) @ w + b
    return m.reshape(c.shape[0], N_MOD, D_MODEL)


def modulate(x, g, shift, scale):
    return rms_norm(x, g) * (1 + scale[:, None, :]) + shift[:, None, :]


def swiglu(h, w1, w2):
    gate, up = jnp.split(h @ w1, 2, axis=-1)
    return (jax.nn.silu(gate) * up) @ w2


def ffn_sublayer(x, mod, i, g, w1, w2):
    h = modulate(x, g, mod[:, 3 * i], mod[:, 3 * i + 1])
    return x + 0.5 * mod[:, 3 * i + 2][:, None, :] * swiglu(h, w1, w2)


def project_in(h, w_in, qn, kn):
    b, s, _ = h.shape
    z = h @ w_in
    cuts = [ATTN_WIDTH, 2 * ATTN_WIDTH, 3 * ATTN_WIDTH,
            3 * ATTN_WIDTH + POOL_WIDTH, 3 * ATTN_WIDTH + POOL_WIDTH + D_MODEL]
    q, k, v, u, ga, gp = jnp.split(z, cuts, axis=-1)
    q = rms_norm(q.reshape(b, s, N_HEADS, HEAD_DIM), qn)
    k = rms_norm(k.reshape(b, s, N_HEADS, HEAD_DIM), kn)
    v = v.reshape(b, s, N_HEADS, HEAD_DIM)
    return q, k, v, u, ga, gp


def band_dilated_attention(q, k, v, window, dil, slopes):
    b, s, h, dh = q.shape
    wsub = window // dil
    L = s // dil
    nb = -(-L // wsub)
    Lp = nb * wsub
    pad_q = ((0, 0), (0, Lp - L), (0, 0), (0, 0), (0, 0))
    pad_k = ((0, 0), (wsub, Lp - L), (0, 0), (0, 0), (0, 0))
    qs = jnp.pad(q.reshape(b, L, dil, h, dh), pad_q).reshape(b, nb, wsub, dil, h, dh)

    def key_blocks(t):
        tb = jnp.pad(t.reshape(b, L, dil, h, dh), pad_k).reshape(b, nb + 1, wsub, dil, h, dh)
        return jnp.concatenate([tb[:, :-1], tb[:, 1:]], axis=2)

    kk = key_blocks(k)
    vv = key_blocks(v)
    sc = jnp.einsum('bnqrhd,bnkrhd->bnrhqk', qs, kk,
                    preferred_element_type=jnp.float32) / math.sqrt(dh)
    qi = jnp.arange(wsub)
    kj = jnp.arange(2 * wsub)
    diff = wsub + qi[:, None] - kj[None, :]
    lk = (jnp.arange(nb)[:, None] - 1) * wsub + kj[None, :]
    valid = ((diff >= 0) & (diff <= wsub))[None] & (lk >= 0)[:, None, :]
    bias = -slopes[:, None, None] * (diff * dil).astype(jnp.float32)[None]
    sc = jnp.where(valid[None, :, None, None], sc + bias[None, None, None], -jnp.inf)
    lse = jax.nn.logsumexp(sc, axis=-1)
    p = jnp.exp(sc - lse[..., None])
    o = jnp.einsum('bnrhqk,bnkrhd->bnqrhd', p.astype(v.dtype), vv)
    o = o.reshape(b, Lp, dil, h, dh)[:, :L].reshape(b, s, h, dh)
    lse = lse.transpose(0, 1, 4, 2, 3).reshape(b, Lp, dil, h)[:, :L].reshape(b, s, h)
    return o, lse


def dilated_attention_step(q, k_all, v_all, window, dil, slopes, n_past):
    t_new = q.shape[1]
    dh = q.shape[-1]
    wsub = window // dil
    offs = dil * jnp.arange(wsub + 1)
    idx = (n_past + jnp.arange(t_new))[:, None] - offs[None, :]
    valid = idx >= 0
    idxc = jnp.maximum(idx, 0)
    kg = k_all[:, idxc]
    vg = v_all[:, idxc]
    sc = jnp.einsum('bthd,btkhd->bthk', q, kg,
                    preferred_element_type=jnp.float32) / math.sqrt(dh)
    bias = -slopes[:, None] * offs.astype(jnp.float32)[None, :]
    sc = jnp.where(valid[None, :, None, :], sc + bias[None, None], -jnp.inf)
    lse = jax.nn.logsumexp(sc, axis=-1)
    p = jnp.exp(sc - lse[..., None])
    o = jnp.einsum('bthk,btkhd->bthd', p.astype(v_all.dtype), vg)
    return o, lse


def pool_mixer(u_ext, pos_ext, n_out, pool_w, pool_scale):
    uf = u_ext.astype(jnp.float32)
    cs = jnp.cumsum(uf, axis=1)
    outs = []
    for gi, w in enumerate(POOL_WINDOWS):
        sl = slice(gi * POOL_GROUP_DIM, (gi + 1) * POOL_GROUP_DIM)
        c = cs[..., sl]
        c_prev = jnp.pad(c, ((0, 0), (w, 0), (0, 0)))[:, :c.shape[1]]
        cnt = jnp.minimum(w, pos_ext + 1).astype(jnp.float32)[None, :, None]
        d = (c - c_prev) / cnt - uf[..., sl]
        outs.append(d[:, -n_out:].astype(u_ext.dtype) @ pool_w[gi])
    return jnp.concatenate(outs, axis=-1) * pool_scale


def merge_out(o_groups, lse_groups, pool_y, ga, gp, w_oa, w_op, w_out):
    b, s = pool_y.shape[:2]
    alpha = jax.nn.softmax(jnp.stack(lse_groups, axis=2), axis=2)
    o = jnp.concatenate([o_g * alpha[:, :, g, :, None].astype(o_g.dtype)
                         for g, o_g in enumerate(o_groups)], axis=2)
    a = o.reshape(b, s, ATTN_WIDTH) @ w_oa
    p = pool_y @ w_op
    return (jax.nn.sigmoid(ga) * a + jax.nn.sigmoid(gp) * p) @ w_out


def setup_inputs(seed: int = 0) -> dict:
    key = jax.random.key(seed)
    ks = jax.random.split(key, 32)
    f32 = jnp.float32

    def nrm(k, shape, s):
        return jax.random.normal(k, shape, f32) * s

    def cshape(w):
        return (DEPTH, DEC_BATCH, min(w, PAST_LEN), HEADS_PER_GROUP, HEAD_DIM)

    w0, w1, w2 = ATTN_GROUPS[0][0], ATTN_GROUPS[1][0], ATTN_GROUPS[2][0]
    return {
        'x_prompt': nrm(ks[0], (BATCH, SEQ, D_MODEL), 1.0),
        'x_sample': nrm(ks[1], (DEC_BATCH, DEC_SEQ, D_MODEL), 1.0),
        'cache_k_g0': nrm(ks[2], cshape(w0), 1.0),
        'cache_v_g0': nrm(ks[3], cshape(w0), 1.0),
        'cache_k_g1': nrm(ks[4], cshape(w1), 1.0),
        'cache_v_g1': nrm(ks[5], cshape(w1), 1.0),
        'cache_k_g2': nrm(ks[6], cshape(w2), 1.0),
        'cache_v_g2': nrm(ks[7], cshape(w2), 1.0),
        'state_pool': nrm(ks[8], (DEPTH, DEC_BATCH, POOL_STATE, POOL_WIDTH), 1.0),
        'c_prompt': nrm(ks[9], (BATCH, D_MODEL), 1.0),
        'c_sample': nrm(ks[10], (DEC_BATCH, D_MODEL), 1.0),
        'w_ada': nrm(ks[11], (DEPTH, D_MODEL, N_MOD * D_MODEL), 0.5 * D_MODEL ** -0.5),
        'b_ada': nrm(ks[12], (DEPTH, N_MOD * D_MODEL), 0.02),
        'norm_g': 1.0 + nrm(ks[13], (DEPTH, 3, D_MODEL), 0.02),
        'ffn_w1': nrm(ks[14], (DEPTH, 2, D_MODEL, 2 * D_FF), D_MODEL ** -0.5),
        'ffn_w2': nrm(ks[15], (DEPTH, 2, D_FF, D_MODEL), D_FF ** -0.5),
        'w_in': nrm(ks[16], (DEPTH, D_MODEL, IN_WIDTH), D_MODEL ** -0.5),
        'q_norm_g': 1.0 + nrm(ks[17], (DEPTH, HEAD_DIM), 0.02),
        'k_norm_g': 1.0 + nrm(ks[18], (DEPTH, HEAD_DIM), 0.02),
        'w_oa': nrm(ks[19], (DEPTH, ATTN_WIDTH, D_MODEL), ATTN_WIDTH ** -0.5),
        'pool_w': nrm(ks[20], (DEPTH, len(POOL_WINDOWS), POOL_GROUP_DIM, POOL_GROUP_DIM), POOL_GROUP_DIM ** -0.5),
        'pool_scale': 1.0 + nrm(ks[21], (DEPTH, POOL_WIDTH), 0.02),
        'w_op': nrm(ks[22], (DEPTH, POOL_WIDTH, D_MODEL), POOL_WIDTH ** -0.5),
        'w_out': nrm(ks[23], (DEPTH, D_MODEL, D_MODEL), D_MODEL ** -0.5),
    }


def reference(x_prompt, x_sample, cache_k_g0, cache_v_g0, cache_k_g1, cache_v_g1,
              cache_k_g2, cache_v_g2, state_pool, c_prompt, c_sample,
              w_ada, b_ada, norm_g, ffn_w1, ffn_w2, w_in, q_norm_g, k_norm_g,
              w_oa, pool_w, pool_scale, w_op, w_out):
    slopes = alibi_slopes()
    cache_k = (cache_k_g0, cache_k_g1, cache_k_g2)
    cache_v = (cache_v_g0, cache_v_g1, cache_v_g2)
    xp, xs = x_prompt, x_sample
    s_len = xp.shape[1]
    t_new = xs.shape[1]
    nkp = [[] for _ in range(N_GROUPS)]
    nvp = [[] for _ in range(N_GROUPS)]
    nks = [[] for _ in range(N_GROUPS)]
    nvs = [[] for _ in range(N_GROUPS)]
    npool_p, npool_s = [], []

    for l in range(DEPTH):
        mod_p = adaln_mod(c_prompt, w_ada[l], b_ada[l])
        mod_s = adaln_mod(c_sample, w_ada[l], b_ada[l])

        xp = ffn_sublayer(xp, mod_p, 0, norm_g[l, 0], ffn_w1[l, 0], ffn_w2[l, 0])
        xs = ffn_sublayer(xs, mod_s, 0, norm_g[l, 0], ffn_w1[l, 0], ffn_w2[l, 0])

        hp = modulate(xp, norm_g[l, 1], mod_p[:, 3], mod_p[:, 4])
        q, k, v, u, ga, gp = project_in(hp, w_in[l], q_norm_g[l], k_norm_g[l])
        o_list, lse_list = [], []
        for g, (win, dil) in enumerate(ATTN_GROUPS):
            hsl = slice(g * HEADS_PER_GROUP, (g + 1) * HEADS_PER_GROUP)
            o, lse = band_dilated_attention(q[:, :, hsl], k[:, :, hsl], v[:, :, hsl],
                                            win, dil, slopes[hsl])
            o_list.append(o)
            lse_list.append(lse)
            keep = min(win, s_len)
            nkp[g].append(k[:, s_len - keep:, hsl])
            nvp[g].append(v[:, s_len - keep:, hsl])
        pos = jnp.arange(s_len)
        yp_pool = pool_mixer(u, pos, s_len, pool_w[l], pool_scale[l])
        npool_p.append(u[:, s_len - POOL_STATE:])
        mix_p = merge_out(o_list, lse_list, yp_pool, ga, gp, w_oa[l], w_op[l], w_out[l])
        xp = xp + mod_p[:, 5][:, None, :] * mix_p

        hs_ = modulate(xs, norm_g[l, 1], mod_s[:, 3], mod_s[:, 4])
        q, k, v, u, ga, gp = project_in(hs_, w_in[l], q_norm_g[l], k_norm_g[l])
        o_list, lse_list = [], []
        for g, (win, dil) in enumerate(ATTN_GROUPS):
            hsl = slice(g * HEADS_PER_GROUP, (g + 1) * HEADS_PER_GROUP)
            n_past = cache_k[g].shape[2]
            k_all = jnp.concatenate([cache_k[g][l], k[:, :, hsl]], axis=1)
            v_all = jnp.concatenate([cache_v[g][l], v[:, :, hsl]], axis=1)
            o, lse = dilated_attention_step(q[:, :, hsl], k_all, v_all, win, dil,
                                            slopes[hsl], n_past)
            o_list.append(o)
            lse_list.append(lse)
            rows = k_all.shape[1]
            keep = min(win, rows)
            nks[g].append(k_all[:, rows - keep:])
            nvs[g].append(v_all[:, rows - keep:])
        u_ext = jnp.concatenate([state_pool[l], u], axis=1)
        pos_ext = PAST_LEN - POOL_STATE + jnp.arange(POOL_STATE + t_new)
        ys_pool = pool_mixer(u_ext, pos_ext, t_new, pool_w[l], pool_scale[l])
        npool_s.append(u_ext[:, u_ext.shape[1] - POOL_STATE:])
        mix_s = merge_out(o_list, lse_list, ys_pool, ga, gp, w_oa[l], w_op[l], w_out[l])
        xs = xs + mod_s[:, 5][:, None, :] * mix_s

        xp = ffn_sublayer(xp, mod_p, 2, norm_g[l, 2], ffn_w1[l, 1], ffn_w2[l, 1])
        xs = ffn_sublayer(xs, mod_s, 2, norm_g[l, 2], ffn_w1[l, 1], ffn_w2[l, 1])

    st = jnp.stack
    return (xp, xs,
            st(nkp[0]), st(nvp[0]), st(nkp[1]), st(nvp[1]), st(nkp[2]), st(nvp[2]), st(npool_p),
            st(nks[0]), st(nvs[0]), st(nks[1]), st(nvs[1]), st(nks[2]), st(nvs[2]), st(npool_s))
```

```python
import contextlib
import numpy as np
import concourse.bass as bass
import concourse.mybir as mybir
from concourse.bass_utils import run_bass_kernel_spmd

F32 = mybir.dt.float32
BF16 = mybir.dt.bfloat16
AF = mybir.ActivationFunctionType
ALU = mybir.AluOpType
AX = mybir.AxisListType

D = 1024
KC = 8
SEQ = 2048
NS = 16
NT = SEQ + NS
DFF = 2816
NJ = 22
DEPTH = 2
HD = 64
NCORES = 8
IN_W = 4608
EPS = 1e-6
GROUPS = ((128, 1), (512, 4), (2048, 16))
TT = [(0, 512), (512, 512), (1024, 512), (1536, 512), (2048, 16)]
FF_PARTS = [(0, 6), (6, 6), (12, 5), (17, 5)]
SLOPES = [float(np.power(2.0, -8.0 * h / 12.0)) for h in range(1, 13)]
NEG = -30000.0
MCW = 864
POOL_WINDOWS = (2, 4, 8, 16)


class BufT:
    __slots__ = ("name", "lw", "rd", "wsem", "wcnt", "rsem", "rcnt", "excl")

    def __init__(self, name, excl=False):
        self.name = name
        self.excl = excl
        self.lw = None
        self.rd = {}
        self.wsem = None
        self.wcnt = 0
        self.rsem = None
        self.rcnt = 0


class Eng:
    def __init__(self, hw, sem, name):
        self.hw = hw
        self.sem = sem
        self.cnt = 0
        self.seen = {}
        self.name = name


class TK:
    def __init__(self, nc, es):
        self.nc = nc
        self.es = es
        self.eng = {}
        for nm, hw in (("pe", nc.tensor), ("act", nc.scalar), ("dve", nc.vector),
                       ("pool", nc.gpsimd), ("sp", nc.sync)):
            self.eng[nm] = Eng(hw, es.enter_context(nc.semaphore("e_" + nm)), nm)
        self.sems = {}
        self.dram_sems = []
        self.out_events = []

    def new_sem(self, name):
        s = self.es.enter_context(self.nc.semaphore(name))
        self.sems[s.num] = s
        return s

    def _collect(self, e, reads, writes):
        need = {}

        def add(ev):
            if ev is None:
                return
            sem, val = ev
            k = sem.num
            if k not in need or need[k][1] < val:
                need[k] = (sem, val)

        for b in reads:
            add(b.lw)
            if b.excl:
                for k, ev in b.rd.items():
                    if k != e.sem.num:
                        add(ev)
        for b in writes:
            add(b.lw)
            for k, ev in b.rd.items():
                add(ev)
        out = []
        for k, (sem, val) in need.items():
            if e.seen.get(k, 0) < val:
                out.append((sem, val))
                e.seen[k] = val
        return out

    def _emit_waits(self, e, waits):
        for sem, val in waits:
            e.hw.wait_ge(sem, val)

    def op(self, en, fn, reads=(), writes=()):
        e = self.eng[en]
        self._emit_waits(e, self._collect(e, reads, writes))
        inst = fn(e.hw)
        e.cnt += 1
        inst.then_inc(e.sem, 1)
        ev = (e.sem, e.cnt)
        for b in reads:
            b.rd[e.sem.num] = ev
        for b in writes:
            b.lw = ev
            b.rd = {}
        return inst

    def group(self, en, fns, reads=(), writes=()):
        e = self.eng[en]
        self._emit_waits(e, self._collect(e, reads, writes))
        inst = None
        for fn in fns:
            inst = fn(e.hw)
        e.cnt += 1
        inst.then_inc(e.sem, 1)
        ev = (e.sem, e.cnt)
        for b in reads:
            b.rd[e.sem.num] = ev
        for b in writes:
            b.lw = ev
            b.rd = {}

    def dma(self, qn, out, in_=None, reads=(), writes=(), **kw):
        e = self.eng[qn]
        pairs = out if in_ is None else [(out, in_)]
        self._emit_waits(e, self._collect(e, reads, writes))
        if writes:
            b = writes[0]
            if b.wsem is None:
                b.wsem = self.new_sem("w_" + b.name)
            for o, i in pairs:
                e.hw.dma_start(out=o, in_=i, **kw).then_inc(b.wsem, 16)
                b.wcnt += 16
            ev = (b.wsem, b.wcnt)
            for w in writes:
                w.lw = ev
                w.rd = {}
            for r in reads:
                r.rd[b.wsem.num] = ev
        elif reads:
            b = reads[0]
            if b.rsem is None:
                b.rsem = self.new_sem("r_" + b.name)
            for o, i in pairs:
                e.hw.dma_start(out=o, in_=i, **kw).then_inc(b.rsem, 16)
                b.rcnt += 16
            ev = (b.rsem, b.rcnt)
            for r in reads:
                r.rd[b.rsem.num] = ev
            self.out_events.append(ev)
        else:
            if not self.dram_sems:
                self.dram_sems.append([self.new_sem("dram"), 0])
            ds = self.dram_sems[0]
            for o, i in pairs:
                e.hw.dma_start(out=o, in_=i, **kw).then_inc(ds[0], 16)
                ds[1] += 16

    def finish(self):
        e = self.eng["sp"]
        last = {}
        for sem, val in self.out_events:
            if sem.num not in last or last[sem.num][1] < val:
                last[sem.num] = (sem, val)
        for ds in self.dram_sems:
            last[ds[0].num] = (ds[0], ds[1])
        for sem, val in last.values():
            e.hw.wait_ge(sem, val)
        for nm in ("pe", "act", "dve", "pool"):
            o = self.eng[nm]
            if o.cnt:
                e.hw.wait_ge(o.sem, o.cnt)


class StopBuild(Exception):
    pass


class Ring:
    def __init__(self, bufs):
        self.bufs = bufs
        self.i = 0

    def get(self):
        b = self.bufs[self.i % len(self.bufs)]
        self.i += 1
        return b


def build_program(stop_after=(None, None)):
    nc = bass.Bass("TRN2", target_bir_lowering=False)
    es = contextlib.ExitStack()
    nc._es = es
    tk = TK(nc, es)

    _bufs = {}

    def Buf(name, excl=False):
        if name not in _bufs:
            _bufs[name] = BufT(name, excl)
        return _bufs[name]

    def din(name, shape):
        return nc.dram_tensor(name, list(shape), F32, kind="ExternalInput").ap()

    def dout(name, shape):
        return nc.dram_tensor(name, list(shape), F32, kind="ExternalOutput").ap()

    x_p = din("x_p", [SEQ, D])
    x_s = din("x_s", [NS, D])
    c_all = din("c_all", [NS + 1, D])
    cache_k = [din(f"ck{g}", [DEPTH, NS, GROUPS[g][0], 256]) for g in range(3)]
    cache_v = [din(f"cv{g}", [DEPTH, NS, GROUPS[g][0], 256]) for g in range(3)]
    state_pool = din("state_pool", [DEPTH, NS * 15, 256])
    w_ada = din("w_ada", [DEPTH, D, 9 * D])
    b_ada = din("b_ada", [DEPTH, 72, 128])
    norm_g = din("norm_g", [DEPTH, 24, 128])
    ffn_w1 = din("ffn_w1", [DEPTH, 2, D, 2 * DFF])
    ffn_w2 = din("ffn_w2", [DEPTH, 2, DFF, D])
    w_in = din("w_in", [DEPTH, D, IN_W])
    qk_g = din("qk_g", [DEPTH, 2, HD])
    w_oa = din("w_oa", [DEPTH, 768, D])
    pool_w = din("pool_w", [DEPTH, 4, 64, 64])
    pool_scale = din("pool_scale", [DEPTH, 2, 128])
    w_op = din("w_op", [DEPTH, 256, D])
    w_out = din("w_out", [DEPTH, D, D])
    consts = din("consts", [128, 256])
    mconsts = din("mconsts", [128, MCW])

    y_p = dout("y_p", [SEQ, D])
    y_s = dout("y_s", [NS, D])
    nk_p = [dout(f"nkp{g}", [DEPTH, GROUPS[g][0], 256]) for g in range(3)]
    nv_p = [dout(f"nvp{g}", [DEPTH, GROUPS[g][0], 256]) for g in range(3)]
    npool_p = dout("npool_p", [DEPTH, 15, 256])
    nk_s = [dout(f"nks{g}", [DEPTH, NS, GROUPS[g][0], 256]) for g in range(3)]
    nv_s = [dout(f"nvs{g}", [DEPTH, NS, GROUPS[g][0], 256]) for g in range(3)]
    npool_s = dout("npool_s", [DEPTH, NS, 15, 256])

    def sb(name, shape, dt=F32):
        return es.enter_context(nc.sbuf_tensor(name, list(shape), dt))

    xT = sb("xT", [128, KC, NT])
    hT = sb("hT", [128, KC, NT], BF16)
    modT1 = sb("modT", [128, 1, 72, NS + 1])
    scT = sb("scT", [128, KC, NS + 1], BF16)
    Amod = sb("Amod", [128, 3, KC, NS + 1])
    gT = sb("gT", [128, DEPTH, 24])
    badaT = sb("badaT", [128, DEPTH, 72])
    cst = sb("cst", [128, 256])
    ident_b = sb("ident_b", [128, 128], BF16)
    ones_b = sb("ones_b", [128, 128], BF16)
    ARENA_BYTES = 100 * 1024
    arena = sb("arena", [128, ARENA_BYTES // 4])
    arena_off = [0]

    B_xT = [[Buf(f"xT{m}_{t}") for t in range(5)] for m in range(KC)]
    B_hT = [[Buf(f"hT{m}_{t}") for t in range(5)] for m in range(KC)]
    B_modT = Buf("modT")
    B_Amod = Buf("Amod")
    B_gT = Buf("gT")
    B_bada = Buf("bada")
    B_cst = Buf("cst")
    B_identb = Buf("identb")
    B_onesb = Buf("onesb")

    def arena_reset():
        arena_off[0] = 0

    def arena_alloc(shape, dt):
        esz = 4 if dt == F32 else 2
        n = int(np.prod(shape[1:]))
        nbytes = (n * esz + 3) // 4 * 4
        o = arena_off[0]
        assert o + nbytes <= ARENA_BYTES, (o, nbytes, shape)
        arena_off[0] = o + nbytes
        v = arena[:, o // 4:(o + nbytes) // 4]
        if dt != F32:
            v = v.bitcast(dt)
        v = v[:, 0:n]
        if len(shape) > 2:
            names = " ".join(f"d{i}" for i in range(len(shape) - 1))
            kw = {f"d{i}": shape[i + 1] for i in range(len(shape) - 1)}
            v = v.rearrange(f"p ({names}) -> p {names}", **kw)
        return v[0:shape[0]]

    psum = [es.enter_context(nc.psum_tensor(f"ps{i}", [128, 512], F32)) for i in range(8)]
    B_ps = [Buf(f"ps{i}", excl=True) for i in range(8)]
    ps_i = [0]

    def ps_get():
        i = ps_i[0] % 8
        ps_i[0] += 1
        return psum[i], B_ps[i]

    def barrier():
        names = ("pe", "act", "dve", "pool", "sp")
        last = {}
        for sem, val in tk.out_events:
            if sem.num not in last or last[sem.num][1] < val:
                last[sem.num] = (sem, val)
        for a in names:
            ea = tk.eng[a]
            for k, (sem, val) in last.items():
                if ea.seen.get(k, 0) < val:
                    ea.hw.wait_ge(sem, val)
                    ea.seen[k] = val
            for b in names:
                if a == b:
                    continue
                eb = tk.eng[b]
                if eb.cnt and ea.seen.get(eb.sem.num, 0) < eb.cnt:
                    ea.hw.wait_ge(eb.sem, eb.cnt)
                    ea.seen[eb.sem.num] = eb.cnt

    tk.dma("sp", cst[:], consts[:, :], writes=[B_cst])
    IDF = cst[:, 0:128]
    tk.op("dve", lambda h: h.tensor_copy(out=ident_b[:], in_=cst[:, 0:128]), [B_cst], [B_identb])
    tk.op("dve", lambda h: h.memset(ones_b[:], 1.0), [], [B_onesb])

    B_scT = Buf("scT")

    class _ModView:
        def __getitem__(self, idx):
            idx = list(idx)
            idx[1] = 0
            return modT1[tuple(idx)]
    modT = _ModView()

    def adaln_setup():
        arena_reset()
        call = arena_alloc([NS + 1, D], F32)
        B_call = Buf("call")
        scb = arena_alloc([NS + 1, D], BF16)
        B_scb = Buf("scb")
        tmp24 = arena_alloc([72, 128], F32)
        B_tmp24 = Buf("tmp24")
        tk.dma("sp", call, c_all[:, :], writes=[B_call])
        tk.op("act", lambda h: h.activation(out=scb, in_=call, func=AF.Silu), [B_call], [B_scb])
        pt, bpt = ps_get()
        ptb = pt[:].bitcast(BF16)
        tk.group("pe", [
            (lambda h, kc=kc: h.transpose(out=ptb[:, kc * 32:kc * 32 + NS + 1], in_=scb[:, kc * 128:(kc + 1) * 128],
                                          identity=ident_b[0:NS + 1, 0:NS + 1]))
            for kc in range(KC)], [B_scb, B_identb], [bpt])
        tk.op("dve", lambda h: h.tensor_copy(
            out=scT[:], in_=ptb[:, 0:KC * 32].rearrange("p (k c) -> p k c", c=32)[:, :, 0:NS + 1]), [bpt], [B_scT])
        for l in range(DEPTH):
            tk.dma("sp", tmp24[0:72, :], b_ada[l], writes=[B_tmp24])
            pt, bpt = ps_get()
            tk.group("pe", [lambda h: h.transpose(out=pt[:, 0:72], in_=tmp24[0:72, :], identity=IDF[0:72, 0:72])],
                     [B_tmp24, B_cst], [bpt])
            tk.op("dve", lambda h: h.tensor_copy(out=badaT[:, l, :], in_=pt[:, 0:72]), [bpt], [B_bada])
            tk.dma("sp", tmp24[0:24, :], norm_g[l], writes=[B_tmp24])
            pt, bpt = ps_get()
            tk.group("pe", [lambda h: h.transpose(out=pt[:, 0:24], in_=tmp24[0:24, :], identity=IDF[0:24, 0:24])],
                     [B_tmp24, B_cst], [bpt])
            tk.op("dve", lambda h: h.tensor_copy(out=gT[:, l, :], in_=pt[:, 0:24]), [bpt], [B_gT])

    def adaln(l):
        barrier()
        arena_reset()
        wada = [arena_alloc([128, KC, 512], BF16) for _ in range(3)]
        B_wada = [Buf(f"wada{i}") for i in range(3)]
        mtok = [arena_alloc([NS + 1, 512], F32) for _ in range(2)]
        B_mtok = [Buf(f"mtok{i}") for i in range(2)]
        for pc in range(18):
            slot = pc % 3
            tk.dma("pool", wada[slot],
                   w_ada[l, :, pc * 512:(pc + 1) * 512].rearrange("(k p) c -> p k c", p=128),
                   writes=[B_wada[slot]])
            pm, bpm = ps_get()
            tk.group("pe", [
                (lambda h, kc=kc: h.matmul(pm[0:NS + 1, :], lhsT=scT[:, kc, :], rhs=wada[slot][:, kc, :],
                                           start=(kc == 0), stop=(kc == KC - 1)))
                for kc in range(KC)], [B_scT, B_wada[slot]], [bpm])
            ms = pc % 2
            tk.op("act", lambda h: h.copy(out=mtok[ms], in_=pm[0:NS + 1, :]), [bpm], [B_mtok[ms]])
            pt, bpt = ps_get()
            tk.group("pe", [
                (lambda h, q=q: h.transpose(out=pt[:, q * 32:q * 32 + NS + 1], in_=mtok[ms][:, q * 128:(q + 1) * 128],
                                            identity=IDF[0:NS + 1, 0:NS + 1]))
                for q in range(4)], [B_mtok[ms], B_cst], [bpt])
            tk.op("dve", lambda h: h.tensor_tensor(
                out=modT[:, l, pc * 4:(pc + 1) * 4, :],
                in0=pt[:, 0:128].rearrange("p (q c) -> p q c", c=32)[:, :, 0:NS + 1],
                in1=badaT[:, l, pc * 4:(pc + 1) * 4].unsqueeze(2).to_broadcast([128, 4, NS + 1]),
                op=ALU.add), [bpt, B_bada], [B_modT])
        for i in (0, 2):
            tk.op("dve", lambda h: h.tensor_scalar(
                out=modT[:, l, (3 * i + 2) * 8:(3 * i + 3) * 8, :], in0=modT[:, l, (3 * i + 2) * 8:(3 * i + 3) * 8, :],
                scalar1=0.5, scalar2=None, op0=ALU.mult), [B_modT], [B_modT])

    adaln_setup()

    barrier()
    arena_reset()
    xst = [arena_alloc([128, D], F32) for _ in range(2)]
    B_xst = [Buf(f"xst{i}") for i in range(2)]
    for tt in range(17):
        s = tt % 2
        nrow = 128 if tt < 16 else NS
        src = x_p[tt * 128:(tt + 1) * 128, :] if tt < 16 else x_s[:, :]
        tk.dma("sp", xst[s][0:nrow, :], src, writes=[B_xst[s]])
        T = tt // 4 if tt < 16 else 4
        col0 = tt * 128
        for half in range(2):
            pt, bpt = ps_get()
            tk.group("pe", [
                (lambda h, q=q: h.transpose(out=pt[:, q * 128:q * 128 + nrow],
                                            in_=xst[s][0:nrow, (half * 4 + q) * 128:(half * 4 + q + 1) * 128],
                                            identity=IDF[0:nrow, 0:nrow]))
                for q in range(4)], [B_xst[s], B_cst], [bpt])
            eng = "act" if half == 0 else "dve"
            if eng == "act":
                fn = lambda h: h.copy(out=xT[:, half * 4:half * 4 + 4, col0:col0 + nrow],
                                      in_=pt[:].rearrange("p (q c) -> p q c", c=128)[:, :, 0:nrow])
            else:
                fn = lambda h: h.tensor_copy(out=xT[:, half * 4:half * 4 + 4, col0:col0 + nrow],
                                             in_=pt[:].rearrange("p (q c) -> p q c", c=128)[:, :, 0:nrow])
            tk.op(eng, fn, [bpt], [B_xT[half * 4 + q][T] for q in range(4)])

    def prologue(l, i):
        tk.op("dve", lambda h: h.scalar_tensor_tensor(
            out=Amod[:, i], in0=modT[:, l, (3 * i + 1) * 8:(3 * i + 2) * 8, :], scalar=1.0,
            in1=gT[:, l, i * 8:(i + 1) * 8].unsqueeze(2).to_broadcast([128, 8, NS + 1]),
            op0=ALU.add, op1=ALU.mult), [B_modT, B_gT], [B_Amod])
        sq = [arena_alloc([128, KC, 512], BF16) for _ in range(1)]
        B_sq = [Buf(f"sq{k}") for k in range(1)]
        rs = [arena_alloc([128, 512], F32) for _ in range(2)]
        B_rs = [Buf(f"rs{k}") for k in range(2)]
        tm = [arena_alloc([128, 512], F32) for _ in range(3)]
        B_tm = [Buf(f"tm{k}") for k in range(3)]
        tmi = 0
        for T, (c0, n) in enumerate(TT):
            s = T % 2
            s0 = 0
            tk.op("act", lambda h: h.activation(out=sq[s0][:, :, 0:n], in_=xT[:, :, c0:c0 + n], func=AF.Square),
                  [B_xT[m][T] for m in range(KC)], [B_sq[s0]])
            pq, bpq = ps_get()
            tk.group("pe", [
                (lambda h, kc=kc: h.matmul(pq[:, 0:n], lhsT=ones_b[:], rhs=sq[s0][:, kc, 0:n],
                                           start=(kc == 0), stop=(kc == KC - 1)))
                for kc in range(KC)], [B_sq[s0], B_onesb], [bpq])
            tk.op("act", lambda h: h.activation(out=rs[s][:, 0:n], in_=pq[:, 0:n], func=AF.Sqrt,
                                                scale=1.0 / D, bias=cst[:, 130:131]), [bpq, B_cst], [B_rs[s]])
            tk.op("dve", lambda h: h.reciprocal(out=rs[s][:, 0:n], in_=rs[s][:, 0:n]), [B_rs[s]], [B_rs[s]])
            if T < 4:
                for kc in range(KC):
                    k3 = tmi % 3
                    tmi += 1
                    tk.op("dve", lambda h: h.scalar_tensor_tensor(
                        out=tm[k3][:, 0:n], in0=xT[:, kc, c0:c0 + n], scalar=Amod[:, i, kc, 0:1],
                        in1=rs[s][:, 0:n], op0=ALU.mult, op1=ALU.mult),
                        [B_xT[kc][T], B_Amod, B_rs[s]], [B_tm[k3]])
                    tk.op("act", lambda h: h.activation(
                        out=hT[:, kc, c0:c0 + n], in_=tm[k3][:, 0:n], func=AF.Identity,
                        bias=modT[:, l, 3 * i * 8 + kc, 0:1], scale=1.0),
                        [B_tm[k3], B_modT], [B_hT[kc][T]])
            else:
                t3 = arena_alloc([128, KC, NS], F32)
                B_t3 = Buf("t3")
                tk.op("dve", lambda h: h.tensor_tensor(
                    out=t3, in0=xT[:, :, c0:c0 + n],
                    in1=rs[s][:, 0:n].unsqueeze(1).to_broadcast([128, KC, NS]), op=ALU.mult),
                    [B_xT[m][T] for m in range(KC)] + [B_rs[s]], [B_t3])
                tk.op("dve", lambda h: h.tensor_tensor(out=t3, in0=t3, in1=Amod[:, i, :, 1:NS + 1], op=ALU.mult),
                      [B_t3, B_Amod], [B_t3])
                tk.op("dve", lambda h: h.tensor_tensor(
                    out=hT[:, :, c0:c0 + n], in0=t3, in1=modT[:, l, 3 * i * 8:3 * i * 8 + 8, 1:NS + 1], op=ALU.add),
                    [B_t3, B_modT], [B_hT[m][T] for m in range(KC)])

    def resid_update(l, gi, pm, bpm, m, T, c0, n, tmpb, B_tmpb):
        grow = (3 * gi + 2) * 8 + m
        if T < 4:
            tk.op("dve", lambda h: h.scalar_tensor_tensor(
                out=xT[:, m, c0:c0 + n], in0=pm[:, 0:n], scalar=modT[:, l, grow, 0:1],
                in1=xT[:, m, c0:c0 + n], op0=ALU.mult, op1=ALU.add),
                [bpm, B_modT, B_xT[m][T]], [B_xT[m][T]])
        else:
            tk.op("dve", lambda h: h.tensor_tensor(out=tmpb[:, 0:n], in0=pm[:, 0:n],
                                                   in1=modT[:, l, grow, 1:NS + 1], op=ALU.mult),
                  [bpm, B_modT], [B_tmpb])
            tk.op("dve", lambda h: h.tensor_tensor(out=xT[:, m, c0:c0 + n], in0=tmpb[:, 0:n],
                                                   in1=xT[:, m, c0:c0 + n], op=ALU.add),
                  [B_tmpb, B_xT[m][T]], [B_xT[m][T]])

    def ffn(l, which):
        i = 0 if which == 0 else 2
        barrier()
        arena_reset()
        prologue(l, i)
        w1s = [arena_alloc([128, KC, 2, 384], BF16) for _ in range(2)]
        B_w1s = [Buf(f"w1s{k}") for k in range(2)]
        w2s = [arena_alloc([128, 6, D], BF16) for _ in range(2)]
        B_w2s = [Buf(f"w2s{k}") for k in range(2)]
        actp = arena_alloc([128, 6, NT], BF16)
        B_act = [[Buf(f"act{j}_{t}") for t in range(5)] for j in range(6)]
        sg = [arena_alloc([128, 512], F32) for _ in range(2)]
        B_sg = [Buf(f"sg{k}") for k in range(2)]
        tmpb = arena_alloc([128, NS], F32)
        B_tmpb = Buf("tmpb")
        w1v = ffn_w1[l, which].rearrange("(k p) (g c) -> p k g c", p=128, g=2)
        w2v = ffn_w2[l, which].rearrange("(j p) c -> p j c", p=128)
        pi = 0
        sgi = 0
        for part, (j0, nj) in enumerate(FF_PARTS):
            ws = part % 2
            tk.dma("pool", w2s[ws][:, 0:nj, :], w2v[:, j0:j0 + nj, :], writes=[B_w2s[ws]])
            jj = 0
            while jj < nj:
                npc = min(3, nj - jj)
                s1 = pi % 2
                pi += 1
                tk.dma("pool", [(w1s[s1][:, :, g2, 0:npc * 128],
                                 w1v[:, :, g2, (j0 + jj) * 128:(j0 + jj + npc) * 128]) for g2 in range(2)],
                       writes=[B_w1s[s1]])
                for q in range(npc):
                    jr = jj + q
                    for T, (c0, n) in enumerate(TT):
                        pg, bpg = ps_get()
                        pu, bpu = ps_get()
                        rds = [B_w1s[s1]] + [B_hT[kc][T] for kc in range(KC)]
                        tk.group("pe", [
                            (lambda h, kc=kc: h.matmul(pg[:, 0:n], lhsT=w1s[s1][:, kc, 0, q * 128:(q + 1) * 128],
                                                       rhs=hT[:, kc, c0:c0 + n], start=(kc == 0), stop=(kc == KC - 1)))
                            for kc in range(KC)], rds, [bpg])
                        tk.group("pe", [
                            (lambda h, kc=kc: h.matmul(pu[:, 0:n], lhsT=w1s[s1][:, kc, 1, q * 128:(q + 1) * 128],
                                                       rhs=hT[:, kc, c0:c0 + n], start=(kc == 0), stop=(kc == KC - 1)))
                            for kc in range(KC)], rds, [bpu])
                        k2 = sgi % 2
                        sgi += 1
                        tk.op("act", lambda h: h.activation(out=sg[k2][:, 0:n], in_=pg[:, 0:n], func=AF.Silu),
                              [bpg], [B_sg[k2]])
                        tk.op("dve", lambda h: h.tensor_tensor(out=actp[:, jr, c0:c0 + n], in0=pu[:, 0:n],
                                                               in1=sg[k2][:, 0:n], op=ALU.mult),
                              [bpu, B_sg[k2]], [B_act[jr][T]])
                jj += npc
            for m in range(KC):
                for T, (c0, n) in enumerate(TT):
                    py, bpy = ps_get()
                    tk.group("pe", [
                        (lambda h, jr=jr: h.matmul(py[:, 0:n], lhsT=w2s[ws][:, jr, m * 128:(m + 1) * 128],
                                                   rhs=actp[:, jr, c0:c0 + n], start=(jr == 0), stop=(jr == nj - 1)))
                        for jr in range(nj)], [B_w2s[ws]] + [B_act[jr][T] for jr in range(nj)], [bpy])
                    resid_update(l, i, py, bpy, m, T, c0, n, tmpb, B_tmpb)

    def mixer(l):
        barrier()
        arena_reset()
        prologue(l, 1)
        barrier()
        arena_reset()
        i_sub = 1
        mc = arena_alloc([128, MCW], F32)
        B_mc = Buf("mc")
        tk.dma("sp", mc, mconsts[:, :], writes=[B_mc])
        DISTG = [mc[:, g_ * 256:(g_ + 1) * 256] for g_ in range(3)]
        INVC = mc[:, 768:800].rearrange("p (c t) -> p c t", t=16)
        SBIAS = mc[:, 800:812]
        EYE16 = mc[0:NS, 816:832]
        zib = arena_alloc([128, 32], BF16)
        B_zib = Buf("zib")
        tk.op("dve", lambda h: h.tensor_copy(out=zib, in_=mc[:, 832:864]), [B_mc], [B_zib])
        pool_yT = arena_alloc([128, 2, NT], BF16)
        B_py = [Buf(f"py{t}") for t in range(5)]
        mark1 = arena_off[0]

        uT = arena_alloc([128, 2, NT], F32)
        B_uT = [Buf("uT0"), Buf("uT1")]
        wu = arena_alloc([128, KC, 256], BF16)
        B_wu = Buf("wu")
        tk.dma("pool", wu, w_in[l, :, 2304:2560].rearrange("(k p) c -> p k c", p=128), writes=[B_wu])
        for c in range(2):
            for T, (c0, n) in enumerate(TT):
                pu, bpu = ps_get()
                tk.group("pe", [
                    (lambda h, kc=kc: h.matmul(pu[:, 0:n], lhsT=wu[:, kc, c * 128:(c + 1) * 128],
                                               rhs=hT[:, kc, c0:c0 + n], start=(kc == 0), stop=(kc == KC - 1)))
                    for kc in range(KC)], [B_wu] + [B_hT[kc][T] for kc in range(KC)], [bpu])
                tk.op("act", lambda h: h.copy(out=uT[:, c, c0:c0 + n], in_=pu[:, 0:n]), [bpu], [B_uT[c]])
        npst = arena_alloc([NS, 2, 256], F32)
        B_npst = Buf("npst")
        pt, bpt = ps_get()
        tk.group("pe", [
            (lambda h, c=c: h.transpose(out=pt[0:15, c * 128:(c + 1) * 128], in_=uT[:, c, SEQ - 15:SEQ], identity=IDF))
            for c in range(2)] + [
            (lambda h, c=c: h.transpose(out=pt[0:NS, 256 + c * 128:256 + (c + 1) * 128], in_=uT[:, c, SEQ:NT], identity=IDF))
            for c in range(2)], B_uT + [B_cst], [bpt])
        tk.op("act", lambda h: h.copy(out=npst[:, :, :], in_=pt[0:NS, :].rearrange("p (a c) -> p a c", c=256)), [bpt], [B_npst])
        tk.dma("sp", [(npool_p[l], npst[0:15, 0, :]), (npool_s[l, :, 14, :], npst[0:NS, 1, :])], reads=[B_npst])
        tk.dma("sp", npool_s[l, :, 0:14, :], state_pool[l].rearrange("(s r) c -> s r c", r=15)[:, 1:15, :])
        stg = arena_alloc([120, 2, 256], F32)
        B_stg = Buf("stg")
        tk.dma("sp", stg, state_pool[l].rearrange("(h p) c -> p h c", p=120), writes=[B_stg])
        stT = arena_alloc([128, 2, 240], F32)
        B_stT = Buf("stT")
        pt, bpt = ps_get()
        tk.group("pe", [
            (lambda h, c=c, hh=hh: h.transpose(out=pt[:, (c * 2 + hh) * 120:(c * 2 + hh + 1) * 120],
                                               in_=stg[:, hh, c * 128:(c + 1) * 128], identity=IDF[0:120, 0:120]))
            for c in range(2) for hh in range(2)], [B_stg, B_cst], [bpt])
        tk.op("act", lambda h: h.copy(out=stT, in_=pt[:, 0:480].rearrange("p (c x) -> p c x", x=240)), [bpt], [B_stT])
        ssum = arena_alloc([128, 2, NS], F32)
        B_ssum = Buf("ssum")
        dT = arena_alloc([128, 2, NT], BF16)
        B_dT = Buf("dT")
        for c in range(2):
            for hf in range(2):
                w = POOL_WINDOWS[2 * c + hf]
                ps_ = slice(hf * 64, (hf + 1) * 64)
                tk.op("dve", lambda h: h.tensor_reduce(
                    out=ssum[ps_, c, :], in_=stT[ps_, c, :].rearrange("p (s r) -> p s r", r=15)[:, :, 15 - (w - 1):15],
                    axis=AX.X, op=ALU.add), [B_stT], [B_ssum])
        tk.op("dve", lambda h: h.tensor_tensor(out=ssum, in0=ssum, in1=uT[:, :, SEQ:NT], op=ALU.add),
              [B_ssum] + B_uT, [B_ssum])
        for c in range(2):
            for hf in range(2):
                w = POOL_WINDOWS[2 * c + hf]
                ps_ = slice(hf * 64, (hf + 1) * 64)
                tk.op("dve", lambda h: h.scalar_tensor_tensor(
                    out=dT[ps_, c, SEQ:NT], in0=ssum[ps_, c, :], scalar=1.0 / w, in1=uT[ps_, c, SEQ:NT],
                    op0=ALU.mult, op1=ALU.subtract), [B_ssum] + B_uT, [B_dT])
        PA = arena_alloc([128, 2, SEQ], F32)
        PB = arena_alloc([128, 2, SEQ], F32)
        B_PA = Buf("PA")
        B_PB = Buf("PB")
        U = uT[:, :, 0:SEQ]
        tk.op("dve", lambda h: h.tensor_tensor(out=PA[:, :, 1:SEQ], in0=uT[:, :, 1:SEQ], in1=uT[:, :, 0:SEQ - 1], op=ALU.add),
              B_uT, [B_PA])
        tk.op("dve", lambda h: h.tensor_copy(out=PA[:, :, 0:1], in_=uT[:, :, 0:1]), B_uT, [B_PA])
        tk.op("dve", lambda h: h.tensor_tensor(out=PB[:, :, 2:SEQ], in0=PA[:, :, 2:SEQ], in1=PA[:, :, 0:SEQ - 2], op=ALU.add),
              [B_PA], [B_PB])
        tk.op("dve", lambda h: h.tensor_copy(out=PB[:, :, 0:2], in_=PA[:, :, 0:2]), [B_PA], [B_PB])
        tk.op("dve", lambda h: h.tensor_tensor(out=PA[:, 1, 4:SEQ], in0=PB[:, 1, 4:SEQ], in1=PB[:, 1, 0:SEQ - 4], op=ALU.add),
              [B_PB], [B_PA])
        tk.op("dve", lambda h: h.tensor_copy(out=PA[:, 1, 0:4], in_=PB[:, 1, 0:4]), [B_PB], [B_PA])
        tk.op("dve", lambda h: h.tensor_tensor(out=PB[64:128, 1, 8:SEQ], in0=PA[64:128, 1, 8:SEQ],
                                               in1=PA[64:128, 1, 0:SEQ - 8], op=ALU.add), [B_PA], [B_PB])
        tk.op("dve", lambda h: h.tensor_copy(out=PB[64:128, 1, 0:8], in_=PA[64:128, 1, 0:8]), [B_PA], [B_PB])
        t16 = arena_alloc([128, 16], F32)
        B_t16 = Buf("t16")
        for c in range(2):
            for hf in range(2):
                w = POOL_WINDOWS[2 * c + hf]
                ps_ = slice(hf * 64, (hf + 1) * 64)
                S = PA if hf == 0 else PB
                tk.op("dve", lambda h: h.scalar_tensor_tensor(
                    out=dT[ps_, c, 16:SEQ], in0=S[ps_, c, 16:SEQ], scalar=1.0 / w, in1=uT[ps_, c, 16:SEQ],
                    op0=ALU.mult, op1=ALU.subtract), [B_PA, B_PB] + B_uT, [B_dT])
                tk.op("dve", lambda h: h.tensor_tensor(out=t16[ps_, :], in0=S[ps_, c, 0:16], in1=INVC[ps_, c, :], op=ALU.mult),
                      [B_PA, B_PB, B_mc], [B_t16])
                tk.op("dve", lambda h: h.tensor_tensor(out=dT[ps_, c, 0:16], in0=t16[ps_, :], in1=uT[ps_, c, 0:16],
                                                       op=ALU.subtract), [B_t16] + B_uT, [B_dT])
        pwf = arena_alloc([128, 2, 128], F32)
        B_pwf = Buf("pwf")
        pwb = arena_alloc([128, 2, 128], BF16)
        B_pwb = Buf("pwb")
        tk.op("dve", lambda h: h.memset(pwf, 0.0), [], [B_pwf])
        tk.dma("sp", [(pwf[(gi % 2) * 64:(gi % 2 + 1) * 64, gi // 2, (gi % 2) * 64:(gi % 2 + 1) * 64], pool_w[l, gi])
                      for gi in range(4)], writes=[B_pwf])
        tk.op("dve", lambda h: h.tensor_copy(out=pwb, in_=pwf), [B_pwf], [B_pwb])
        psc_in = arena_alloc([2, 128], F32)
        B_pscin = Buf("pscin")
        pscT = arena_alloc([128, 2], F32)
        B_pscT = Buf("pscT")
        tk.dma("sp", psc_in, pool_scale[l], writes=[B_pscin])
        pt, bpt = ps_get()
        tk.group("pe", [lambda h: h.transpose(out=pt[:, 0:2], in_=psc_in, identity=IDF[0:2, 0:2])], [B_pscin, B_cst], [bpt])
        tk.op("dve", lambda h: h.tensor_copy(out=pscT, in_=pt[:, 0:2]), [bpt], [B_pscT])
        for c in range(2):
            for T, (c0, n) in enumerate(TT):
                pp, bpp = ps_get()
                tk.group("pe", [lambda h: h.matmul(pp[:, 0:n], lhsT=pwb[:, c, :], rhs=dT[:, c, c0:c0 + n], start=True, stop=True)],
                         [B_pwb, B_dT], [bpp])
                tk.op("act", lambda h: h.activation(out=pool_yT[:, c, c0:c0 + n], in_=pp[:, 0:n], func=AF.Copy,
                                                    scale=pscT[:, c:c + 1]), [bpp, B_pscT], [B_py[T]])

        if stop_after == ("mixP", l):
            raise StopBuild()
        barrier()
        arena_off[0] = mark1
        OT = arena_alloc([128, 6, NT], BF16)
        B_OT = [Buf(f"OT{k}") for k in range(6)]
        B_OTs = Buf("OTs")
        mark2 = arena_off[0]
        Dsum = arena_alloc([128, SEQ], F32)
        B_Ds = Buf("Dsum")
        wq = [arena_alloc([128, KC, 3, 128], BF16) for _ in range(2)]
        B_wq = [Buf("wq0"), Buf("wq1")]
        qkT = arena_alloc([128, 2, SEQ], BF16)
        B_qkT = [Buf(f"qkT{s_}") for s_ in range(16)]
        Vg = arena_alloc([128, 16, 128], BF16)
        B_V = [Buf(f"V{s_}") for s_ in range(16)]
        gtmp = arena_alloc([128, 2, 64], F32)
        B_gtmp = Buf("gtmp")
        gains = arena_alloc([128, 2, 2, 64], F32)
        B_gains = Buf("gains")
        tk.dma("sp", gtmp, qk_g[l].partition_broadcast(128), writes=[B_gtmp])
        tk.op("dve", lambda h: h.tensor_scalar(out=gains[:, 0], in0=gtmp[:, 0:1, :].to_broadcast([128, 2, 64]),
                                               scalar1=0.125, scalar2=None, op0=ALU.mult), [B_gtmp], [B_gains])
        tk.op("dve", lambda h: h.tensor_copy(out=gains[:, 1], in_=gtmp[:, 1:2, :].to_broadcast([128, 2, 64])),
              [B_gtmp], [B_gains])
        nhalf = arena_alloc([128, 4], F32)
        B_nhalf = Buf("nhalf")
        tk.op("dve", lambda h: h.memset(nhalf, -0.5), [], [B_nhalf])
        NSL = 2
        sqf = [arena_alloc([128, 256], F32) for _ in range(NSL)]
        B_sqf = [Buf(f"sqf{k}") for k in range(NSL)]
        ssb = [arena_alloc([128, 4], F32) for _ in range(NSL)]
        B_ssb = [Buf(f"ssb{k}") for k in range(NSL)]
        qkn = [arena_alloc([128, 256], F32) for _ in range(NSL)]
        B_qkn = [Buf(f"qkn{k}") for k in range(NSL)]
        qkb = [arena_alloc([128, 256], BF16) for _ in range(NSL)]
        B_qkb = [Buf(f"qkb{k}") for k in range(NSL)]
        vf = [arena_alloc([128, 128], F32) for _ in range(NSL)]
        B_vf = [Buf(f"vf{k}") for k in range(NSL)]
        sbt = [arena_alloc([128, 256], F32) for _ in range(2)]
        B_sbt = [Buf("sbt0"), Buf("sbt1")]
        Pt = [arena_alloc([128, 2, 256], BF16) for _ in range(3)]
        B_Pt = [Buf(f"Pt{k}") for k in range(3)]
        qkn_s = arena_alloc([NS, 256], F32)
        B_qkns = Buf("qkn_s")
        vf_s = arena_alloc([NS, 128], F32)
        B_vfs = Buf("vf_s")
        Kc = arena_alloc([128, NS, 128], BF16)
        B_Kc = Buf("Kc")
        Vc = arena_alloc([128, NS, 128], BF16)
        B_Vc = Buf("Vc")
        qdiag = arena_alloc([NS, NS, 128], BF16)
        B_qdiag = Buf("qdiag")
        prod = arena_alloc([128, 512], F32)
        B_prod = Buf("prod")
        scs = arena_alloc([128, NS, 2], F32)
        B_scs = Buf("scs")
        Ps = arena_alloc([128, NS, 2], BF16)
        B_Ps = Buf("Ps")
        Wt = Vc
        B_Wt = B_Vc
        sfs = arena_alloc([NS, 128], F32)
        B_sfs = Buf("sfs")
        pself = arena_alloc([NS, 2], F32)
        B_pself = Buf("pself")
        numS = arena_alloc([NS, 3, 128], F32)
        B_numS = Buf("numS")
        denS = arena_alloc([NS, 2], F32)
        B_denS = Buf("denS")
        oSb = arena_alloc([NS, 3, 128], BF16)
        B_oSb = Buf("oSb")
        tsl = 0
        sbi = 0
        for pair in range(2):
            for g, (win, dil) in enumerate(GROUPS):
                nblk = SEQ // dil // 128
                pg_i = pair * 3 + g
                ws = pg_i % 2
                colbase = g * 256 + pair * 128
                tk.dma("pool", [(wq[ws][:, :, wh, :],
                                 w_in[l, :, wh * 768 + colbase:wh * 768 + colbase + 128].rearrange("(k p) c -> p k c", p=128))
                                for wh in range(3)], writes=[B_wq[ws]])
                if pair == 0:
                    tk.dma("sp", [(nk_s[g][l, :, 0:win - 1, :], cache_k[g][l, :, 1:win, :]),
                                  (nv_s[g][l, :, 0:win - 1, :], cache_v[g][l, :, 1:win, :])])
                tk.dma("pool", Kc, cache_k[g][l].rearrange("s (j d) c -> j s d c", d=dil)[:, :, 0, pair * 128:(pair + 1) * 128],
                       writes=[B_Kc])
                tk.dma("pool", Vc, cache_v[g][l].rearrange("s (j d) c -> j s d c", d=dil)[:, :, 0, pair * 128:(pair + 1) * 128],
                       writes=[B_Vc])

                def tile_cols(s_):
                    if g == 0:
                        r, blk = 0, s_
                    elif g == 1:
                        r, blk = s_ // 4, s_ % 4
                    else:
                        r, blk = s_, 0
                    start = r + dil * 128 * blk
                    return r, blk, start

                for s_ in range(17):
                    k2 = tsl % NSL
                    tsl += 1
                    if s_ < 16:
                        r, blk, start = tile_cols(s_)
                        nrow = 128
                        cols = slice(start, start + dil * 127 + 1, dil)
                        hrd = [B_hT[kc][T] for kc in range(KC) for T in range(4)]
                    else:
                        nrow = NS
                        cols = slice(SEQ, NT)
                        hrd = [B_hT[kc][4] for kc in range(KC)]
                    pq, bpq = ps_get()
                    tk.group("pe", [
                        (lambda h, kc=kc: h.matmul(pq[0:nrow, 0:256], lhsT=hT[:, kc, cols], rhs=wq[ws][:, kc, 0:2, :],
                                                   start=(kc == 0), stop=(kc == KC - 1)))
                        for kc in range(KC)] + [
                        (lambda h, kc=kc: h.matmul(pq[0:nrow, 256:384], lhsT=hT[:, kc, cols], rhs=wq[ws][:, kc, 2, :],
                                                   start=(kc == 0), stop=(kc == KC - 1)))
                        for kc in range(KC)], [B_wq[ws]] + hrd, [bpq])
                    tk.op("act", lambda h: h.activation(out=sqf[k2][0:nrow, :], in_=pq[0:nrow, 0:256], func=AF.Square),
                          [bpq], [B_sqf[k2]])
                    tk.op("dve", lambda h: h.tensor_reduce(out=ssb[k2][0:nrow, :],
                                                           in_=sqf[k2][0:nrow, :].rearrange("p (a d) -> p a d", d=64),
                                                           axis=AX.X, op=ALU.add), [B_sqf[k2]], [B_ssb[k2]])
                    tk.op("act", lambda h: h.activation(out=ssb[k2][0:nrow, :], in_=ssb[k2][0:nrow, :], func=AF.Sqrt,
                                                        scale=1.0 / 64, bias=cst[0:nrow, 130:131]), [B_ssb[k2], B_cst], [B_ssb[k2]])
                    tk.op("dve", lambda h: h.reciprocal(out=ssb[k2][0:nrow, :], in_=ssb[k2][0:nrow, :]), [B_ssb[k2]], [B_ssb[k2]])
                    qdst = qkn[k2] if s_ < 16 else qkn_s
                    B_qdst = B_qkn[k2] if s_ < 16 else B_qkns
                    tk.op("dve", lambda h: h.tensor_tensor(
                        out=qdst[0:nrow, :].rearrange("p (a d) -> p a d", d=64),
                        in0=pq[0:nrow, 0:256].rearrange("p (a d) -> p a d", d=64),
                        in1=ssb[k2][0:nrow, :].unsqueeze(2).to_broadcast([nrow, 4, 64]), op=ALU.mult),
                        [bpq, B_ssb[k2]], [B_qdst])
                    tk.op("dve", lambda h: h.tensor_tensor(out=qdst[0:nrow, :], in0=qdst[0:nrow, :],
                                                           in1=gains[0:nrow].rearrange("p a b d -> p (a b d)"), op=ALU.mult),
                          [B_qdst, B_gains], [B_qdst])
                    if s_ < 16:
                        tk.op("act", lambda h: h.copy(out=qkb[k2], in_=qkn[k2]), [B_qkn[k2]], [B_qkb[k2]])
                        ptq, bptq = ps_get()
                        ptb = ptq[:].bitcast(BF16)
                        tk.group("pe", [
                            (lambda h, a=a: h.transpose(out=ptb[:, a * 128:(a + 1) * 128], in_=qkb[k2][:, a * 128:(a + 1) * 128],
                                                        identity=ident_b[:]))
                            for a in range(2)], [B_qkb[k2], B_identb], [bptq])
                        tk.op("act", lambda h: h.copy(out=qkT[:, :, s_ * 128:(s_ + 1) * 128],
                                                      in_=ptb[:, 0:256].rearrange("p (a c) -> p a c", c=128)),
                              [bptq], [B_qkT[s_]])
                        tk.op("act", lambda h: h.copy(out=Vg[:, s_, :], in_=pq[:, 256:384]), [bpq], [B_V[s_]])
                        is_out = (g == 0 and s_ == 15) or (g == 1 and s_ % 4 == 3) or g == 2
                        if is_out:
                            tk.op("dve", lambda h: h.tensor_copy(out=vf[k2], in_=pq[:, 256:384]), [bpq], [B_vf[k2]])
                            kdst = nk_p[g][l].rearrange("(i d) c -> d i c", d=dil)[r, :, pair * 128:(pair + 1) * 128]
                            vdst = nv_p[g][l].rearrange("(i d) c -> d i c", d=dil)[r, :, pair * 128:(pair + 1) * 128]
                            tk.dma("sp", kdst, qkn[k2][:, 128:256], reads=[B_qkn[k2]])
                            tk.dma("sp", vdst, vf[k2][:, :], reads=[B_vf[k2]])
                    else:
                        tk.op("dve", lambda h: h.tensor_copy(out=vf_s, in_=pq[0:NS, 256:384]), [bpq], [B_vfs])
                        tk.dma("sp", nk_s[g][l, :, win - 1, pair * 128:(pair + 1) * 128], qkn_s[:, 128:256], reads=[B_qkns])
                        tk.dma("sp", nv_s[g][l, :, win - 1, pair * 128:(pair + 1) * 128], vf_s[:, :], reads=[B_vfs])

                if stop_after == ("mixA1", l):
                    raise StopBuild()
                prevP = None
                for s_ in range(16):
                    r, blk, start = tile_cols(s_)
                    nq = 2 if blk < nblk - 1 else 1
                    pk = s_ % 3
                    for hh in range(2):
                        import os
                        hb = slice(hh * 64, (hh + 1) * 64)
                        hidx = g * 4 + pair * 2 + hh
                        pS, bpS = ps_get()
                        if os.environ.get("DBG_PRINT") and s_ < 3 and pair == 0 and g == 0:
                            print("S bank", s_, hh, (ps_i[0] - 1) % 8, flush=True)
                        if os.environ.get("DBG_DRAIN"):
                            nc.tensor.drain()
                        sx = s_
                        if os.environ.get("DBG_FAKE"):
                            sx = int(os.environ.get("DBG_FAKE"))
                        hbx = hb
                        if os.environ.get("DBG_FULLK"):
                            hbx = slice(0, 128)
                        if os.environ.get("DBG_ONEMM") and s_ >= 1 and hh == 1:
                            continue
                        if os.environ.get("DBG_ALTDVE") and s_ >= 1:
                            tk.op("dve", lambda h: h.memset(sbt[0][:, :], 0.0), [], [B_sbt[0]])
                            tk.op("act", lambda h: h.copy(out=sbt[1][:, :], in_=sbt[0][:, :]), [B_sbt[0]], [B_sbt[1]])
                            continue
                        if os.environ.get("DBG_ALTMM") and s_ >= 1:
                            tk.group("pe", [lambda h: h.matmul(pS[:, 0:256], lhsT=ones_b[:, :], rhs=ident_b[:, 0:128].unsqueeze(1).to_broadcast([128, 2, 128]),
                                                               start=True, stop=True)], [B_onesb, B_identb], [bpS])
                            continue
                        tk.group("pe", [lambda h: h.matmul(pS[:, 0:nq * 128], lhsT=qkT[hbx, 1, sx * 128:(sx + 1) * 128],
                                                           rhs=qkT[hbx, 0, sx * 128:(sx + nq) * 128], start=True, stop=True)],
                                 [B_qkT[s_]] + ([B_qkT[s_ + 1]] if nq == 2 else []), [bpS])
                        import os
                        if os.environ.get("DBG_SKIP") == "stt" and s_ >= 1:
                            continue
                        kb = sbi % 2
                        sbi += 1
                        tk.op("dve", lambda h: h.scalar_tensor_tensor(
                            out=sbt[kb][:, 0:nq * 128], in0=DISTG[g][:, 0:nq * 128], scalar=-SLOPES[hidx] * dil,
                            in1=pS[:, 0:nq * 128], op0=ALU.mult, op1=ALU.add), [B_mc, bpS], [B_sbt[kb]])
                        import os
                        if os.environ.get("DBG_SKIP") == "exp" and s_ >= 1:
                            continue
                        if os.environ.get("DBG_CLAMP"):
                            tk.op("dve", lambda h: h.tensor_scalar(out=sbt[kb][:, 0:nq * 128], in0=sbt[kb][:, 0:nq * 128],
                                                                   scalar1=float(os.environ.get("DBG_CLAMP")), scalar2=None, op0=ALU.max),
                                  [B_sbt[kb]], [B_sbt[kb]])
                        if os.environ.get("DBG_NOEXP"):
                            fnn = getattr(AF, os.environ.get("DBG_NOEXP"))
                            tk.op("act", lambda h: h.activation(out=Pt[pk][:, hh, 0:nq * 128], in_=sbt[kb][:, 0:nq * 128], func=fnn,
                                                                scale=float(os.environ.get("DBG_SCALE", "1.0"))),
                                  [B_sbt[kb]], [B_Pt[pk]])
                        else:
                            tk.op("act", lambda h: h.activation(out=Pt[pk][:, hh, 0:nq * 128], in_=sbt[kb][:, 0:nq * 128], func=AF.Exp),
                                  [B_sbt[kb]], [B_Pt[pk]])
                    if stop_after[0] == "mixA2s" and s_ == stop_after[2]:
                        raise StopBuild()
                    if stop_after == ("mixA2a", l):
                        dbg = nc.dram_tensor("dbg", [128, 512], F32, kind="ExternalOutput").ap()
                        tk.dma("sp", dbg[:, 0:256], sbt[0][:, :], reads=[B_sbt[0]])
                        tk.dma("sp", dbg[:, 256:512], sbt[1][:, :], reads=[B_sbt[1]])
                        raise StopBuild()
                    pO, bpO = ps_get()
                    mms = []
                    for hh in range(2):
                        hb = slice(hh * 64, (hh + 1) * 64)
                        for kind in range(2):
                            oc = slice(kind * 128, (kind + 1) * 128)
                            import os
                            DBGNP = os.environ.get("DBG_NOPREV")
                            FULLM = os.environ.get("DBG_FULLM")
                            vsl = slice(0, 128) if FULLM else slice(hh * 64, (hh + 1) * 64)
                            osl = slice(0, 128) if FULLM else slice(0, 64)
                            hbo = slice(0, 128) if FULLM else hb
                            if blk > 0 and not DBGNP:
                                lhs_prev = Vg[:, s_ - 1, vsl] if kind == 0 else ones_b[:, osl]
                                mms.append(lambda h, lhs_prev=lhs_prev, hbo=hbo, oc=oc, hh=hh: h.matmul(
                                    pO[hbo, oc], lhsT=lhs_prev, rhs=Pt[prevP][:, hh, 128:256], start=True, stop=False))
                            lhs_cur = Vg[:, s_, vsl] if kind == 0 else ones_b[:, osl]
                            mms.append(lambda h, lhs_cur=lhs_cur, hbo=hbo, oc=oc, hh=hh: h.matmul(
                                pO[hbo, oc], lhsT=lhs_cur, rhs=Pt[pk][:, hh, 0:128], start=(blk == 0 or bool(DBGNP)), stop=True))
                    rds = [B_V[s_], B_Pt[pk], B_onesb]
                    if blk > 0:
                        rds += [B_V[s_ - 1], B_Pt[prevP]]
                    tk.group("pe", mms, rds, [bpO])
                    if stop_after == ("mixA2b", l):
                        raise StopBuild()
                    if stop_after[0] == "mixA2n" and s_ + 1 == stop_after[2]:
                        raise StopBuild()
                    ocols = slice(start, start + dil * 127 + 1, dil)
                    import os
                    if os.environ.get("DBG_NOEVAC"):
                        prevP = pk
                        if stop_after[0] == "mixA2n" and s_ + 1 == stop_after[2]:
                            raise StopBuild()
                        continue
                    EV = os.environ.get("DBG_EVAC", "")
                    if EV != "dveonly":
                        tk.op("act", lambda h: h.copy(out=OT[:, g * 2 + pair, ocols], in_=pO[:, 0:128]), [bpO], [B_OT[g * 2 + pair]])
                    if EV == "actonly":
                        pass
                    elif EV == "act":
                        tk.op("act", lambda h: h.copy(out=Dsum[:, ocols], in_=pO[:, 128:256]), [bpO], [B_Ds])
                    elif g == 0:
                        tk.op("dve", lambda h: h.tensor_copy(out=Dsum[:, ocols], in_=pO[:, 128:256]), [bpO], [B_Ds])
                    else:
                        tk.op("dve", lambda h: h.tensor_tensor(out=Dsum[:, ocols], in0=pO[:, 128:256], in1=Dsum[:, ocols], op=ALU.add),
                              [bpO, B_Ds], [B_Ds])
                    prevP = pk

                if stop_after == ("mixA2", l):
                    raise StopBuild()
                tk.op("dve", lambda h: h.tensor_tensor(
                    out=qdiag, in0=qkn_s[:, 0:128].unsqueeze(1).to_broadcast([NS, NS, 128]),
                    in1=EYE16.unsqueeze(2).to_broadcast([NS, NS, 128]), op=ALU.mult), [B_qkns, B_mc], [B_qdiag])
                for b4 in range(4):
                    pb, bpb = ps_get()
                    tk.group("pe", [lambda h: h.matmul(pb[:, :], lhsT=ones_b[0:NS, :], rhs=qdiag[:, b4 * 4:(b4 + 1) * 4, :],
                                                       start=True, stop=True)], [B_qdiag, B_onesb], [bpb])
                    tk.op("dve", lambda h: h.tensor_tensor(out=prod, in0=Kc[:, b4 * 4:(b4 + 1) * 4, :].rearrange("p s c -> p (s c)"),
                                                           in1=pb[:, :], op=ALU.mult), [B_Kc, bpb], [B_prod])
                    tk.op("dve", lambda h: h.tensor_reduce(out=scs[:, b4 * 4:(b4 + 1) * 4, :].rearrange("p s a -> p (s a)"),
                                                           in_=prod.rearrange("p (a d) -> p a d", d=64), axis=AX.X, op=ALU.add),
                          [B_prod], [B_scs])
                h0 = g * 4 + pair * 2
                tk.op("dve", lambda h: h.tensor_tensor(out=scs, in0=scs,
                                                       in1=SBIAS[:, h0:h0 + 2].unsqueeze(1).to_broadcast([128, NS, 2]), op=ALU.add),
                      [B_scs, B_mc], [B_scs])
                tk.op("act", lambda h: h.activation(out=Ps, in_=scs, func=AF.Exp), [B_scs], [B_Ps])
                tk.op("dve", lambda h: h.tensor_tensor(
                    out=Wt.rearrange("p s (a d) -> p (s a) d", d=64), in0=Vc.rearrange("p s (a d) -> p (s a) d", d=64),
                    in1=Ps.rearrange("p s a -> p (s a)").unsqueeze(2).to_broadcast([128, NS * 2, 64]), op=ALU.mult),
                    [B_Vc, B_Ps], [B_Vc])
                pN, bpN = ps_get()
                tk.group("pe", [
                    (lambda h, s2=s2: h.matmul(pN[0:NS, 0:128], lhsT=zib[:, 15 - s2:31 - s2], rhs=Wt[:, s2, :],
                                               start=(s2 == 0), stop=(s2 == NS - 1))) for s2 in range(NS)] + [
                    (lambda h, s2=s2: h.matmul(pN[0:NS, 128:130], lhsT=zib[:, 15 - s2:31 - s2], rhs=Ps[:, s2, :],
                                               start=(s2 == 0), stop=(s2 == NS - 1))) for s2 in range(NS)],
                    [B_zib, B_Wt, B_Ps], [bpN])
                tk.op("dve", lambda h: h.tensor_tensor(out=sfs, in0=qkn_s[:, 0:128], in1=qkn_s[:, 128:256], op=ALU.mult),
                      [B_qkns], [B_sfs])
                tk.op("dve", lambda h: h.tensor_reduce(out=pself, in_=sfs.rearrange("p (a d) -> p a d", d=64), axis=AX.X, op=ALU.add),
                      [B_sfs], [B_pself])
                tk.op("act", lambda h: h.activation(out=pself, in_=pself, func=AF.Exp), [B_pself], [B_pself])
                tk.op("dve", lambda h: h.tensor_tensor(
                    out=sfs.rearrange("p (a d) -> p a d", d=64), in0=vf_s.rearrange("p (a d) -> p a d", d=64),
                    in1=pself.unsqueeze(2).to_broadcast([NS, 2, 64]), op=ALU.mult), [B_vfs, B_pself], [B_sfs])
                tk.op("dve", lambda h: h.tensor_tensor(out=numS[:, g, :], in0=pN[0:NS, 0:128], in1=sfs, op=ALU.add),
                      [bpN, B_sfs], [B_numS])
                if g == 0:
                    tk.op("dve", lambda h: h.tensor_tensor(out=denS, in0=pN[0:NS, 128:130], in1=pself, op=ALU.add),
                          [bpN, B_pself], [B_denS])
                else:
                    tk.op("dve", lambda h: h.tensor_tensor(out=pself, in0=pN[0:NS, 128:130], in1=pself, op=ALU.add),
                          [bpN, B_pself], [B_pself])
                    tk.op("dve", lambda h: h.tensor_tensor(out=denS, in0=denS, in1=pself, op=ALU.add),
                          [B_denS, B_pself], [B_denS])
                if stop_after == ("mixA3", l):
                    raise StopBuild()
            tk.op("dve", lambda h: h.reciprocal(out=Dsum, in_=Dsum), [B_Ds], [B_Ds])
            for g in range(3):
                tk.op("dve", lambda h: h.tensor_tensor(out=OT[:, g * 2 + pair, 0:SEQ], in0=OT[:, g * 2 + pair, 0:SEQ], in1=Dsum,
                                                       op=ALU.mult), [B_OT[g * 2 + pair], B_Ds], [B_OT[g * 2 + pair]])
            tk.op("dve", lambda h: h.reciprocal(out=denS, in_=denS), [B_denS], [B_denS])
            tk.op("dve", lambda h: h.tensor_tensor(
                out=oSb.rearrange("p g (a d) -> p g a d", d=64), in0=numS.rearrange("p g (a d) -> p g a d", d=64),
                in1=denS.unsqueeze(1).unsqueeze(3).to_broadcast([NS, 3, 2, 64]), op=ALU.mult), [B_numS, B_denS], [B_oSb])
            ptq, bptq = ps_get()
            ptb = ptq[:].bitcast(BF16)
            tk.group("pe", [
                (lambda h, g=g: h.transpose(out=ptb[:, g * 32:g * 32 + NS], in_=oSb[:, g, :], identity=ident_b[0:NS, 0:NS]))
                for g in range(3)], [B_oSb, B_identb], [bptq])
            for g in range(3):
                tk.op("act", lambda h: h.copy(out=OT[:, g * 2 + pair, SEQ:NT], in_=ptb[:, g * 32:g * 32 + NS]), [bptq], [B_OTs])

        if stop_after == ("mixA", l):
            raise StopBuild()
        barrier()
        arena_off[0] = mark2
        mT = arena_alloc([128, KC, NT], BF16)
        B_mT = [[Buf(f"mT{c}_{t}") for t in range(5)] for c in range(KC)]
        wc = [arena_alloc([128, 24, 256], BF16) for _ in range(2)]
        B_wc = [Buf("wc0"), Buf("wc1")]
        sga = [arena_alloc([128, 512], F32) for _ in range(2)]
        B_sga = [Buf("sga0"), Buf("sga1")]
        sgp = [arena_alloc([128, 512], F32) for _ in range(2)]
        B_sgp = [Buf("sgp0"), Buf("sgp1")]
        tmpb = arena_alloc([128, NS], F32)
        B_tmpb = Buf("tmpb2")
        ti = 0
        for qc in range(4):
            ws = qc % 2
            cs = slice(qc * 256, (qc + 1) * 256)
            tk.dma("pool", [
                (wc[ws][:, 0:8, :], w_in[l, :, 2560 + qc * 256:2560 + (qc + 1) * 256].rearrange("(k p) c -> p k c", p=128)),
                (wc[ws][:, 8:16, :], w_in[l, :, 3584 + qc * 256:3584 + (qc + 1) * 256].rearrange("(k p) c -> p k c", p=128)),
                (wc[ws][:, 16:22, :], w_oa[l, :, cs].rearrange("(k p) c -> p k c", p=128)),
                (wc[ws][:, 22:24, :], w_op[l, :, cs].rearrange("(k p) c -> p k c", p=128))], writes=[B_wc[ws]])
            for cc in range(2):
                c = qc * 2 + cc
                wsl = slice(cc * 128, (cc + 1) * 128)
                for T, (c0, n) in enumerate(TT):
                    hrd = [B_hT[kc][T] for kc in range(KC)]
                    pga, bpga = ps_get()
                    tk.group("pe", [
                        (lambda h, kc=kc: h.matmul(pga[:, 0:n], lhsT=wc[ws][:, kc, wsl], rhs=hT[:, kc, c0:c0 + n],
                                                   start=(kc == 0), stop=(kc == KC - 1))) for kc in range(KC)],
                        [B_wc[ws]] + hrd, [bpga])
                    pgp, bpgp = ps_get()
                    tk.group("pe", [
                        (lambda h, kc=kc: h.matmul(pgp[:, 0:n], lhsT=wc[ws][:, 8 + kc, wsl], rhs=hT[:, kc, c0:c0 + n],
                                                   start=(kc == 0), stop=(kc == KC - 1))) for kc in range(KC)],
                        [B_wc[ws]] + hrd, [bpgp])
                    pa, bpa = ps_get()
                    tk.group("pe", [
                        (lambda h, k6=k6: h.matmul(pa[:, 0:n], lhsT=wc[ws][:, 16 + k6, wsl], rhs=OT[:, k6, c0:c0 + n],
                                                   start=(k6 == 0), stop=(k6 == 5))) for k6 in range(6)],
                        [B_wc[ws]] + (B_OT if T < 4 else [B_OTs]), [bpa])
                    pp, bpp = ps_get()
                    tk.group("pe", [
                        (lambda h, k2=k2: h.matmul(pp[:, 0:n], lhsT=wc[ws][:, 22 + k2, wsl], rhs=pool_yT[:, k2, c0:c0 + n],
                                                   start=(k2 == 0), stop=(k2 == 1))) for k2 in range(2)],
                        [B_wc[ws], B_py[T]], [bpp])
                    k2 = ti % 2
                    ti += 1
                    tk.op("act", lambda h: h.activation(out=sga[k2][:, 0:n], in_=pga[:, 0:n], func=AF.Sigmoid), [bpga], [B_sga[k2]])
                    tk.op("act", lambda h: h.activation(out=sgp[k2][:, 0:n], in_=pgp[:, 0:n], func=AF.Sigmoid), [bpgp], [B_sgp[k2]])
                    tk.op("dve", lambda h: h.tensor_tensor(out=sga[k2][:, 0:n], in0=pa[:, 0:n], in1=sga[k2][:, 0:n], op=ALU.mult),
                          [bpa, B_sga[k2]], [B_sga[k2]])
                    tk.op("dve", lambda h: h.tensor_tensor(out=sgp[k2][:, 0:n], in0=pp[:, 0:n], in1=sgp[k2][:, 0:n], op=ALU.mult),
                          [bpp, B_sgp[k2]], [B_sgp[k2]])
                    tk.op("dve", lambda h: h.tensor_tensor(out=mT[:, c, c0:c0 + n], in0=sga[k2][:, 0:n], in1=sgp[k2][:, 0:n], op=ALU.add),
                          [B_sga[k2], B_sgp[k2]], [B_mT[c][T]])
        for qc in range(4):
            ws = qc % 2
            tk.dma("pool", wc[ws][:, 0:8, :], w_out[l, :, qc * 256:(qc + 1) * 256].rearrange("(k p) c -> p k c", p=128),
                   writes=[B_wc[ws]])
            for cc in range(2):
                c = qc * 2 + cc
                wsl = slice(cc * 128, (cc + 1) * 128)
                for T, (c0, n) in enumerate(TT):
                    pm, bpm = ps_get()
                    tk.group("pe", [
                        (lambda h, kc=kc: h.matmul(pm[:, 0:n], lhsT=wc[ws][:, kc, wsl], rhs=mT[:, kc, c0:c0 + n],
                                                   start=(kc == 0), stop=(kc == KC - 1))) for kc in range(KC)],
                        [B_wc[ws]] + [B_mT[kc][T] for kc in range(KC)], [bpm])
                    resid_update(l, 1, pm, bpm, c, T, c0, n, tmpb, B_tmpb)

    done = False
    for l in range(DEPTH):
        adaln(l)
        ffn(l, 0)
        if stop_after == ("ffn1", l):
            done = True
            break
        try:
            mixer(l)
        except StopBuild:
            done = True
            break
        if stop_after == ("mix", l):
            done = True
            break
        ffn(l, 1)
        if stop_after == ("ffn2", l):
            done = True
            break

    barrier()
    arena_reset()
    yst = [arena_alloc([128, D], F32) for _ in range(2)]
    B_yst = [Buf(f"yst{i}") for i in range(2)]
    for tt in range(17):
        s = tt % 2
        nrow = 128 if tt < 16 else NS
        T = tt // 4 if tt < 16 else 4
        col0 = tt * 128
        for half in range(2):
            pt, bpt = ps_get()
            tk.group("pe", [
                (lambda h, q=q: h.transpose(out=pt[0:nrow, q * 128:(q + 1) * 128],
                                            in_=xT[:, half * 4 + q, col0:col0 + nrow], identity=IDF))
                for q in range(4)], [B_xT[half * 4 + q][T] for q in range(4)] + [B_cst], [bpt])
            if half == 0:
                tk.op("act", lambda h: h.copy(out=yst[s][0:nrow, 0:512], in_=pt[0:nrow, :]), [bpt], [B_yst[s]])
            else:
                tk.op("dve", lambda h: h.tensor_copy(out=yst[s][0:nrow, 512:1024], in_=pt[0:nrow, :]), [bpt], [B_yst[s]])
        dst = y_p[tt * 128:(tt + 1) * 128, :] if tt < 16 else y_s[:, :]
        tk.dma("sp", dst, yst[s][0:nrow, :], reads=[B_yst[s]])

    tk.finish()
    return nc


def make_consts():
    c = np.zeros((128, 256), np.float32)
    c[:, 0:128] = np.eye(128, dtype=np.float32)
    c[:, 130] = EPS
    return c


def make_mconsts():
    m = np.zeros((128, MCW), np.float32)
    k = np.arange(128)[:, None]
    q = np.arange(128)[None, :]
    for g, (win, dil) in enumerate(GROUPS):
        BIG = 70.0 / (SLOPES[g * 4 + 3] * dil)
        m[:, g * 256:g * 256 + 128] = np.where(q >= k, (q - k).astype(np.float32), BIG)
        m[:, g * 256 + 128:(g + 1) * 256] = np.where(q <= k, (128 + q - k).astype(np.float32), BIG)
    for p in range(128):
        for c in range(2):
            w = POOL_WINDOWS[2 * c + p // 64]
            for t in range(16):
                m[p, 768 + c * 16 + t] = 1.0 / min(w, t + 1)
    for g, (win, dil) in enumerate(GROUPS):
        for hh in range(4):
            m[:, 800 + g * 4 + hh] = -SLOPES[g * 4 + hh] * dil * (128 - np.arange(128))
    m[0:16, 816:832] = np.eye(16, dtype=np.float32)
    m[:, 832 + 15] = 1.0
    return m


def shard_inputs(inp):
    consts = make_consts()
    mconsts = make_mconsts()
    maps = []
    for c in range(NCORES):
        sl = slice(NS * c, NS * (c + 1))
        m = {
            "x_p": np.ascontiguousarray(inp["x_prompt"][c]),
            "x_s": np.ascontiguousarray(inp["x_sample"][sl, 0, :]),
            "c_all": np.ascontiguousarray(np.concatenate([inp["c_prompt"][c:c + 1], inp["c_sample"][sl]], axis=0)),
            "state_pool": np.ascontiguousarray(inp["state_pool"][:, sl]).reshape(DEPTH, NS * 15, 256),
            "w_ada": inp["w_ada"],
            "b_ada": np.ascontiguousarray(inp["b_ada"]).reshape(DEPTH, 72, 128),
            "norm_g": np.ascontiguousarray(inp["norm_g"]).reshape(DEPTH, 24, 128),
            "ffn_w1": inp["ffn_w1"],
            "ffn_w2": inp["ffn_w2"],
            "w_in": inp["w_in"],
            "qk_g": np.ascontiguousarray(np.stack([inp["q_norm_g"], inp["k_norm_g"]], axis=1)),
            "w_oa": inp["w_oa"],
            "pool_w": inp["pool_w"],
            "pool_scale": np.ascontiguousarray(inp["pool_scale"]).reshape(DEPTH, 2, 128),
            "w_op": inp["w_op"],
            "w_out": inp["w_out"],
            "consts": consts,
            "mconsts": mconsts,
        }
        for g in range(3):
            W = GROUPS[g][0]
            m[f"ck{g}"] = np.ascontiguousarray(inp[f"cache_k_g{g}"][:, sl]).reshape(DEPTH, NS, W, 256)
            m[f"cv{g}"] = np.ascontiguousarray(inp[f"cache_v_g{g}"][:, sl]).reshape(DEPTH, NS, W, 256)
        maps.append(m)
    return maps


def gather_outputs(results):
    def cat(name, axis, shape_tail=None):
        return np.stack([r[name] for r in results], axis=axis)

    y_p = np.stack([r["y_p"] for r in results], axis=0)
    y_s = np.concatenate([r["y_s"] for r in results], axis=0).reshape(NCORES * NS, 1, D)
    outs = [y_p, y_s]
    for g in range(3):
        W = GROUPS[g][0]
        outs.append(np.stack([r[f"nkp{g}"] for r in results], axis=1).reshape(DEPTH, NCORES, W, 4, HD))
        outs.append(np.stack([r[f"nvp{g}"] for r in results], axis=1).reshape(DEPTH, NCORES, W, 4, HD))
    outs.append(np.stack([r["npool_p"] for r in results], axis=1))
    for g in range(3):
        W = GROUPS[g][0]
        outs.append(np.concatenate([r[f"nks{g}"] for r in results], axis=1).reshape(DEPTH, NCORES * NS, W, 4, HD))
        outs.append(np.concatenate([r[f"nvs{g}"] for r in results], axis=1).reshape(DEPTH, NCORES * NS, W, 4, HD))
    outs.append(np.concatenate([r["npool_s"] for r in results], axis=1))
    return tuple(np.ascontiguousarray(o, dtype=np.float32) for o in outs)


def kernel(**inputs):
    inp = {k: np.asarray(v) for k, v in inputs.items()}
    nc = build_program()
    maps = shard_inputs(inp)
    res = run_bass_kernel_spmd(nc, maps, core_ids=list(range(NCORES)))
    return gather_outputs(res.results)
```

```python
import contextlib
import numpy as np
import concourse.bass as bass
import concourse.mybir as mybir
from concourse.bass_utils import run_bass_kernel_spmd

F32 = mybir.dt.float32
BF16 = mybir.dt.bfloat16
AF = mybir.ActivationFunctionType
ALU = mybir.AluOpType
AX = mybir.AxisListType

D = 1024
KC = 8
SEQ = 2048
NS = 16
NT = SEQ + NS
DFF = 2816
NJ = 22
DEPTH = 2
HD = 64
NCORES = 8
IN_W = 4608
EPS = 1e-6
GROUPS = ((128, 1), (512, 4), (2048, 16))
TT = [(0, 512), (512, 512), (1024, 512), (1536, 512), (2048, 16)]
FF_PARTS = [(0, 6), (6, 6), (12, 5), (17, 5)]
SLOPES = [float(np.power(2.0, -8.0 * h / 12.0)) for h in range(1, 13)]
NEG = -30000.0
MCW = 864
POOL_WINDOWS = (2, 4, 8, 16)


class BufT:
    __slots__ = ("name", "lw", "rd", "wsem", "wcnt", "rsem", "rcnt", "excl")

    def __init__(self, name, excl=False):
        self.name = name
        self.excl = excl
        self.lw = None
        self.rd = {}
        self.wsem = None
        self.wcnt = 0
        self.rsem = None
        self.rcnt = 0


class Eng:
    def __init__(self, hw, sem, name):
        self.hw = hw
        self.sem = sem
        self.cnt = 0
        self.seen = {}
        self.name = name


class TK:
    def __init__(self, nc, es):
        self.nc = nc
        self.es = es
        self.eng = {}
        for nm, hw in (("pe", nc.tensor), ("act", nc.scalar), ("dve", nc.vector),
                       ("pool", nc.gpsimd), ("sp", nc.sync)):
            self.eng[nm] = Eng(hw, es.enter_context(nc.semaphore("e_" + nm)), nm)
        self.sems = {}
        self.dram_sems = []
        self.out_events = []

    def new_sem(self, name):
        s = self.es.enter_context(self.nc.semaphore(name))
        self.sems[s.num] = s
        return s

    def _collect(self, e, reads, writes):
        need = {}

        def add(ev):
            if ev is None:
                return
            sem, val = ev
            k = sem.num
            if k not in need or need[k][1] < val:
                need[k] = (sem, val)

        for b in reads:
            add(b.lw)
            if b.excl:
                for k, ev in b.rd.items():
                    if k != e.sem.num:
                        add(ev)
        for b in writes:
            add(b.lw)
            for k, ev in b.rd.items():
                add(ev)
        out = []
        for k, (sem, val) in need.items():
            if e.seen.get(k, 0) < val:
                out.append((sem, val))
                e.seen[k] = val
        return out

    def _emit_waits(self, e, waits):
        for sem, val in waits:
            e.hw.wait_ge(sem, val)

    def op(self, en, fn, reads=(), writes=()):
        e = self.eng[en]
        self._emit_waits(e, self._collect(e, reads, writes))
        inst = fn(e.hw)
        e.cnt += 1
        inst.then_inc(e.sem, 1)
        ev = (e.sem, e.cnt)
        for b in reads:
            b.rd[e.sem.num] = ev
        for b in writes:
            b.lw = ev
            b.rd = {}
        return inst

    def group(self, en, fns, reads=(), writes=()):
        e = self.eng[en]
        self._emit_waits(e, self._collect(e, reads, writes))
        inst = None
        for fn in fns:
            inst = fn(e.hw)
        e.cnt += 1
        inst.then_inc(e.sem, 1)
        ev = (e.sem, e.cnt)
        for b in reads:
            b.rd[e.sem.num] = ev
        for b in writes:
            b.lw = ev
            b.rd = {}

    def dma(self, qn, out, in_=None, reads=(), writes=(), **kw):
        e = self.eng[qn]
        pairs = out if in_ is None else [(out, in_)]
        self._emit_waits(e, self._collect(e, reads, writes))
        if writes:
            b = writes[0]
            if b.wsem is None:
                b.wsem = self.new_sem("w_" + b.name)
            for o, i in pairs:
                e.hw.dma_start(out=o, in_=i, **kw).then_inc(b.wsem, 16)
                b.wcnt += 16
            ev = (b.wsem, b.wcnt)
            for w in writes:
                w.lw = ev
                w.rd = {}
            for r in reads:
                r.rd[b.wsem.num] = ev
        elif reads:
            b = reads[0]
            if b.rsem is None:
                b.rsem = self.new_sem("r_" + b.name)
            for o, i in pairs:
                e.hw.dma_start(out=o, in_=i, **kw).then_inc(b.rsem, 16)
                b.rcnt += 16
            ev = (b.rsem, b.rcnt)
            for r in reads:
                r.rd[b.rsem.num] = ev
            self.out_events.append(ev)
        else:
            if not self.dram_sems:
                self.dram_sems.append([self.new_sem("dram"), 0])
            ds = self.dram_sems[0]
            for o, i in pairs:
                e.hw.dma_start(out=o, in_=i, **kw).then_inc(ds[0], 16)
                ds[1] += 16

    def finish(self):
        e = self.eng["sp"]
        last = {}
        for sem, val in self.out_events:
            if sem.num not in last or last[sem.num][1] < val:
                last[sem.num] = (sem, val)
        for ds in self.dram_sems:
            last[ds[0].num] = (ds[0], ds[1])
        for sem, val in last.values():
            e.hw.wait_ge(sem, val)
        for nm in ("pe", "act", "dve", "pool"):
            o = self.eng[nm]
            if o.cnt:
                e.hw.wait_ge(o.sem, o.cnt)


class StopBuild(Exception):
    pass


class Ring:
    def __init__(self, bufs):
        self.bufs = bufs
        self.i = 0

    def get(self):
        b = self.bufs[self.i % len(self.bufs)]
        self.i += 1
        return b


def build_program(stop_after=(None, None)):
    nc = bass.Bass("TRN2", target_bir_lowering=False)
    es = contextlib.ExitStack()
    nc._es = es
    tk = TK(nc, es)

    _bufs = {}

    def Buf(name, excl=False):
        if name not in _bufs:
            _bufs[name] = BufT(name, excl)
        return _bufs[name]

    def din(name, shape):
        return nc.dram_tensor(name, list(shape), F32, kind="ExternalInput").ap()

    def dout(name, shape):
        return nc.dram_tensor(name, list(shape), F32, kind="ExternalOutput").ap()

    x_p = din("x_p", [SEQ, D])
    x_s = din("x_s", [NS, D])
    c_all = din("c_all", [NS + 1, D])
    cache_k = [din(f"ck{g}", [DEPTH, NS, GROUPS[g][0], 256]) for g in range(3)]
    cache_v = [din(f"cv{g}", [DEPTH, NS, GROUPS[g][0], 256]) for g in range(3)]
    state_pool = din("state_pool", [DEPTH, NS * 15, 256])
    w_ada = din("w_ada", [DEPTH, D, 9 * D])
    b_ada = din("b_ada", [DEPTH, 72, 128])
    norm_g = din("norm_g", [DEPTH, 24, 128])
    ffn_w1 = din("ffn_w1", [DEPTH, 2, D, 2 * DFF])
    ffn_w2 = din("ffn_w2", [DEPTH, 2, DFF, D])
    w_in = din("w_in", [DEPTH, D, IN_W])
    qk_g = din("qk_g", [DEPTH, 2, HD])
    w_oa = din("w_oa", [DEPTH, 768, D])
    pool_w = din("pool_w", [DEPTH, 4, 64, 64])
    pool_scale = din("pool_scale", [DEPTH, 2, 128])
    w_op = din("w_op", [DEPTH, 256, D])
    w_out = din("w_out", [DEPTH, D, D])
    consts = din("consts", [128, 256])
    mconsts = din("mconsts", [128, MCW])

    y_p = dout("y_p", [SEQ, D])
    y_s = dout("y_s", [NS, D])
    nk_p = [dout(f"nkp{g}", [DEPTH, GROUPS[g][0], 256]) for g in range(3)]
    nv_p = [dout(f"nvp{g}", [DEPTH, GROUPS[g][0], 256]) for g in range(3)]
    npool_p = dout("npool_p", [DEPTH, 15, 256])
    nk_s = [dout(f"nks{g}", [DEPTH, NS, GROUPS[g][0], 256]) for g in range(3)]
    nv_s = [dout(f"nvs{g}", [DEPTH, NS, GROUPS[g][0], 256]) for g in range(3)]
    npool_s = dout("npool_s", [DEPTH, NS, 15, 256])

    def sb(name, shape, dt=F32):
        return es.enter_context(nc.sbuf_tensor(name, list(shape), dt))

    xT = sb("xT", [128, KC, NT])
    hT = sb("hT", [128, KC, NT], BF16)
    modT1 = sb("modT", [128, 1, 72, NS + 1])
    scT = sb("scT", [128, KC, NS + 1], BF16)
    Amod = sb("Amod", [128, 3, KC, NS + 1])
    gT = sb("gT", [128, DEPTH, 24])
    badaT = sb("badaT", [128, DEPTH, 72])
    cst = sb("cst", [128, 256])
    ident_b = sb("ident_b", [128, 128], BF16)
    ones_b = sb("ones_b", [128, 128], BF16)
    ARENA_BYTES = 100 * 1024
    arena = sb("arena", [128, ARENA_BYTES // 4])
    arena_off = [0]

    B_xT = [[Buf(f"xT{m}_{t}") for t in range(5)] for m in range(KC)]
    B_hT = [[Buf(f"hT{m}_{t}") for t in range(5)] for m in range(KC)]
    B_modT = Buf("modT")
    B_Amod = Buf("Amod")
    B_gT = Buf("gT")
    B_bada = Buf("bada")
    B_cst = Buf("cst")
    B_identb = Buf("identb")
    B_onesb = Buf("onesb")

    def arena_reset():
        arena_off[0] = 0

    def arena_alloc(shape, dt):
        esz = 4 if dt == F32 else 2
        n = int(np.prod(shape[1:]))
        nbytes = (n * esz + 3) // 4 * 4
        o = arena_off[0]
        assert o + nbytes <= ARENA_BYTES, (o, nbytes, shape)
        arena_off[0] = o + nbytes
        v = arena[:, o // 4:(o + nbytes) // 4]
        if dt != F32:
            v = v.bitcast(dt)
        v = v[:, 0:n]
        if len(shape) > 2:
            names = " ".join(f"d{i}" for i in range(len(shape) - 1))
            kw = {f"d{i}": shape[i + 1] for i in range(len(shape) - 1)}
            v = v.rearrange(f"p ({names}) -> p {names}", **kw)
        return v[0:shape[0]]

    psum = [es.enter_context(nc.psum_tensor(f"ps{i}", [128, 512], F32)) for i in range(8)]
    B_ps = [Buf(f"ps{i}", excl=True) for i in range(8)]
    ps_i = [0]

    def ps_get():
        i = ps_i[0] % 8
        ps_i[0] += 1
        return psum[i], B_ps[i]

    def barrier():
        names = ("pe", "act", "dve", "pool", "sp")
        last = {}
        for sem, val in tk.out_events:
            if sem.num not in last or last[sem.num][1] < val:
                last[sem.num] = (sem, val)
        for a in names:
            ea = tk.eng[a]
            for k, (sem, val) in last.items():
                if ea.seen.get(k, 0) < val:
                    ea.hw.wait_ge(sem, val)
                    ea.seen[k] = val
            for b in names:
                if a == b:
                    continue
                eb = tk.eng[b]
                if eb.cnt and ea.seen.get(eb.sem.num, 0) < eb.cnt:
                    ea.hw.wait_ge(eb.sem, eb.cnt)
                    ea.seen[eb.sem.num] = eb.cnt

    tk.dma("sp", cst[:], consts[:, :], writes=[B_cst])
    IDF = cst[:, 0:128]
    tk.op("dve", lambda h: h.tensor_copy(out=ident_b[:], in_=cst[:, 0:128]), [B_cst], [B_identb])
    tk.op("dve", lambda h: h.memset(ones_b[:], 1.0), [], [B_onesb])

    B_scT = Buf("scT")

    class _ModView:
        def __getitem__(self, idx):
            idx = list(idx)
            idx[1] = 0
            return modT1[tuple(idx)]
    modT = _ModView()

    def adaln_setup():
        arena_reset()
        call = arena_alloc([NS + 1, D], F32)
        B_call = Buf("call")
        scb = arena_alloc([NS + 1, D], BF16)
        B_scb = Buf("scb")
        tmp24 = arena_alloc([72, 128], F32)
        B_tmp24 = Buf("tmp24")
        tk.dma("sp", call, c_all[:, :], writes=[B_call])
        tk.op("act", lambda h: h.activation(out=scb, in_=call, func=AF.Silu), [B_call], [B_scb])
        pt, bpt = ps_get()
        ptb = pt[:].bitcast(BF16)
        tk.group("pe", [
            (lambda h, kc=kc: h.transpose(out=ptb[:, kc * 32:kc * 32 + NS + 1], in_=scb[:, kc * 128:(kc + 1) * 128],
                                          identity=ident_b[0:NS + 1, 0:NS + 1]))
            for kc in range(KC)], [B_scb, B_identb], [bpt])
        tk.op("dve", lambda h: h.tensor_copy(
            out=scT[:], in_=ptb[:, 0:KC * 32].rearrange("p (k c) -> p k c", c=32)[:, :, 0:NS + 1]), [bpt], [B_scT])
        for l in range(DEPTH):
            tk.dma("sp", tmp24[0:72, :], b_ada[l], writes=[B_tmp24])
            pt, bpt = ps_get()
            tk.group("pe", [lambda h: h.transpose(out=pt[:, 0:72], in_=tmp24[0:72, :], identity=IDF[0:72, 0:72])],
                     [B_tmp24, B_cst], [bpt])
            tk.op("dve", lambda h: h.tensor_copy(out=badaT[:, l, :], in_=pt[:, 0:72]), [bpt], [B_bada])
            tk.dma("sp", tmp24[0:24, :], norm_g[l], writes=[B_tmp24])
            pt, bpt = ps_get()
            tk.group("pe", [lambda h: h.transpose(out=pt[:, 0:24], in_=tmp24[0:24, :], identity=IDF[0:24, 0:24])],
                     [B_tmp24, B_cst], [bpt])
            tk.op("dve", lambda h: h.tensor_copy(out=gT[:, l, :], in_=pt[:, 0:24]), [bpt], [B_gT])

    def adaln(l):
        barrier()
        arena_reset()
        wada = [arena_alloc([128, KC, 512], BF16) for _ in range(3)]
        B_wada = [Buf(f"wada{i}") for i in range(3)]
        mtok = [arena_alloc([NS + 1, 512], F32) for _ in range(2)]
        B_mtok = [Buf(f"mtok{i}") for i in range(2)]
        for pc in range(18):
            slot = pc % 3
            tk.dma("pool", wada[slot],
                   w_ada[l, :, pc * 512:(pc + 1) * 512].rearrange("(k p) c -> p k c", p=128),
                   writes=[B_wada[slot]])
            pm, bpm = ps_get()
            tk.group("pe", [
                (lambda h, kc=kc: h.matmul(pm[0:NS + 1, :], lhsT=scT[:, kc, :], rhs=wada[slot][:, kc, :],
                                           start=(kc == 0), stop=(kc == KC - 1)))
                for kc in range(KC)], [B_scT, B_wada[slot]], [bpm])
            ms = pc % 2
            tk.op("act", lambda h: h.copy(out=mtok[ms], in_=pm[0:NS + 1, :]), [bpm], [B_mtok[ms]])
            pt, bpt = ps_get()
            tk.group("pe", [
                (lambda h, q=q: h.transpose(out=pt[:, q * 32:q * 32 + NS + 1], in_=mtok[ms][:, q * 128:(q + 1) * 128],
                                            identity=IDF[0:NS + 1, 0:NS + 1]))
                for q in range(4)], [B_mtok[ms], B_cst], [bpt])
            tk.op("dve", lambda h: h.tensor_tensor(
                out=modT[:, l, pc * 4:(pc + 1) * 4, :],
                in0=pt[:, 0:128].rearrange("p (q c) -> p q c", c=32)[:, :, 0:NS + 1],
                in1=badaT[:, l, pc * 4:(pc + 1) * 4].unsqueeze(2).to_broadcast([128, 4, NS + 1]),
                op=ALU.add), [bpt, B_bada], [B_modT])
        for i in (0, 2):
            tk.op("dve", lambda h: h.tensor_scalar(
                out=modT[:, l, (3 * i + 2) * 8:(3 * i + 3) * 8, :], in0=modT[:, l, (3 * i + 2) * 8:(3 * i + 3) * 8, :],
                scalar1=0.5, scalar2=None, op0=ALU.mult), [B_modT], [B_modT])

    adaln_setup()
    for l_ in range(DEPTH):
        for g_, (win_, dil_) in enumerate(GROUPS):
            tk.dma("act", [(nk_s[g_][l_, :, 0:win_ - 1, :], cache_k[g_][l_, :, 1:win_, :]),
                           (nv_s[g_][l_, :, 0:win_ - 1, :], cache_v[g_][l_, :, 1:win_, :])])
        tk.dma("act", npool_s[l_, :, 0:14, :], state_pool[l_].rearrange("(s r) c -> s r c", r=15)[:, 1:15, :])

    barrier()
    arena_reset()
    xst = [arena_alloc([128, D], F32) for _ in range(2)]
    B_xst = [Buf(f"xst{i}") for i in range(2)]
    for tt in range(17):
        s = tt % 2
        nrow = 128 if tt < 16 else NS
        src = x_p[tt * 128:(tt + 1) * 128, :] if tt < 16 else x_s[:, :]
        tk.dma("sp", xst[s][0:nrow, :], src, writes=[B_xst[s]])
        T = tt // 4 if tt < 16 else 4
        col0 = tt * 128
        for half in range(2):
            pt, bpt = ps_get()
            tk.group("pe", [
                (lambda h, q=q: h.transpose(out=pt[:, q * 128:q * 128 + nrow],
                                            in_=xst[s][0:nrow, (half * 4 + q) * 128:(half * 4 + q + 1) * 128],
                                            identity=IDF[0:nrow, 0:nrow]))
                for q in range(4)], [B_xst[s], B_cst], [bpt])
            eng = "act" if half == 0 else "dve"
            if eng == "act":
                fn = lambda h: h.copy(out=xT[:, half * 4:half * 4 + 4, col0:col0 + nrow],
                                      in_=pt[:].rearrange("p (q c) -> p q c", c=128)[:, :, 0:nrow])
            else:
                fn = lambda h: h.tensor_copy(out=xT[:, half * 4:half * 4 + 4, col0:col0 + nrow],
                                             in_=pt[:].rearrange("p (q c) -> p q c", c=128)[:, :, 0:nrow])
            tk.op(eng, fn, [bpt], [B_xT[half * 4 + q][T] for q in range(4)])

    def prologue(l, i):
        tk.op("dve", lambda h: h.scalar_tensor_tensor(
            out=Amod[:, i], in0=modT[:, l, (3 * i + 1) * 8:(3 * i + 2) * 8, :], scalar=1.0,
            in1=gT[:, l, i * 8:(i + 1) * 8].unsqueeze(2).to_broadcast([128, 8, NS + 1]),
            op0=ALU.add, op1=ALU.mult), [B_modT, B_gT], [B_Amod])
        sq = [arena_alloc([128, KC, 512], BF16) for _ in range(1)]
        B_sq = [Buf(f"sq{k}") for k in range(1)]
        rs = [arena_alloc([128, 512], F32) for _ in range(2)]
        B_rs = [Buf(f"rs{k}") for k in range(2)]
        tm = [arena_alloc([128, 512], F32) for _ in range(3)]
        B_tm = [Buf(f"tm{k}") for k in range(3)]
        tmi = 0
        for T, (c0, n) in enumerate(TT):
            s = T % 2
            s0 = 0
            tk.op("act", lambda h: h.activation(out=sq[s0][:, :, 0:n], in_=xT[:, :, c0:c0 + n], func=AF.Square),
                  [B_xT[m][T] for m in range(KC)], [B_sq[s0]])
            pq, bpq = ps_get()
            tk.group("pe", [
                (lambda h, kc=kc: h.matmul(pq[:, 0:n], lhsT=ones_b[:], rhs=sq[s0][:, kc, 0:n],
                                           start=(kc == 0), stop=(kc == KC - 1)))
                for kc in range(KC)], [B_sq[s0], B_onesb], [bpq])
            tk.op("act", lambda h: h.activation(out=rs[s][:, 0:n], in_=pq[:, 0:n], func=AF.Sqrt,
                                                scale=1.0 / D, bias=cst[:, 130:131]), [bpq, B_cst], [B_rs[s]])
            tk.op("dve", lambda h: h.reciprocal(out=rs[s][:, 0:n], in_=rs[s][:, 0:n]), [B_rs[s]], [B_rs[s]])
            if T < 4:
                for kc in range(KC):
                    k3 = tmi % 3
                    tmi += 1
                    tk.op("dve", lambda h: h.scalar_tensor_tensor(
                        out=tm[k3][:, 0:n], in0=xT[:, kc, c0:c0 + n], scalar=Amod[:, i, kc, 0:1],
                        in1=rs[s][:, 0:n], op0=ALU.mult, op1=ALU.mult),
                        [B_xT[kc][T], B_Amod, B_rs[s]], [B_tm[k3]])
                    tk.op("act", lambda h: h.activation(
                        out=hT[:, kc, c0:c0 + n], in_=tm[k3][:, 0:n], func=AF.Identity,
                        bias=modT[:, l, 3 * i * 8 + kc, 0:1], scale=1.0),
                        [B_tm[k3], B_modT], [B_hT[kc][T]])
            else:
                t3 = arena_alloc([128, KC, NS], F32)
                B_t3 = Buf("t3")
                tk.op("dve", lambda h: h.tensor_tensor(
                    out=t3, in0=xT[:, :, c0:c0 + n],
                    in1=rs[s][:, 0:n].unsqueeze(1).to_broadcast([128, KC, NS]), op=ALU.mult),
                    [B_xT[m][T] for m in range(KC)] + [B_rs[s]], [B_t3])
                tk.op("dve", lambda h: h.tensor_tensor(out=t3, in0=t3, in1=Amod[:, i, :, 1:NS + 1], op=ALU.mult),
                      [B_t3, B_Amod], [B_t3])
                tk.op("dve", lambda h: h.tensor_tensor(
                    out=hT[:, :, c0:c0 + n], in0=t3, in1=modT[:, l, 3 * i * 8:3 * i * 8 + 8, 1:NS + 1], op=ALU.add),
                    [B_t3, B_modT], [B_hT[m][T] for m in range(KC)])

    def resid_update(l, gi, pm, bpm, m, T, c0, n, tmpb, B_tmpb):
        grow = (3 * gi + 2) * 8 + m
        if T < 4:
            tk.op("dve", lambda h: h.scalar_tensor_tensor(
                out=xT[:, m, c0:c0 + n], in0=pm[:, 0:n], scalar=modT[:, l, grow, 0:1],
                in1=xT[:, m, c0:c0 + n], op0=ALU.mult, op1=ALU.add),
                [bpm, B_modT, B_xT[m][T]], [B_xT[m][T]])
        else:
            tk.op("dve", lambda h: h.tensor_tensor(out=tmpb[:, 0:n], in0=pm[:, 0:n],
                                                   in1=modT[:, l, grow, 1:NS + 1], op=ALU.mult),
                  [bpm, B_modT], [B_tmpb])
            tk.op("dve", lambda h: h.tensor_tensor(out=xT[:, m, c0:c0 + n], in0=tmpb[:, 0:n],
                                                   in1=xT[:, m, c0:c0 + n], op=ALU.add),
                  [B_tmpb, B_xT[m][T]], [B_xT[m][T]])

    def ffn(l, which):
        i = 0 if which == 0 else 2
        barrier()
        arena_reset()
        prologue(l, i)
        w1s = [arena_alloc([128, KC, 2, 384], BF16) for _ in range(2)]
        B_w1s = [Buf(f"w1s{k}") for k in range(2)]
        w2s = [arena_alloc([128, 6, D], BF16) for _ in range(2)]
        B_w2s = [Buf(f"w2s{k}") for k in range(2)]
        actp = arena_alloc([128, 6, NT], BF16)
        B_act = [[Buf(f"act{j}_{t}") for t in range(5)] for j in range(6)]
        sg = [arena_alloc([128, 512], F32) for _ in range(2)]
        B_sg = [Buf(f"sg{k}") for k in range(2)]
        tmpb = arena_alloc([128, NS], F32)
        B_tmpb = Buf("tmpb")
        w1v = ffn_w1[l, which].rearrange("(k p) (g c) -> p k g c", p=128, g=2)
        w2v = ffn_w2[l, which].rearrange("(j p) c -> p j c", p=128)
        pi = 0
        sgi = 0
        for part, (j0, nj) in enumerate(FF_PARTS):
            ws = part % 2
            tk.dma("pool", w2s[ws][:, 0:nj, :], w2v[:, j0:j0 + nj, :], writes=[B_w2s[ws]])
            jj = 0
            while jj < nj:
                npc = min(3, nj - jj)
                s1 = pi % 2
                pi += 1
                tk.dma("pool", [(w1s[s1][:, :, g2, 0:npc * 128],
                                 w1v[:, :, g2, (j0 + jj) * 128:(j0 + jj + npc) * 128]) for g2 in range(2)],
                       writes=[B_w1s[s1]])
                for q in range(npc):
                    jr = jj + q
                    for T, (c0, n) in enumerate(TT):
                        pg, bpg = ps_get()
                        pu, bpu = ps_get()
                        rds = [B_w1s[s1]] + [B_hT[kc][T] for kc in range(KC)]
                        tk.group("pe", [
                            (lambda h, kc=kc: h.matmul(pg[:, 0:n], lhsT=w1s[s1][:, kc, 0, q * 128:(q + 1) * 128],
                                                       rhs=hT[:, kc, c0:c0 + n], start=(kc == 0), stop=(kc == KC - 1)))
                            for kc in range(KC)], rds, [bpg])
                        tk.group("pe", [
                            (lambda h, kc=kc: h.matmul(pu[:, 0:n], lhsT=w1s[s1][:, kc, 1, q * 128:(q + 1) * 128],
                                                       rhs=hT[:, kc, c0:c0 + n], start=(kc == 0), stop=(kc == KC - 1)))
                            for kc in range(KC)], rds, [bpu])
                        k2 = sgi % 2
                        sgi += 1
                        tk.op("act", lambda h: h.activation(out=sg[k2][:, 0:n], in_=pg[:, 0:n], func=AF.Silu),
                              [bpg], [B_sg[k2]])
                        tk.op("dve", lambda h: h.tensor_tensor(out=actp[:, jr, c0:c0 + n], in0=pu[:, 0:n],
                                                               in1=sg[k2][:, 0:n], op=ALU.mult),
                              [bpu, B_sg[k2]], [B_act[jr][T]])
                jj += npc
            for m in range(KC):
                for T, (c0, n) in enumerate(TT):
                    py, bpy = ps_get()
                    tk.group("pe", [
                        (lambda h, jr=jr: h.matmul(py[:, 0:n], lhsT=w2s[ws][:, jr, m * 128:(m + 1) * 128],
                                                   rhs=actp[:, jr, c0:c0 + n], start=(jr == 0), stop=(jr == nj - 1)))
                        for jr in range(nj)], [B_w2s[ws]] + [B_act[jr][T] for jr in range(nj)], [bpy])
                    resid_update(l, i, py, bpy, m, T, c0, n, tmpb, B_tmpb)

    def mixer(l):
        barrier()
        arena_reset()
        prologue(l, 1)
        barrier()
        arena_reset()
        i_sub = 1
        mc = arena_alloc([128, MCW], F32)
        B_mc = Buf("mc")
        tk.dma("sp", mc, mconsts[:, :], writes=[B_mc])
        DISTG = [mc[:, g_ * 256:(g_ + 1) * 256] for g_ in range(3)]
        INVC = mc[:, 768:800].rearrange("p (c t) -> p c t", t=16)
        SBIAS = mc[:, 800:812]
        EYE16 = mc[0:NS, 816:832]
        zib = arena_alloc([128, 32], BF16)
        B_zib = Buf("zib")
        tk.op("dve", lambda h: h.tensor_copy(out=zib, in_=mc[:, 832:864]), [B_mc], [B_zib])
        pool_yT = arena_alloc([128, 2, NT], BF16)
        B_py = [Buf(f"py{t}") for t in range(5)]
        mark1 = arena_off[0]

        uT = arena_alloc([128, 2, NT], F32)
        B_uT = [Buf("uT0"), Buf("uT1")]
        wu = arena_alloc([128, KC, 256], BF16)
        B_wu = Buf("wu")
        tk.dma("pool", wu, w_in[l, :, 2304:2560].rearrange("(k p) c -> p k c", p=128), writes=[B_wu])
        for c in range(2):
            for T, (c0, n) in enumerate(TT):
                pu, bpu = ps_get()
                tk.group("pe", [
                    (lambda h, kc=kc: h.matmul(pu[:, 0:n], lhsT=wu[:, kc, c * 128:(c + 1) * 128],
                                               rhs=hT[:, kc, c0:c0 + n], start=(kc == 0), stop=(kc == KC - 1)))
                    for kc in range(KC)], [B_wu] + [B_hT[kc][T] for kc in range(KC)], [bpu])
                tk.op("act", lambda h: h.copy(out=uT[:, c, c0:c0 + n], in_=pu[:, 0:n]), [bpu], [B_uT[c]])
        npst = arena_alloc([NS, 2, 256], F32)
        B_npst = Buf("npst")
        pt, bpt = ps_get()
        tk.group("pe", [
            (lambda h, c=c: h.transpose(out=pt[0:15, c * 128:(c + 1) * 128], in_=uT[:, c, SEQ - 15:SEQ], identity=IDF))
            for c in range(2)] + [
            (lambda h, c=c: h.transpose(out=pt[0:NS, 256 + c * 128:256 + (c + 1) * 128], in_=uT[:, c, SEQ:NT], identity=IDF))
            for c in range(2)], B_uT + [B_cst], [bpt])
        tk.op("act", lambda h: h.copy(out=npst[:, :, :], in_=pt[0:NS, :].rearrange("p (a c) -> p a c", c=256)), [bpt], [B_npst])
        tk.dma("sp", [(npool_p[l], npst[0:15, 0, :]), (npool_s[l, :, 14, :], npst[0:NS, 1, :])], reads=[B_npst])
        stg = arena_alloc([120, 2, 256], F32)
        B_stg = Buf("stg")
        tk.dma("sp", stg, state_pool[l].rearrange("(h p) c -> p h c", p=120), writes=[B_stg])
        stT = arena_alloc([128, 2, 240], F32)
        B_stT = Buf("stT")
        pt, bpt = ps_get()
        tk.group("pe", [
            (lambda h, c=c, hh=hh: h.transpose(out=pt[:, (c * 2 + hh) * 120:(c * 2 + hh + 1) * 120],
                                               in_=stg[:, hh, c * 128:(c + 1) * 128], identity=IDF[0:120, 0:120]))
            for c in range(2) for hh in range(2)], [B_stg, B_cst], [bpt])
        tk.op("act", lambda h: h.copy(out=stT, in_=pt[:, 0:480].rearrange("p (c x) -> p c x", x=240)), [bpt], [B_stT])
        ssum = arena_alloc([128, 2, NS], F32)
        B_ssum = Buf("ssum")
        dT = arena_alloc([128, 2, NT], BF16)
        B_dT = Buf("dT")
        for c in range(2):
            for hf in range(2):
                w = POOL_WINDOWS[2 * c + hf]
                ps_ = slice(hf * 64, (hf + 1) * 64)
                tk.op("dve", lambda h: h.tensor_reduce(
                    out=ssum[ps_, c, :], in_=stT[ps_, c, :].rearrange("p (s r) -> p s r", r=15)[:, :, 15 - (w - 1):15],
                    axis=AX.X, op=ALU.add), [B_stT], [B_ssum])
        tk.op("dve", lambda h: h.tensor_tensor(out=ssum, in0=ssum, in1=uT[:, :, SEQ:NT], op=ALU.add),
              [B_ssum] + B_uT, [B_ssum])
        for c in range(2):
            for hf in range(2):
                w = POOL_WINDOWS[2 * c + hf]
                ps_ = slice(hf * 64, (hf + 1) * 64)
                tk.op("dve", lambda h: h.scalar_tensor_tensor(
                    out=dT[ps_, c, SEQ:NT], in0=ssum[ps_, c, :], scalar=1.0 / w, in1=uT[ps_, c, SEQ:NT],
                    op0=ALU.mult, op1=ALU.subtract), [B_ssum] + B_uT, [B_dT])
        PA = arena_alloc([128, 2, SEQ], F32)
        PB = arena_alloc([128, 2, SEQ], F32)
        B_PA = Buf("PA")
        B_PB = Buf("PB")
        U = uT[:, :, 0:SEQ]
        tk.op("dve", lambda h: h.tensor_tensor(out=PA[:, :, 1:SEQ], in0=uT[:, :, 1:SEQ], in1=uT[:, :, 0:SEQ - 1], op=ALU.add),
              B_uT, [B_PA])
        tk.op("dve", lambda h: h.tensor_copy(out=PA[:, :, 0:1], in_=uT[:, :, 0:1]), B_uT, [B_PA])
        tk.op("dve", lambda h: h.tensor_tensor(out=PB[:, :, 2:SEQ], in0=PA[:, :, 2:SEQ], in1=PA[:, :, 0:SEQ - 2], op=ALU.add),
              [B_PA], [B_PB])
        tk.op("dve", lambda h: h.tensor_copy(out=PB[:, :, 0:2], in_=PA[:, :, 0:2]), [B_PA], [B_PB])
        tk.op("dve", lambda h: h.tensor_tensor(out=PA[:, 1, 4:SEQ], in0=PB[:, 1, 4:SEQ], in1=PB[:, 1, 0:SEQ - 4], op=ALU.add),
              [B_PB], [B_PA])
        tk.op("dve", lambda h: h.tensor_copy(out=PA[:, 1, 0:4], in_=PB[:, 1, 0:4]), [B_PB], [B_PA])
        tk.op("dve", lambda h: h.tensor_tensor(out=PB[64:128, 1, 8:SEQ], in0=PA[64:128, 1, 8:SEQ],
                                               in1=PA[64:128, 1, 0:SEQ - 8], op=ALU.add), [B_PA], [B_PB])
        tk.op("dve", lambda h: h.tensor_copy(out=PB[64:128, 1, 0:8], in_=PA[64:128, 1, 0:8]), [B_PA], [B_PB])
        t16 = arena_alloc([128, 16], F32)
        B_t16 = Buf("t16")
        for c in range(2):
            for hf in range(2):
                w = POOL_WINDOWS[2 * c + hf]
                ps_ = slice(hf * 64, (hf + 1) * 64)
                S = PA if hf == 0 else PB
                tk.op("dve", lambda h: h.scalar_tensor_tensor(
                    out=dT[ps_, c, 16:SEQ], in0=S[ps_, c, 16:SEQ], scalar=1.0 / w, in1=uT[ps_, c, 16:SEQ],
                    op0=ALU.mult, op1=ALU.subtract), [B_PA, B_PB] + B_uT, [B_dT])
                tk.op("dve", lambda h: h.tensor_tensor(out=t16[ps_, :], in0=S[ps_, c, 0:16], in1=INVC[ps_, c, :], op=ALU.mult),
                      [B_PA, B_PB, B_mc], [B_t16])
                tk.op("dve", lambda h: h.tensor_tensor(out=dT[ps_, c, 0:16], in0=t16[ps_, :], in1=uT[ps_, c, 0:16],
                                                       op=ALU.subtract), [B_t16] + B_uT, [B_dT])
        pwf = arena_alloc([128, 2, 128], F32)
        B_pwf = Buf("pwf")
        pwb = arena_alloc([128, 2, 128], BF16)
        B_pwb = Buf("pwb")
        tk.op("dve", lambda h: h.memset(pwf, 0.0), [], [B_pwf])
        tk.dma("sp", [(pwf[(gi % 2) * 64:(gi % 2 + 1) * 64, gi // 2, (gi % 2) * 64:(gi % 2 + 1) * 64], pool_w[l, gi])
                      for gi in range(4)], writes=[B_pwf])
        tk.op("dve", lambda h: h.tensor_copy(out=pwb, in_=pwf), [B_pwf], [B_pwb])
        psc_in = arena_alloc([2, 128], F32)
        B_pscin = Buf("pscin")
        pscT = arena_alloc([128, 2], F32)
        B_pscT = Buf("pscT")
        tk.dma("sp", psc_in, pool_scale[l], writes=[B_pscin])
        pt, bpt = ps_get()
        tk.group("pe", [lambda h: h.transpose(out=pt[:, 0:2], in_=psc_in, identity=IDF[0:2, 0:2])], [B_pscin, B_cst], [bpt])
        tk.op("dve", lambda h: h.tensor_copy(out=pscT, in_=pt[:, 0:2]), [bpt], [B_pscT])
        for c in range(2):
            for T, (c0, n) in enumerate(TT):
                pp, bpp = ps_get()
                tk.group("pe", [lambda h: h.matmul(pp[:, 0:n], lhsT=pwb[:, c, :], rhs=dT[:, c, c0:c0 + n], start=True, stop=True)],
                         [B_pwb, B_dT], [bpp])
                tk.op("act", lambda h: h.activation(out=pool_yT[:, c, c0:c0 + n], in_=pp[:, 0:n], func=AF.Copy,
                                                    scale=pscT[:, c:c + 1]), [bpp, B_pscT], [B_py[T]])

        if stop_after == ("mixP", l):
            raise StopBuild()
        barrier()
        arena_off[0] = mark1
        OT = arena_alloc([128, 6, NT], BF16)
        B_OT = [Buf(f"OT{k}") for k in range(6)]
        B_OTs = Buf("OTs")
        mark2 = arena_off[0]
        Dsum = arena_alloc([128, SEQ], F32)
        B_Ds = Buf("Dsum")
        wq = [arena_alloc([128, KC, 3, 128], BF16) for _ in range(1)]
        B_wq = [Buf("wq0")]
        qkT = arena_alloc([128, 2, SEQ], BF16)
        B_qkT = [Buf(f"qkT{s_}") for s_ in range(16)]
        Vg = arena_alloc([128, 16, 128], BF16)
        B_V = [Buf(f"V{s_}") for s_ in range(16)]
        gtmp = arena_alloc([128, 2, 64], F32)
        B_gtmp = Buf("gtmp")
        gains = arena_alloc([128, 2, 2, 64], F32)
        B_gains = Buf("gains")
        tk.dma("sp", gtmp, qk_g[l].partition_broadcast(128), writes=[B_gtmp])
        tk.op("dve", lambda h: h.tensor_scalar(out=gains[:, 0], in0=gtmp[:, 0:1, :].to_broadcast([128, 2, 64]),
                                               scalar1=0.125, scalar2=None, op0=ALU.mult), [B_gtmp], [B_gains])
        tk.op("dve", lambda h: h.tensor_copy(out=gains[:, 1], in_=gtmp[:, 1:2, :].to_broadcast([128, 2, 64])),
              [B_gtmp], [B_gains])
        nhalf = arena_alloc([128, 4], F32)
        B_nhalf = Buf("nhalf")
        tk.op("dve", lambda h: h.memset(nhalf, -0.5), [], [B_nhalf])
        NSL = 4
        sqf = [arena_alloc([128, 256], F32) for _ in range(NSL)]
        B_sqf = [Buf(f"sqf{k}") for k in range(NSL)]
        ssb = [arena_alloc([128, 4], F32) for _ in range(NSL)]
        B_ssb = [Buf(f"ssb{k}") for k in range(NSL)]
        qkn = [arena_alloc([128, 256], F32) for _ in range(NSL)]
        B_qkn = [Buf(f"qkn{k}") for k in range(NSL)]
        qkb = [arena_alloc([128, 256], BF16) for _ in range(NSL)]
        B_qkb = [Buf(f"qkb{k}") for k in range(NSL)]
        vf = [arena_alloc([128, 128], F32) for _ in range(NSL)]
        B_vf = [Buf(f"vf{k}") for k in range(NSL)]
        sbt = [arena_alloc([128, 256], F32) for _ in range(2)]
        B_sbt = [Buf("sbt0"), Buf("sbt1")]
        Pt = [arena_alloc([128, 2, 256], BF16) for _ in range(3)]
        B_Pt = [Buf(f"Pt{k}") for k in range(3)]
        qkn_s = arena_alloc([NS, 256], F32)
        B_qkns = Buf("qkn_s")
        vf_s = arena_alloc([NS, 128], F32)
        B_vfs = Buf("vf_s")
        Kc = arena_alloc([128, NS, 128], BF16)
        B_Kc = Buf("Kc")
        Vc = arena_alloc([128, NS, 128], BF16)
        B_Vc = Buf("Vc")
        qdiag = arena_alloc([NS, NS, 128], BF16)
        B_qdiag = Buf("qdiag")
        prod = arena_alloc([128, 512], F32)
        B_prod = Buf("prod")
        scs = arena_alloc([128, NS, 2], F32)
        B_scs = Buf("scs")
        Ps = arena_alloc([128, NS, 2], BF16)
        B_Ps = Buf("Ps")
        Wt = Vc
        B_Wt = B_Vc
        sfs = arena_alloc([NS, 128], F32)
        B_sfs = Buf("sfs")
        pself = arena_alloc([NS, 2], F32)
        B_pself = Buf("pself")
        numS = arena_alloc([NS, 3, 128], F32)
        B_numS = Buf("numS")
        denS = arena_alloc([NS, 2], F32)
        B_denS = Buf("denS")
        oSb = arena_alloc([NS, 3, 128], BF16)
        B_oSb = Buf("oSb")
        tsl = 0
        sbi = 0
        for pair in range(2):
            for g, (win, dil) in enumerate(GROUPS):
                nblk = SEQ // dil // 128
                pg_i = pair * 3 + g
                ws = 0
                colbase = g * 256 + pair * 128
                tk.dma("pool", [(wq[ws][:, :, wh, :],
                                 w_in[l, :, wh * 768 + colbase:wh * 768 + colbase + 128].rearrange("(k p) c -> p k c", p=128))
                                for wh in range(3)], writes=[B_wq[ws]])
                tk.dma("pool", Kc, cache_k[g][l].rearrange("s (j d) c -> j s d c", d=dil)[:, :, 0, pair * 128:(pair + 1) * 128],
                       writes=[B_Kc])
                tk.dma("pool", Vc, cache_v[g][l].rearrange("s (j d) c -> j s d c", d=dil)[:, :, 0, pair * 128:(pair + 1) * 128],
                       writes=[B_Vc])

                def tile_cols(s_):
                    if g == 0:
                        r, blk = 0, s_
                    elif g == 1:
                        r, blk = s_ // 4, s_ % 4
                    else:
                        r, blk = s_, 0
                    start = r + dil * 128 * blk
                    return r, blk, start

                for s_ in range(17):
                    k2 = tsl % NSL
                    tsl += 1
                    if s_ < 16:
                        r, blk, start = tile_cols(s_)
                        nrow = 128
                        cols = slice(start, start + dil * 127 + 1, dil)
                        hrd = [B_hT[kc][T] for kc in range(KC) for T in range(4)]
                    else:
                        nrow = NS
                        cols = slice(SEQ, NT)
                        hrd = [B_hT[kc][4] for kc in range(KC)]
                    pq, bpq = ps_get()
                    tk.group("pe", [
                        (lambda h, kc=kc: h.matmul(pq[0:nrow, 0:256], lhsT=hT[:, kc, cols], rhs=wq[ws][:, kc, 0:2, :],
                                                   start=(kc == 0), stop=(kc == KC - 1)))
                        for kc in range(KC)] + [
                        (lambda h, kc=kc: h.matmul(pq[0:nrow, 256:384], lhsT=hT[:, kc, cols], rhs=wq[ws][:, kc, 2, :],
                                                   start=(kc == 0), stop=(kc == KC - 1)))
                        for kc in range(KC)], [B_wq[ws]] + hrd, [bpq])
                    tk.op("act", lambda h: h.activation(out=sqf[k2][0:nrow, :], in_=pq[0:nrow, 0:256], func=AF.Square),
                          [bpq], [B_sqf[k2]])
                    tk.op("dve", lambda h: h.tensor_reduce(out=ssb[k2][0:nrow, :],
                                                           in_=sqf[k2][0:nrow, :].rearrange("p (a d) -> p a d", d=64),
                                                           axis=AX.X, op=ALU.add), [B_sqf[k2]], [B_ssb[k2]])
                    tk.op("act", lambda h: h.activation(out=ssb[k2][0:nrow, :], in_=ssb[k2][0:nrow, :], func=AF.Sqrt,
                                                        scale=1.0 / 64, bias=cst[0:nrow, 130:131]), [B_ssb[k2], B_cst], [B_ssb[k2]])
                    tk.op("dve", lambda h: h.reciprocal(out=ssb[k2][0:nrow, :], in_=ssb[k2][0:nrow, :]), [B_ssb[k2]], [B_ssb[k2]])
                    qdst = qkn[k2] if s_ < 16 else qkn_s
                    B_qdst = B_qkn[k2] if s_ < 16 else B_qkns
                    tk.op("dve", lambda h: h.tensor_tensor(
                        out=qdst[0:nrow, :].rearrange("p (a d) -> p a d", d=64),
                        in0=pq[0:nrow, 0:256].rearrange("p (a d) -> p a d", d=64),
                        in1=ssb[k2][0:nrow, :].unsqueeze(2).to_broadcast([nrow, 4, 64]), op=ALU.mult),
                        [bpq, B_ssb[k2]], [B_qdst])
                    tk.op("dve", lambda h: h.tensor_tensor(out=qdst[0:nrow, :], in0=qdst[0:nrow, :],
                                                           in1=gains[0:nrow].rearrange("p a b d -> p (a b d)"), op=ALU.mult),
                          [B_qdst, B_gains], [B_qdst])
                    if s_ < 16:
                        tk.op("act", lambda h: h.copy(out=qkb[k2], in_=qkn[k2]), [B_qkn[k2]], [B_qkb[k2]])
                        ptq, bptq = ps_get()
                        ptb = ptq[:].bitcast(BF16)
                        tk.group("pe", [
                            (lambda h, a=a: h.transpose(out=ptb[:, a * 128:(a + 1) * 128], in_=qkb[k2][:, a * 128:(a + 1) * 128],
                                                        identity=ident_b[:]))
                            for a in range(2)], [B_qkb[k2], B_identb], [bptq])
                        tk.op("act", lambda h: h.copy(out=qkT[:, :, s_ * 128:(s_ + 1) * 128],
                                                      in_=ptb[:, 0:256].rearrange("p (a c) -> p a c", c=128)),
                              [bptq], [B_qkT[s_]])
                        tk.op("act", lambda h: h.copy(out=Vg[:, s_, :], in_=pq[:, 256:384]), [bpq], [B_V[s_]])
                        is_out = (g == 0 and s_ == 15) or (g == 1 and s_ % 4 == 3) or g == 2
                        if is_out:
                            tk.op("dve", lambda h: h.tensor_copy(out=vf[k2], in_=pq[:, 256:384]), [bpq], [B_vf[k2]])
                            kdst = nk_p[g][l].rearrange("(i d) c -> d i c", d=dil)[r, :, pair * 128:(pair + 1) * 128]
                            vdst = nv_p[g][l].rearrange("(i d) c -> d i c", d=dil)[r, :, pair * 128:(pair + 1) * 128]
                            tk.dma("sp", kdst, qkn[k2][:, 128:256], reads=[B_qkn[k2]])
                            tk.dma("sp", vdst, vf[k2][:, :], reads=[B_vf[k2]])
                    else:
                        tk.op("dve", lambda h: h.tensor_copy(out=vf_s, in_=pq[0:NS, 256:384]), [bpq], [B_vfs])
                        tk.dma("sp", nk_s[g][l, :, win - 1, pair * 128:(pair + 1) * 128], qkn_s[:, 128:256], reads=[B_qkns])
                        tk.dma("sp", nv_s[g][l, :, win - 1, pair * 128:(pair + 1) * 128], vf_s[:, :], reads=[B_vfs])

                def s_stage(s_):
                    nonlocal sbi
                    r, blk, start = tile_cols(s_)
                    nq = 2 if blk < nblk - 1 else 1
                    pk = s_ % 3
                    for hh in range(2):
                        hb = slice(hh * 64, (hh + 1) * 64)
                        hidx = g * 4 + pair * 2 + hh
                        pS, bpS = ps_get()
                        tk.group("pe", [lambda h: h.matmul(pS[:, 0:nq * 128], lhsT=qkT[hb, 1, s_ * 128:(s_ + 1) * 128],
                                                           rhs=qkT[hb, 0, s_ * 128:(s_ + nq) * 128], start=True, stop=True)],
                                 [B_qkT[s_]] + ([B_qkT[s_ + 1]] if nq == 2 else []), [bpS])
                        kb = sbi % 2
                        sbi += 1
                        tk.op("dve", lambda h: h.scalar_tensor_tensor(
                            out=sbt[kb][:, 0:nq * 128], in0=DISTG[g][:, 0:nq * 128], scalar=-SLOPES[hidx] * dil,
                            in1=pS[:, 0:nq * 128], op0=ALU.mult, op1=ALU.add), [B_mc, bpS], [B_sbt[kb]])
                        tk.op("act", lambda h: h.activation(out=Pt[pk][:, hh, 0:nq * 128], in_=sbt[kb][:, 0:nq * 128], func=AF.Exp),
                              [B_sbt[kb]], [B_Pt[pk]])

                def pv_stage(s_):
                    r, blk, start = tile_cols(s_)
                    pk = s_ % 3
                    prevP = (s_ - 1) % 3
                    pO, bpO = ps_get()
                    mms = []
                    for hh in range(2):
                        hb = slice(hh * 64, (hh + 1) * 64)
                        for kind in range(2):
                            oc = slice(kind * 128, (kind + 1) * 128)
                            if blk > 0:
                                lhs_prev = Vg[:, s_ - 1, hh * 64:(hh + 1) * 64] if kind == 0 else ones_b[:, 0:64]
                                mms.append(lambda h, lhs_prev=lhs_prev, hb=hb, oc=oc, hh=hh: h.matmul(
                                    pO[hb, oc], lhsT=lhs_prev, rhs=Pt[prevP][:, hh, 128:256], start=True, stop=False))
                            lhs_cur = Vg[:, s_, hh * 64:(hh + 1) * 64] if kind == 0 else ones_b[:, 0:64]
                            mms.append(lambda h, lhs_cur=lhs_cur, hb=hb, oc=oc, hh=hh: h.matmul(
                                pO[hb, oc], lhsT=lhs_cur, rhs=Pt[pk][:, hh, 0:128], start=(blk == 0), stop=True))
                    rds = [B_V[s_], B_Pt[pk], B_onesb]
                    if blk > 0:
                        rds += [B_V[s_ - 1], B_Pt[prevP]]
                    tk.group("pe", mms, rds, [bpO])
                    ocols = slice(start, start + dil * 127 + 1, dil)
                    tk.op("act", lambda h: h.copy(out=OT[:, g * 2 + pair, ocols], in_=pO[:, 0:128]), [bpO], [B_OT[g * 2 + pair]])
                    if g == 0:
                        tk.op("dve", lambda h: h.tensor_copy(out=Dsum[:, ocols], in_=pO[:, 128:256]), [bpO], [B_Ds])
                    else:
                        tk.op("dve", lambda h: h.tensor_tensor(out=Dsum[:, ocols], in0=pO[:, 128:256], in1=Dsum[:, ocols], op=ALU.add),
                              [bpO, B_Ds], [B_Ds])

                for s_ in range(17):
                    if s_ < 16:
                        s_stage(s_)
                    if s_ >= 1:
                        pv_stage(s_ - 1)

                if stop_after == ("mixA2", l):
                    raise StopBuild()
                tk.op("dve", lambda h: h.tensor_tensor(
                    out=qdiag, in0=qkn_s[:, 0:128].unsqueeze(1).to_broadcast([NS, NS, 128]),
                    in1=EYE16.unsqueeze(2).to_broadcast([NS, NS, 128]), op=ALU.mult), [B_qkns, B_mc], [B_qdiag])
                for b4 in range(4):
                    pb, bpb = ps_get()
                    tk.group("pe", [lambda h: h.matmul(pb[:, :], lhsT=ones_b[0:NS, :], rhs=qdiag[:, b4 * 4:(b4 + 1) * 4, :],
                                                       start=True, stop=True)], [B_qdiag, B_onesb], [bpb])
                    tk.op("dve", lambda h: h.tensor_tensor(out=prod, in0=Kc[:, b4 * 4:(b4 + 1) * 4, :].rearrange("p s c -> p (s c)"),
                                                           in1=pb[:, :], op=ALU.mult), [B_Kc, bpb], [B_prod])
                    tk.op("dve", lambda h: h.tensor_reduce(out=scs[:, b4 * 4:(b4 + 1) * 4, :].rearrange("p s a -> p (s a)"),
                                                           in_=prod.rearrange("p (a d) -> p a d", d=64), axis=AX.X, op=ALU.add),
                          [B_prod], [B_scs])
                h0 = g * 4 + pair * 2
                tk.op("dve", lambda h: h.tensor_tensor(out=scs, in0=scs,
                                                       in1=SBIAS[:, h0:h0 + 2].unsqueeze(1).to_broadcast([128, NS, 2]), op=ALU.add),
                      [B_scs, B_mc], [B_scs])
                tk.op("act", lambda h: h.activation(out=Ps, in_=scs, func=AF.Exp), [B_scs], [B_Ps])
                tk.op("dve", lambda h: h.tensor_tensor(
                    out=Wt.rearrange("p s (a d) -> p (s a) d", d=64), in0=Vc.rearrange("p s (a d) -> p (s a) d", d=64),
                    in1=Ps.rearrange("p s a -> p (s a)").unsqueeze(2).to_broadcast([128, NS * 2, 64]), op=ALU.mult),
                    [B_Vc, B_Ps], [B_Vc])
                pN, bpN = ps_get()
                tk.group("pe", [
                    (lambda h, s2=s2: h.matmul(pN[0:NS, 0:128], lhsT=zib[:, 15 - s2:31 - s2], rhs=Wt[:, s2, :],
                                               start=(s2 == 0), stop=(s2 == NS - 1))) for s2 in range(NS)] + [
                    (lambda h, s2=s2: h.matmul(pN[0:NS, 128:130], lhsT=zib[:, 15 - s2:31 - s2], rhs=Ps[:, s2, :],
                                               start=(s2 == 0), stop=(s2 == NS - 1))) for s2 in range(NS)],
                    [B_zib, B_Wt, B_Ps], [bpN])
                tk.op("dve", lambda h: h.tensor_tensor(out=sfs, in0=qkn_s[:, 0:128], in1=qkn_s[:, 128:256], op=ALU.mult),
                      [B_qkns], [B_sfs])
                tk.op("dve", lambda h: h.tensor_reduce(out=pself, in_=sfs.rearrange("p (a d) -> p a d", d=64), axis=AX.X, op=ALU.add),
                      [B_sfs], [B_pself])
                tk.op("act", lambda h: h.activation(out=pself, in_=pself, func=AF.Exp), [B_pself], [B_pself])
                tk.op("dve", lambda h: h.tensor_tensor(
                    out=sfs.rearrange("p (a d) -> p a d", d=64), in0=vf_s.rearrange("p (a d) -> p a d", d=64),
                    in1=pself.unsqueeze(2).to_broadcast([NS, 2, 64]), op=ALU.mult), [B_vfs, B_pself], [B_sfs])
                tk.op("dve", lambda h: h.tensor_tensor(out=numS[:, g, :], in0=pN[0:NS, 0:128], in1=sfs, op=ALU.add),
                      [bpN, B_sfs], [B_numS])
                if g == 0:
                    tk.op("dve", lambda h: h.tensor_tensor(out=denS, in0=pN[0:NS, 128:130], in1=pself, op=ALU.add),
                          [bpN, B_pself], [B_denS])
                else:
                    tk.op("dve", lambda h: h.tensor_tensor(out=pself, in0=pN[0:NS, 128:130], in1=pself, op=ALU.add),
                          [bpN, B_pself], [B_pself])
                    tk.op("dve", lambda h: h.tensor_tensor(out=denS, in0=denS, in1=pself, op=ALU.add),
                          [B_denS, B_pself], [B_denS])
                if stop_after == ("mixA3", l):
                    raise StopBuild()
            tk.op("dve", lambda h: h.reciprocal(out=Dsum, in_=Dsum), [B_Ds], [B_Ds])
            for g in range(3):
                tk.op("dve", lambda h: h.tensor_tensor(out=OT[:, g * 2 + pair, 0:SEQ], in0=OT[:, g * 2 + pair, 0:SEQ], in1=Dsum,
                                                       op=ALU.mult), [B_OT[g * 2 + pair], B_Ds], [B_OT[g * 2 + pair]])
            tk.op("dve", lambda h: h.reciprocal(out=denS, in_=denS), [B_denS], [B_denS])
            tk.op("dve", lambda h: h.tensor_tensor(
                out=oSb.rearrange("p g (a d) -> p g a d", d=64), in0=numS.rearrange("p g (a d) -> p g a d", d=64),
                in1=denS.unsqueeze(1).unsqueeze(3).to_broadcast([NS, 3, 2, 64]), op=ALU.mult), [B_numS, B_denS], [B_oSb])
            ptq, bptq = ps_get()
            ptb = ptq[:].bitcast(BF16)
            tk.group("pe", [
                (lambda h, g=g: h.transpose(out=ptb[:, g * 32:g * 32 + NS], in_=oSb[:, g, :], identity=ident_b[0:NS, 0:NS]))
                for g in range(3)], [B_oSb, B_identb], [bptq])
            for g in range(3):
                tk.op("act", lambda h: h.copy(out=OT[:, g * 2 + pair, SEQ:NT], in_=ptb[:, g * 32:g * 32 + NS]), [bptq], [B_OTs])

        if stop_after == ("mixA", l):
            raise StopBuild()
        barrier()
        arena_off[0] = mark2
        mT = arena_alloc([128, KC, NT], BF16)
        B_mT = [[Buf(f"mT{c}_{t}") for t in range(5)] for c in range(KC)]
        wc = [arena_alloc([128, 24, 256], BF16) for _ in range(2)]
        B_wc = [Buf("wc0"), Buf("wc1")]
        sga = [arena_alloc([128, 512], F32) for _ in range(2)]
        B_sga = [Buf("sga0"), Buf("sga1")]
        sgp = [arena_alloc([128, 512], F32) for _ in range(2)]
        B_sgp = [Buf("sgp0"), Buf("sgp1")]
        tmpb = arena_alloc([128, NS], F32)
        B_tmpb = Buf("tmpb2")
        ti = 0
        for qc in range(4):
            ws = qc % 2
            cs = slice(qc * 256, (qc + 1) * 256)
            tk.dma("pool", [
                (wc[ws][:, 0:8, :], w_in[l, :, 2560 + qc * 256:2560 + (qc + 1) * 256].rearrange("(k p) c -> p k c", p=128)),
                (wc[ws][:, 8:16, :], w_in[l, :, 3584 + qc * 256:3584 + (qc + 1) * 256].rearrange("(k p) c -> p k c", p=128)),
                (wc[ws][:, 16:22, :], w_oa[l, :, cs].rearrange("(k p) c -> p k c", p=128)),
                (wc[ws][:, 22:24, :], w_op[l, :, cs].rearrange("(k p) c -> p k c", p=128))], writes=[B_wc[ws]])
            for cc in range(2):
                c = qc * 2 + cc
                wsl = slice(cc * 128, (cc + 1) * 128)
                for T, (c0, n) in enumerate(TT):
                    hrd = [B_hT[kc][T] for kc in range(KC)]
                    pga, bpga = ps_get()
                    tk.group("pe", [
                        (lambda h, kc=kc: h.matmul(pga[:, 0:n], lhsT=wc[ws][:, kc, wsl], rhs=hT[:, kc, c0:c0 + n],
                                                   start=(kc == 0), stop=(kc == KC - 1))) for kc in range(KC)],
                        [B_wc[ws]] + hrd, [bpga])
                    pgp, bpgp = ps_get()
                    tk.group("pe", [
                        (lambda h, kc=kc: h.matmul(pgp[:, 0:n], lhsT=wc[ws][:, 8 + kc, wsl], rhs=hT[:, kc, c0:c0 + n],
                                                   start=(kc == 0), stop=(kc == KC - 1))) for kc in range(KC)],
                        [B_wc[ws]] + hrd, [bpgp])
                    pa, bpa = ps_get()
                    tk.group("pe", [
                        (lambda h, k6=k6: h.matmul(pa[:, 0:n], lhsT=wc[ws][:, 16 + k6, wsl], rhs=OT[:, k6, c0:c0 + n],
                                                   start=(k6 == 0), stop=(k6 == 5))) for k6 in range(6)],
                        [B_wc[ws]] + (B_OT if T < 4 else [B_OTs]), [bpa])
                    pp, bpp = ps_get()
                    tk.group("pe", [
                        (lambda h, k2=k2: h.matmul(pp[:, 0:n], lhsT=wc[ws][:, 22 + k2, wsl], rhs=pool_yT[:, k2, c0:c0 + n],
                                                   start=(k2 == 0), stop=(k2 == 1))) for k2 in range(2)],
                        [B_wc[ws], B_py[T]], [bpp])
                    k2 = ti % 2
                    ti += 1
                    tk.op("act", lambda h: h.activation(out=sga[k2][:, 0:n], in_=pga[:, 0:n], func=AF.Sigmoid), [bpga], [B_sga[k2]])
                    tk.op("act", lambda h: h.activation(out=sgp[k2][:, 0:n], in_=pgp[:, 0:n], func=AF.Sigmoid), [bpgp], [B_sgp[k2]])
                    tk.op("dve", lambda h: h.tensor_tensor(out=sga[k2][:, 0:n], in0=pa[:, 0:n], in1=sga[k2][:, 0:n], op=ALU.mult),
                          [bpa, B_sga[k2]], [B_sga[k2]])
                    tk.op("dve", lambda h: h.tensor_tensor(out=sgp[k2][:, 0:n], in0=pp[:, 0:n], in1=sgp[k2][:, 0:n], op=ALU.mult),
                          [bpp, B_sgp[k2]], [B_sgp[k2]])
                    tk.op("dve", lambda h: h.tensor_tensor(out=mT[:, c, c0:c0 + n], in0=sga[k2][:, 0:n], in1=sgp[k2][:, 0:n], op=ALU.add),
                          [B_sga[k2], B_sgp[k2]], [B_mT[c][T]])
        for qc in range(4):
            ws = qc % 2
            tk.dma("pool", wc[ws][:, 0:8, :], w_out[l, :, qc * 256:(qc + 1) * 256].rearrange("(k p) c -> p k c", p=128),
                   writes=[B_wc[ws]])
            for cc in range(2):
                c = qc * 2 + cc
                wsl = slice(cc * 128, (cc + 1) * 128)
                for T, (c0, n) in enumerate(TT):
                    pm, bpm = ps_get()
                    tk.group("pe", [
                        (lambda h, kc=kc: h.matmul(pm[:, 0:n], lhsT=wc[ws][:, kc, wsl], rhs=mT[:, kc, c0:c0 + n],
                                                   start=(kc == 0), stop=(kc == KC - 1))) for kc in range(KC)],
                        [B_wc[ws]] + [B_mT[kc][T] for kc in range(KC)], [bpm])
                    resid_update(l, 1, pm, bpm, c, T, c0, n, tmpb, B_tmpb)

    done = False
    for l in range(DEPTH):
        adaln(l)
        ffn(l, 0)
        if stop_after == ("ffn1", l):
            done = True
            break
        try:
            mixer(l)
        except StopBuild:
            done = True
            break
        if stop_after == ("mix", l):
            done = True
            break
        ffn(l, 1)
        if stop_after == ("ffn2", l):
            done = True
            break

    barrier()
    arena_reset()
    yst = [arena_alloc([128, D], F32) for _ in range(2)]
    B_yst = [Buf(f"yst{i}") for i in range(2)]
    for tt in range(17):
        s = tt % 2
        nrow = 128 if tt < 16 else NS
        T = tt // 4 if tt < 16 else 4
        col0 = tt * 128
        for half in range(2):
            pt, bpt = ps_get()
            tk.group("pe", [
                (lambda h, q=q: h.transpose(out=pt[0:nrow, q * 128:(q + 1) * 128],
                                            in_=xT[:, half * 4 + q, col0:col0 + nrow], identity=IDF))
                for q in range(4)], [B_xT[half * 4 + q][T] for q in range(4)] + [B_cst], [bpt])
            if half == 0:
                tk.op("act", lambda h: h.copy(out=yst[s][0:nrow, 0:512], in_=pt[0:nrow, :]), [bpt], [B_yst[s]])
            else:
                tk.op("dve", lambda h: h.tensor_copy(out=yst[s][0:nrow, 512:1024], in_=pt[0:nrow, :]), [bpt], [B_yst[s]])
        dst = y_p[tt * 128:(tt + 1) * 128, :] if tt < 16 else y_s[:, :]
        tk.dma("sp", dst, yst[s][0:nrow, :], reads=[B_yst[s]])

    tk.finish()
    return nc


def make_consts():
    c = np.zeros((128, 256), np.float32)
    c[:, 0:128] = np.eye(128, dtype=np.float32)
    c[:, 130] = EPS
    return c


def make_mconsts():
    m = np.zeros((128, MCW), np.float32)
    k = np.arange(128)[:, None]
    q = np.arange(128)[None, :]
    for g, (win, dil) in enumerate(GROUPS):
        BIG = 70.0 / (SLOPES[g * 4 + 3] * dil)
        m[:, g * 256:g * 256 + 128] = np.where(q >= k, (q - k).astype(np.float32), BIG)
        m[:, g * 256 + 128:(g + 1) * 256] = np.where(q <= k, (128 + q - k).astype(np.float32), BIG)
    for p in range(128):
        for c in range(2):
            w = POOL_WINDOWS[2 * c + p // 64]
            for t in range(16):
                m[p, 768 + c * 16 + t] = 1.0 / min(w, t + 1)
    for g, (win, dil) in enumerate(GROUPS):
        for hh in range(4):
            m[:, 800 + g * 4 + hh] = -SLOPES[g * 4 + hh] * dil * (128 - np.arange(128))
    m[0:16, 816:832] = np.eye(16, dtype=np.float32)
    m[:, 832 + 15] = 1.0
    return m


def shard_inputs(inp):
    consts = make_consts()
    mconsts = make_mconsts()
    maps = []
    for c in range(NCORES):
        sl = slice(NS * c, NS * (c + 1))
        m = {
            "x_p": np.ascontiguousarray(inp["x_prompt"][c]),
            "x_s": np.ascontiguousarray(inp["x_sample"][sl, 0, :]),
            "c_all": np.ascontiguousarray(np.concatenate([inp["c_prompt"][c:c + 1], inp["c_sample"][sl]], axis=0)),
            "state_pool": np.ascontiguousarray(inp["state_pool"][:, sl]).reshape(DEPTH, NS * 15, 256),
            "w_ada": inp["w_ada"],
            "b_ada": np.ascontiguousarray(inp["b_ada"]).reshape(DEPTH, 72, 128),
            "norm_g": np.ascontiguousarray(inp["norm_g"]).reshape(DEPTH, 24, 128),
            "ffn_w1": inp["ffn_w1"],
            "ffn_w2": inp["ffn_w2"],
            "w_in": inp["w_in"],
            "qk_g": np.ascontiguousarray(np.stack([inp["q_norm_g"], inp["k_norm_g"]], axis=1)),
            "w_oa": inp["w_oa"],
            "pool_w": inp["pool_w"],
            "pool_scale": np.ascontiguousarray(inp["pool_scale"]).reshape(DEPTH, 2, 128),
            "w_op": inp["w_op"],
            "w_out": inp["w_out"],
            "consts": consts,
            "mconsts": mconsts,
        }
        for g in range(3):
            W = GROUPS[g][0]
            m[f"ck{g}"] = np.ascontiguousarray(inp[f"cache_k_g{g}"][:, sl]).reshape(DEPTH, NS, W, 256)
            m[f"cv{g}"] = np.ascontiguousarray(inp[f"cache_v_g{g}"][:, sl]).reshape(DEPTH, NS, W, 256)
        maps.append(m)
    return maps


def gather_outputs(results):
    def cat(name, axis, shape_tail=None):
        return np.stack([r[name] for r in results], axis=axis)

    y_p = np.stack([r["y_p"] for r in results], axis=0)
    y_s = np.concatenate([r["y_s"] for r in results], axis=0).reshape(NCORES * NS, 1, D)
    outs = [y_p, y_s]
    for g in range(3):
        W = GROUPS[g][0]
        outs.append(np.stack([r[f"nkp{g}"] for r in results], axis=1).reshape(DEPTH, NCORES, W, 4, HD))
        outs.append(np.stack([r[f"nvp{g}"] for r in results], axis=1).reshape(DEPTH, NCORES, W, 4, HD))
    outs.append(np.stack([r["npool_p"] for r in results], axis=1))
    for g in range(3):
        W = GROUPS[g][0]
        outs.append(np.concatenate([r[f"nks{g}"] for r in results], axis=1).reshape(DEPTH, NCORES * NS, W, 4, HD))
        outs.append(np.concatenate([r[f"nvs{g}"] for r in results], axis=1).reshape(DEPTH, NCORES * NS, W, 4, HD))
    outs.append(np.concatenate([r["npool_s"] for r in results], axis=1))
    return tuple(np.ascontiguousarray(o, dtype=np.float32) for o in outs)


def kernel(**inputs):
    inp = {k: np.asarray(v) for k, v in inputs.items()}
    nc = build_program()
    maps = shard_inputs(inp)
    res = run_bass_kernel_spmd(nc, maps, core_ids=list(range(NCORES)))
    return gather_outputs(res.results)
```

```python
import contextlib
import numpy as np
import concourse.bass as bass
import concourse.mybir as mybir
from concourse.bass_utils import run_bass_kernel_spmd

F32 = mybir.dt.float32
BF16 = mybir.dt.bfloat16
AF = mybir.ActivationFunctionType
ALU = mybir.AluOpType
AX = mybir.AxisListType

D = 1024
KC = 8
SEQ = 2048
NS = 16
NT = SEQ + NS
DFF = 2816
NJ = 22
DEPTH = 2
HD = 64
NCORES = 8
IN_W = 4608
EPS = 1e-6
GROUPS = ((128, 1), (512, 4), (2048, 16))
TT = [(0, 512), (512, 512), (1024, 512), (1536, 512), (2048, 16)]
FF_PARTS = [(0, 6), (6, 6), (12, 5), (17, 5)]
SLOPES = [float(np.power(2.0, -8.0 * h / 12.0)) for h in range(1, 13)]
NEG = -30000.0
MCW = 864
POOL_WINDOWS = (2, 4, 8, 16)


class BufT:
    __slots__ = ("name", "lw", "rd", "wsem", "wcnt", "rsem", "rcnt", "excl")

    def __init__(self, name, excl=False):
        self.name = name
        self.excl = excl
        self.lw = None
        self.rd = {}
        self.wsem = None
        self.wcnt = 0
        self.rsem = None
        self.rcnt = 0


class Eng:
    def __init__(self, hw, sem, name):
        self.hw = hw
        self.sem = sem
        self.cnt = 0
        self.seen = {}
        self.name = name


class TK:
    def __init__(self, nc, es):
        self.nc = nc
        self.es = es
        self.eng = {}
        for nm, hw in (("pe", nc.tensor), ("act", nc.scalar), ("dve", nc.vector),
                       ("pool", nc.gpsimd), ("sp", nc.sync)):
            self.eng[nm] = Eng(hw, es.enter_context(nc.semaphore("e_" + nm)), nm)
        self.sems = {}
        self.dram_sems = []
        self.out_events = []

    def new_sem(self, name):
        s = self.es.enter_context(self.nc.semaphore(name))
        self.sems[s.num] = s
        return s

    def _collect(self, e, reads, writes):
        need = {}

        def add(ev):
            if ev is None:
                return
            sem, val = ev
            k = sem.num
            if k not in need or need[k][1] < val:
                need[k] = (sem, val)

        for b in reads:
            add(b.lw)
            if b.excl:
                for k, ev in b.rd.items():
                    if k != e.sem.num:
                        add(ev)
        for b in writes:
            add(b.lw)
            for k, ev in b.rd.items():
                add(ev)
        out = []
        for k, (sem, val) in need.items():
            if e.seen.get(k, 0) < val:
                out.append((sem, val))
                e.seen[k] = val
        return out

    def _emit_waits(self, e, waits):
        for sem, val in waits:
            e.hw.wait_ge(sem, val)

    def op(self, en, fn, reads=(), writes=()):
        e = self.eng[en]
        self._emit_waits(e, self._collect(e, reads, writes))
        inst = fn(e.hw)
        e.cnt += 1
        inst.then_inc(e.sem, 1)
        ev = (e.sem, e.cnt)
        for b in reads:
            b.rd[e.sem.num] = ev
        for b in writes:
            b.lw = ev
            b.rd = {}
        return inst

    def group(self, en, fns, reads=(), writes=()):
        e = self.eng[en]
        self._emit_waits(e, self._collect(e, reads, writes))
        inst = None
        for fn in fns:
            inst = fn(e.hw)
        e.cnt += 1
        inst.then_inc(e.sem, 1)
        ev = (e.sem, e.cnt)
        for b in reads:
            b.rd[e.sem.num] = ev
        for b in writes:
            b.lw = ev
            b.rd = {}

    def dma(self, qn, out, in_=None, reads=(), writes=(), **kw):
        e = self.eng[qn]
        pairs = out if in_ is None else [(out, in_)]
        self._emit_waits(e, self._collect(e, reads, writes))
        if writes:
            b = writes[0]
            if b.wsem is None:
                b.wsem = self.new_sem("w_" + b.name)
            for o, i in pairs:
                e.hw.dma_start(out=o, in_=i, **kw).then_inc(b.wsem, 16)
                b.wcnt += 16
            ev = (b.wsem, b.wcnt)
            for w in writes:
                w.lw = ev
                w.rd = {}
            for r in reads:
                r.rd[b.wsem.num] = ev
        elif reads:
            b = reads[0]
            if b.rsem is None:
                b.rsem = self.new_sem("r_" + b.name)
            for o, i in pairs:
                e.hw.dma_start(out=o, in_=i, **kw).then_inc(b.rsem, 16)
                b.rcnt += 16
            ev = (b.rsem, b.rcnt)
            for r in reads:
                r.rd[b.rsem.num] = ev
            self.out_events.append(ev)
        else:
            if not self.dram_sems:
                self.dram_sems.append([self.new_sem("dram"), 0])
            ds = self.dram_sems[0]
            for o, i in pairs:
                e.hw.dma_start(out=o, in_=i, **kw).then_inc(ds[0], 16)
                ds[1] += 16

    def finish(self):
        e = self.eng["sp"]
        last = {}
        for sem, val in self.out_events:
            if sem.num not in last or last[sem.num][1] < val:
                last[sem.num] = (sem, val)
        for ds in self.dram_sems:
            last[ds[0].num] = (ds[0], ds[1])
        for sem, val in last.values():
            e.hw.wait_ge(sem, val)
        for nm in ("pe", "act", "dve", "pool"):
            o = self.eng[nm]
            if o.cnt:
                e.hw.wait_ge(o.sem, o.cnt)


class StopBuild(Exception):
    pass


class Ring:
    def __init__(self, bufs):
        self.bufs = bufs
        self.i = 0

    def get(self):
        b = self.bufs[self.i % len(self.bufs)]
        self.i += 1
        return b


def build_program(stop_after=(None, None)):
    nc = bass.Bass("TRN2", target_bir_lowering=False)
    es = contextlib.ExitStack()
    nc._es = es
    tk = TK(nc, es)

    _bufs = {}

    def Buf(name, excl=False):
        if name not in _bufs:
            _bufs[name] = BufT(name, excl)
        return _bufs[name]

    def din(name, shape):
        return nc.dram_tensor(name, list(shape), F32, kind="ExternalInput").ap()

    def dout(name, shape):
        return nc.dram_tensor(name, list(shape), F32, kind="ExternalOutput").ap()

    x_p = din("x_p", [SEQ, D])
    x_s = din("x_s", [NS, D])
    c_all = din("c_all", [NS + 1, D])
    cache_k = [din(f"ck{g}", [DEPTH, NS, GROUPS[g][0], 256]) for g in range(3)]
    cache_v = [din(f"cv{g}", [DEPTH, NS, GROUPS[g][0], 256]) for g in range(3)]
    state_pool = din("state_pool", [DEPTH, NS * 15, 256])
    w_ada = din("w_ada", [DEPTH, D, 9 * D])
    b_ada = din("b_ada", [DEPTH, 72, 128])
    norm_g = din("norm_g", [DEPTH, 24, 128])
    ffn_w1 = din("ffn_w1", [DEPTH, 2, D, 2 * DFF])
    ffn_w2 = din("ffn_w2", [DEPTH, 2, DFF, D])
    w_in = din("w_in", [DEPTH, D, IN_W])
    qk_g = din("qk_g", [DEPTH, 2, HD])
    w_oa = din("w_oa", [DEPTH, 768, D])
    pool_w = din("pool_w", [DEPTH, 4, 64, 64])
    pool_scale = din("pool_scale", [DEPTH, 2, 128])
    w_op = din("w_op", [DEPTH, 256, D])
    w_out = din("w_out", [DEPTH, D, D])
    consts = din("consts", [128, 256])
    mconsts = din("mconsts", [128, MCW])

    y_p = dout("y_p", [SEQ, D])
    y_s = dout("y_s", [NS, D])
    nk_p = [dout(f"nkp{g}", [DEPTH, GROUPS[g][0], 256]) for g in range(3)]
    nv_p = [dout(f"nvp{g}", [DEPTH, GROUPS[g][0], 256]) for g in range(3)]
    npool_p = dout("npool_p", [DEPTH, 15, 256])
    nk_s = [dout(f"nks{g}", [DEPTH, NS, GROUPS[g][0], 256]) for g in range(3)]
    nv_s = [dout(f"nvs{g}", [DEPTH, NS, GROUPS[g][0], 256]) for g in range(3)]
    npool_s = dout("npool_s", [DEPTH, NS, 15, 256])

    def sb(name, shape, dt=F32):
        return es.enter_context(nc.sbuf_tensor(name, list(shape), dt))

    xT = sb("xT", [128, KC, NT])
    hT = sb("hT", [128, KC, NT], BF16)
    modT1 = sb("modT", [128, 1, 72, NS + 1])
    scT = sb("scT", [128, KC, NS + 1], BF16)
    Amod = sb("Amod", [128, 3, KC, NS + 1])
    gT = sb("gT", [128, DEPTH, 24])
    badaT = sb("badaT", [128, DEPTH, 72])
    cst = sb("cst", [128, 256])
    ident_b = sb("ident_b", [128, 128], BF16)
    ones_b = sb("ones_b", [128, 128], BF16)
    ARENA_BYTES = 100 * 1024
    arena = sb("arena", [128, ARENA_BYTES // 4])
    arena_off = [0]

    B_xT = [[Buf(f"xT{m}_{t}") for t in range(5)] for m in range(KC)]
    B_hT = [[Buf(f"hT{m}_{t}") for t in range(5)] for m in range(KC)]
    B_modT = Buf("modT")
    B_Amod = Buf("Amod")
    B_gT = Buf("gT")
    B_bada = Buf("bada")
    B_cst = Buf("cst")
    B_identb = Buf("identb")
    B_onesb = Buf("onesb")

    def arena_reset():
        arena_off[0] = 0

    def arena_alloc(shape, dt):
        esz = 4 if dt == F32 else 2
        n = int(np.prod(shape[1:]))
        nbytes = (n * esz + 3) // 4 * 4
        o = arena_off[0]
        assert o + nbytes <= ARENA_BYTES, (o, nbytes, shape)
        arena_off[0] = o + nbytes
        v = arena[:, o // 4:(o + nbytes) // 4]
        if dt != F32:
            v = v.bitcast(dt)
        v = v[:, 0:n]
        if len(shape) > 2:
            names = " ".join(f"d{i}" for i in range(len(shape) - 1))
            kw = {f"d{i}": shape[i + 1] for i in range(len(shape) - 1)}
            v = v.rearrange(f"p ({names}) -> p {names}", **kw)
        return v[0:shape[0]]

    psum = [es.enter_context(nc.psum_tensor(f"ps{i}", [128, 512], F32)) for i in range(8)]
    B_ps = [Buf(f"ps{i}", excl=True) for i in range(8)]
    ps_i = [0]

    def ps_get():
        i = ps_i[0] % 8
        ps_i[0] += 1
        return psum[i], B_ps[i]

    def barrier():
        names = ("pe", "act", "dve", "pool", "sp")
        last = {}
        for sem, val in tk.out_events:
            if sem.num not in last or last[sem.num][1] < val:
                last[sem.num] = (sem, val)
        for a in names:
            ea = tk.eng[a]
            for k, (sem, val) in last.items():
                if ea.seen.get(k, 0) < val:
                    ea.hw.wait_ge(sem, val)
                    ea.seen[k] = val
            for b in names:
                if a == b:
                    continue
                eb = tk.eng[b]
                if eb.cnt and ea.seen.get(eb.sem.num, 0) < eb.cnt:
                    ea.hw.wait_ge(eb.sem, eb.cnt)
                    ea.seen[eb.sem.num] = eb.cnt

    tk.dma("sp", cst[:], consts[:, :], writes=[B_cst])
    IDF = cst[:, 0:128]
    tk.op("dve", lambda h: h.tensor_copy(out=ident_b[:], in_=cst[:, 0:128]), [B_cst], [B_identb])
    tk.op("dve", lambda h: h.memset(ones_b[:], 1.0), [], [B_onesb])

    B_scT = Buf("scT")

    class _ModView:
        def __getitem__(self, idx):
            idx = list(idx)
            idx[1] = 0
            return modT1[tuple(idx)]
    modT = _ModView()

    def adaln_setup():
        arena_reset()
        call = arena_alloc([NS + 1, D], F32)
        B_call = Buf("call")
        scb = arena_alloc([NS + 1, D], BF16)
        B_scb = Buf("scb")
        tmp24 = arena_alloc([72, 128], F32)
        B_tmp24 = Buf("tmp24")
        tk.dma("sp", call, c_all[:, :], writes=[B_call])
        tk.op("act", lambda h: h.activation(out=scb, in_=call, func=AF.Silu), [B_call], [B_scb])
        pt, bpt = ps_get()
        ptb = pt[:].bitcast(BF16)
        tk.group("pe", [
            (lambda h, kc=kc: h.transpose(out=ptb[:, kc * 32:kc * 32 + NS + 1], in_=scb[:, kc * 128:(kc + 1) * 128],
                                          identity=ident_b[0:NS + 1, 0:NS + 1]))
            for kc in range(KC)], [B_scb, B_identb], [bpt])
        tk.op("dve", lambda h: h.tensor_copy(
            out=scT[:], in_=ptb[:, 0:KC * 32].rearrange("p (k c) -> p k c", c=32)[:, :, 0:NS + 1]), [bpt], [B_scT])
        for l in range(DEPTH):
            tk.dma("sp", tmp24[0:72, :], b_ada[l], writes=[B_tmp24])
            pt, bpt = ps_get()
            tk.group("pe", [lambda h: h.transpose(out=pt[:, 0:72], in_=tmp24[0:72, :], identity=IDF[0:72, 0:72])],
                     [B_tmp24, B_cst], [bpt])
            tk.op("dve", lambda h: h.tensor_copy(out=badaT[:, l, :], in_=pt[:, 0:72]), [bpt], [B_bada])
            tk.dma("sp", tmp24[0:24, :], norm_g[l], writes=[B_tmp24])
            pt, bpt = ps_get()
            tk.group("pe", [lambda h: h.transpose(out=pt[:, 0:24], in_=tmp24[0:24, :], identity=IDF[0:24, 0:24])],
                     [B_tmp24, B_cst], [bpt])
            tk.op("dve", lambda h: h.tensor_copy(out=gT[:, l, :], in_=pt[:, 0:24]), [bpt], [B_gT])

    def adaln(l):
        barrier()
        arena_reset()
        wada = [arena_alloc([128, KC, 512], BF16) for _ in range(3)]
        B_wada = [Buf(f"wada{i}") for i in range(3)]
        mtok = [arena_alloc([NS + 1, 512], F32) for _ in range(2)]
        B_mtok = [Buf(f"mtok{i}") for i in range(2)]
        for pc in range(18):
            slot = pc % 3
            tk.dma("pool", wada[slot],
                   w_ada[l, :, pc * 512:(pc + 1) * 512].rearrange("(k p) c -> p k c", p=128),
                   writes=[B_wada[slot]])
            pm, bpm = ps_get()
            tk.group("pe", [
                (lambda h, kc=kc: h.matmul(pm[0:NS + 1, :], lhsT=scT[:, kc, :], rhs=wada[slot][:, kc, :],
                                           start=(kc == 0), stop=(kc == KC - 1)))
                for kc in range(KC)], [B_scT, B_wada[slot]], [bpm])
            ms = pc % 2
            tk.op("act", lambda h: h.copy(out=mtok[ms], in_=pm[0:NS + 1, :]), [bpm], [B_mtok[ms]])
            pt, bpt = ps_get()
            tk.group("pe", [
                (lambda h, q=q: h.transpose(out=pt[:, q * 32:q * 32 + NS + 1], in_=mtok[ms][:, q * 128:(q + 1) * 128],
                                            identity=IDF[0:NS + 1, 0:NS + 1]))
                for q in range(4)], [B_mtok[ms], B_cst], [bpt])
            tk.op("dve", lambda h: h.tensor_tensor(
                out=modT[:, l, pc * 4:(pc + 1) * 4, :],
                in0=pt[:, 0:128].rearrange("p (q c) -> p q c", c=32)[:, :, 0:NS + 1],
                in1=badaT[:, l, pc * 4:(pc + 1) * 4].unsqueeze(2).to_broadcast([128, 4, NS + 1]),
                op=ALU.add), [bpt, B_bada], [B_modT])
        for i in (0, 2):
            tk.op("dve", lambda h: h.tensor_scalar(
                out=modT[:, l, (3 * i + 2) * 8:(3 * i + 3) * 8, :], in0=modT[:, l, (3 * i + 2) * 8:(3 * i + 3) * 8, :],
                scalar1=0.5, scalar2=None, op0=ALU.mult), [B_modT], [B_modT])

    adaln_setup()

    bulk = []
    for s2_ in range(NS):
        for l_ in range(DEPTH):
            for g_, (win_, dil_) in enumerate(GROUPS):
                bulk.append((nk_s[g_][l_, s2_, 0:win_ - 1, :], cache_k[g_][l_, s2_, 1:win_, :]))
                bulk.append((nv_s[g_][l_, s2_, 0:win_ - 1, :], cache_v[g_][l_, s2_, 1:win_, :]))
    for l_ in range(DEPTH):
        bulk.append((npool_s[l_, :, 0:14, :], state_pool[l_].rearrange("(s r) c -> s r c", r=15)[:, 1:15, :]))
    bulk_ctr = [0]

    def bulk_tick(every):
        bulk_ctr[0] += 1
        if bulk and bulk_ctr[0] % every == 0:
            o, i = bulk.pop(0)
            tk.dma("sp", o, i)

    barrier()
    arena_reset()
    xst = [arena_alloc([128, D], F32) for _ in range(2)]
    B_xst = [Buf(f"xst{i}") for i in range(2)]
    for tt in range(17):
        s = tt % 2
        nrow = 128 if tt < 16 else NS
        src = x_p[tt * 128:(tt + 1) * 128, :] if tt < 16 else x_s[:, :]
        tk.dma("sp", xst[s][0:nrow, :], src, writes=[B_xst[s]])
        T = tt // 4 if tt < 16 else 4
        col0 = tt * 128
        for half in range(2):
            pt, bpt = ps_get()
            tk.group("pe", [
                (lambda h, q=q: h.transpose(out=pt[:, q * 128:q * 128 + nrow],
                                            in_=xst[s][0:nrow, (half * 4 + q) * 128:(half * 4 + q + 1) * 128],
                                            identity=IDF[0:nrow, 0:nrow]))
                for q in range(4)], [B_xst[s], B_cst], [bpt])
            eng = "act" if half == 0 else "dve"
            if eng == "act":
                fn = lambda h: h.copy(out=xT[:, half * 4:half * 4 + 4, col0:col0 + nrow],
                                      in_=pt[:].rearrange("p (q c) -> p q c", c=128)[:, :, 0:nrow])
            else:
                fn = lambda h: h.tensor_copy(out=xT[:, half * 4:half * 4 + 4, col0:col0 + nrow],
                                             in_=pt[:].rearrange("p (q c) -> p q c", c=128)[:, :, 0:nrow])
            tk.op(eng, fn, [bpt], [B_xT[half * 4 + q][T] for q in range(4)])

    def prologue(l, i):
        tk.op("dve", lambda h: h.scalar_tensor_tensor(
            out=Amod[:, i], in0=modT[:, l, (3 * i + 1) * 8:(3 * i + 2) * 8, :], scalar=1.0,
            in1=gT[:, l, i * 8:(i + 1) * 8].unsqueeze(2).to_broadcast([128, 8, NS + 1]),
            op0=ALU.add, op1=ALU.mult), [B_modT, B_gT], [B_Amod])
        sq = [arena_alloc([128, KC, 512], BF16) for _ in range(1)]
        B_sq = [Buf(f"sq{k}") for k in range(1)]
        rs = [arena_alloc([128, 512], F32) for _ in range(2)]
        B_rs = [Buf(f"rs{k}") for k in range(2)]
        tm = [arena_alloc([128, 512], F32) for _ in range(3)]
        B_tm = [Buf(f"tm{k}") for k in range(3)]
        tmi = 0
        for T, (c0, n) in enumerate(TT):
            s = T % 2
            s0 = 0
            tk.op("act", lambda h: h.activation(out=sq[s0][:, :, 0:n], in_=xT[:, :, c0:c0 + n], func=AF.Square),
                  [B_xT[m][T] for m in range(KC)], [B_sq[s0]])
            pq, bpq = ps_get()
            tk.group("pe", [
                (lambda h, kc=kc: h.matmul(pq[:, 0:n], lhsT=ones_b[:], rhs=sq[s0][:, kc, 0:n],
                                           start=(kc == 0), stop=(kc == KC - 1)))
                for kc in range(KC)], [B_sq[s0], B_onesb], [bpq])
            tk.op("act", lambda h: h.activation(out=rs[s][:, 0:n], in_=pq[:, 0:n], func=AF.Sqrt,
                                                scale=1.0 / D, bias=cst[:, 130:131]), [bpq, B_cst], [B_rs[s]])
            tk.op("dve", lambda h: h.reciprocal(out=rs[s][:, 0:n], in_=rs[s][:, 0:n]), [B_rs[s]], [B_rs[s]])
            if T < 4:
                for kc in range(KC):
                    k3 = tmi % 3
                    tmi += 1
                    tk.op("dve", lambda h: h.scalar_tensor_tensor(
                        out=tm[k3][:, 0:n], in0=xT[:, kc, c0:c0 + n], scalar=Amod[:, i, kc, 0:1],
                        in1=rs[s][:, 0:n], op0=ALU.mult, op1=ALU.mult),
                        [B_xT[kc][T], B_Amod, B_rs[s]], [B_tm[k3]])
                    tk.op("act", lambda h: h.activation(
                        out=hT[:, kc, c0:c0 + n], in_=tm[k3][:, 0:n], func=AF.Identity,
                        bias=modT[:, l, 3 * i * 8 + kc, 0:1], scale=1.0),
                        [B_tm[k3], B_modT], [B_hT[kc][T]])
            else:
                t3 = arena_alloc([128, KC, NS], F32)
                B_t3 = Buf("t3")
                tk.op("dve", lambda h: h.tensor_tensor(
                    out=t3, in0=xT[:, :, c0:c0 + n],
                    in1=rs[s][:, 0:n].unsqueeze(1).to_broadcast([128, KC, NS]), op=ALU.mult),
                    [B_xT[m][T] for m in range(KC)] + [B_rs[s]], [B_t3])
                tk.op("dve", lambda h: h.tensor_tensor(out=t3, in0=t3, in1=Amod[:, i, :, 1:NS + 1], op=ALU.mult),
                      [B_t3, B_Amod], [B_t3])
                tk.op("dve", lambda h: h.tensor_tensor(
                    out=hT[:, :, c0:c0 + n], in0=t3, in1=modT[:, l, 3 * i * 8:3 * i * 8 + 8, 1:NS + 1], op=ALU.add),
                    [B_t3, B_modT], [B_hT[m][T] for m in range(KC)])

    def resid_update(l, gi, pm, bpm, m, T, c0, n, tmpb, B_tmpb):
        grow = (3 * gi + 2) * 8 + m
        if T < 4:
            tk.op("dve", lambda h: h.scalar_tensor_tensor(
                out=xT[:, m, c0:c0 + n], in0=pm[:, 0:n], scalar=modT[:, l, grow, 0:1],
                in1=xT[:, m, c0:c0 + n], op0=ALU.mult, op1=ALU.add),
                [bpm, B_modT, B_xT[m][T]], [B_xT[m][T]])
        else:
            tk.op("dve", lambda h: h.tensor_tensor(out=tmpb[:, 0:n], in0=pm[:, 0:n],
                                                   in1=modT[:, l, grow, 1:NS + 1], op=ALU.mult),
                  [bpm, B_modT], [B_tmpb])
            tk.op("dve", lambda h: h.tensor_tensor(out=xT[:, m, c0:c0 + n], in0=tmpb[:, 0:n],
                                                   in1=xT[:, m, c0:c0 + n], op=ALU.add),
                  [B_tmpb, B_xT[m][T]], [B_xT[m][T]])

    def ffn(l, which):
        i = 0 if which == 0 else 2
        barrier()
        arena_reset()
        prologue(l, i)
        w1s = [arena_alloc([128, KC, 2, 384], BF16) for _ in range(2)]
        B_w1s = [Buf(f"w1s{k}") for k in range(2)]
        w2s = [arena_alloc([128, 6, D], BF16) for _ in range(2)]
        B_w2s = [Buf(f"w2s{k}") for k in range(2)]
        actp = arena_alloc([128, 6, NT], BF16)
        B_act = [[Buf(f"act{j}_{t}") for t in range(5)] for j in range(6)]
        sg = [arena_alloc([128, 512], F32) for _ in range(2)]
        B_sg = [Buf(f"sg{k}") for k in range(2)]
        tmpb = arena_alloc([128, NS], F32)
        B_tmpb = Buf("tmpb")
        w1v = ffn_w1[l, which].rearrange("(k p) (g c) -> p k g c", p=128, g=2)
        w2v = ffn_w2[l, which].rearrange("(j p) c -> p j c", p=128)
        pi = 0
        sgi = 0
        for part, (j0, nj) in enumerate(FF_PARTS):
            ws = part % 2
            tk.dma("pool", w2s[ws][:, 0:nj, :], w2v[:, j0:j0 + nj, :], writes=[B_w2s[ws]])
            jj = 0
            while jj < nj:
                npc = min(3, nj - jj)
                s1 = pi % 2
                pi += 1
                tk.dma("pool", [(w1s[s1][:, :, g2, 0:npc * 128],
                                 w1v[:, :, g2, (j0 + jj) * 128:(j0 + jj + npc) * 128]) for g2 in range(2)],
                       writes=[B_w1s[s1]])
                for q in range(npc):
                    jr = jj + q
                    for T, (c0, n) in enumerate(TT):
                        bulk_tick(5)
                        pg, bpg = ps_get()
                        pu, bpu = ps_get()
                        rds = [B_w1s[s1]] + [B_hT[kc][T] for kc in range(KC)]
                        tk.group("pe", [
                            (lambda h, kc=kc: h.matmul(pg[:, 0:n], lhsT=w1s[s1][:, kc, 0, q * 128:(q + 1) * 128],
                                                       rhs=hT[:, kc, c0:c0 + n], start=(kc == 0), stop=(kc == KC - 1)))
                            for kc in range(KC)], rds, [bpg])
                        tk.group("pe", [
                            (lambda h, kc=kc: h.matmul(pu[:, 0:n], lhsT=w1s[s1][:, kc, 1, q * 128:(q + 1) * 128],
                                                       rhs=hT[:, kc, c0:c0 + n], start=(kc == 0), stop=(kc == KC - 1)))
                            for kc in range(KC)], rds, [bpu])
                        k2 = sgi % 2
                        sgi += 1
                        tk.op("act", lambda h: h.activation(out=sg[k2][:, 0:n], in_=pg[:, 0:n], func=AF.Silu),
                              [bpg], [B_sg[k2]])
                        tk.op("dve", lambda h: h.tensor_tensor(out=actp[:, jr, c0:c0 + n], in0=pu[:, 0:n],
                                                               in1=sg[k2][:, 0:n], op=ALU.mult),
                              [bpu, B_sg[k2]], [B_act[jr][T]])
                jj += npc
            for m in range(KC):
                for T, (c0, n) in enumerate(TT):
                    bulk_tick(5)
                    py, bpy = ps_get()
                    tk.group("pe", [
                        (lambda h, jr=jr: h.matmul(py[:, 0:n], lhsT=w2s[ws][:, jr, m * 128:(m + 1) * 128],
                                                   rhs=actp[:, jr, c0:c0 + n], start=(jr == 0), stop=(jr == nj - 1)))
                        for jr in range(nj)], [B_w2s[ws]] + [B_act[jr][T] for jr in range(nj)], [bpy])
                    resid_update(l, i, py, bpy, m, T, c0, n, tmpb, B_tmpb)

    def mixer(l):
        barrier()
        arena_reset()
        prologue(l, 1)
        barrier()
        arena_reset()
        i_sub = 1
        mc = arena_alloc([128, MCW], F32)
        B_mc = Buf("mc")
        tk.dma("sp", mc, mconsts[:, :], writes=[B_mc])
        DISTG = [mc[:, g_ * 256:(g_ + 1) * 256] for g_ in range(3)]
        INVC = mc[:, 768:800].rearrange("p (c t) -> p c t", t=16)
        SBIAS = mc[:, 800:812]
        EYE16 = mc[0:NS, 816:832]
        zib = arena_alloc([128, 32], BF16)
        B_zib = Buf("zib")
        tk.op("dve", lambda h: h.tensor_copy(out=zib, in_=mc[:, 832:864]), [B_mc], [B_zib])
        pool_yT = arena_alloc([128, 2, NT], BF16)
        B_py = [Buf(f"py{t}") for t in range(5)]
        mark1 = arena_off[0]

        uT = arena_alloc([128, 2, NT], F32)
        B_uT = [Buf("uT0"), Buf("uT1")]
        wu = arena_alloc([128, KC, 256], BF16)
        B_wu = Buf("wu")
        tk.dma("pool", wu, w_in[l, :, 2304:2560].rearrange("(k p) c -> p k c", p=128), writes=[B_wu])
        for c in range(2):
            for T, (c0, n) in enumerate(TT):
                pu, bpu = ps_get()
                tk.group("pe", [
                    (lambda h, kc=kc: h.matmul(pu[:, 0:n], lhsT=wu[:, kc, c * 128:(c + 1) * 128],
                                               rhs=hT[:, kc, c0:c0 + n], start=(kc == 0), stop=(kc == KC - 1)))
                    for kc in range(KC)], [B_wu] + [B_hT[kc][T] for kc in range(KC)], [bpu])
                tk.op("act", lambda h: h.copy(out=uT[:, c, c0:c0 + n], in_=pu[:, 0:n]), [bpu], [B_uT[c]])
        npst = arena_alloc([NS, 2, 256], F32)
        B_npst = Buf("npst")
        pt, bpt = ps_get()
        tk.group("pe", [
            (lambda h, c=c: h.transpose(out=pt[0:15, c * 128:(c + 1) * 128], in_=uT[:, c, SEQ - 15:SEQ], identity=IDF))
            for c in range(2)] + [
            (lambda h, c=c: h.transpose(out=pt[0:NS, 256 + c * 128:256 + (c + 1) * 128], in_=uT[:, c, SEQ:NT], identity=IDF))
            for c in range(2)], B_uT + [B_cst], [bpt])
        tk.op("act", lambda h: h.copy(out=npst[:, :, :], in_=pt[0:NS, :].rearrange("p (a c) -> p a c", c=256)), [bpt], [B_npst])
        tk.dma("sp", [(npool_p[l], npst[0:15, 0, :]), (npool_s[l, :, 14, :], npst[0:NS, 1, :])], reads=[B_npst])
        stg = arena_alloc([120, 2, 256], F32)
        B_stg = Buf("stg")
        tk.dma("sp", stg, state_pool[l].rearrange("(h p) c -> p h c", p=120), writes=[B_stg])
        stT = arena_alloc([128, 2, 240], F32)
        B_stT = Buf("stT")
        pt, bpt = ps_get()
        tk.group("pe", [
            (lambda h, c=c, hh=hh: h.transpose(out=pt[:, (c * 2 + hh) * 120:(c * 2 + hh + 1) * 120],
                                               in_=stg[:, hh, c * 128:(c + 1) * 128], identity=IDF[0:120, 0:120]))
            for c in range(2) for hh in range(2)], [B_stg, B_cst], [bpt])
        tk.op("act", lambda h: h.copy(out=stT, in_=pt[:, 0:480].rearrange("p (c x) -> p c x", x=240)), [bpt], [B_stT])
        ssum = arena_alloc([128, 2, NS], F32)
        B_ssum = Buf("ssum")
        dT = arena_alloc([128, 2, NT], BF16)
        B_dT = Buf("dT")
        for c in range(2):
            for hf in range(2):
                w = POOL_WINDOWS[2 * c + hf]
                ps_ = slice(hf * 64, (hf + 1) * 64)
                tk.op("dve", lambda h: h.tensor_reduce(
                    out=ssum[ps_, c, :], in_=stT[ps_, c, :].rearrange("p (s r) -> p s r", r=15)[:, :, 15 - (w - 1):15],
                    axis=AX.X, op=ALU.add), [B_stT], [B_ssum])
        tk.op("dve", lambda h: h.tensor_tensor(out=ssum, in0=ssum, in1=uT[:, :, SEQ:NT], op=ALU.add),
              [B_ssum] + B_uT, [B_ssum])
        for c in range(2):
            for hf in range(2):
                w = POOL_WINDOWS[2 * c + hf]
                ps_ = slice(hf * 64, (hf + 1) * 64)
                tk.op("dve", lambda h: h.scalar_tensor_tensor(
                    out=dT[ps_, c, SEQ:NT], in0=ssum[ps_, c, :], scalar=1.0 / w, in1=uT[ps_, c, SEQ:NT],
                    op0=ALU.mult, op1=ALU.subtract), [B_ssum] + B_uT, [B_dT])
        PA = arena_alloc([128, 2, SEQ], F32)
        PB = arena_alloc([128, 2, SEQ], F32)
        B_PA = Buf("PA")
        B_PB = Buf("PB")
        U = uT[:, :, 0:SEQ]
        tk.op("dve", lambda h: h.tensor_tensor(out=PA[:, :, 1:SEQ], in0=uT[:, :, 1:SEQ], in1=uT[:, :, 0:SEQ - 1], op=ALU.add),
              B_uT, [B_PA])
        tk.op("dve", lambda h: h.tensor_copy(out=PA[:, :, 0:1], in_=uT[:, :, 0:1]), B_uT, [B_PA])
        tk.op("dve", lambda h: h.tensor_tensor(out=PB[:, :, 2:SEQ], in0=PA[:, :, 2:SEQ], in1=PA[:, :, 0:SEQ - 2], op=ALU.add),
              [B_PA], [B_PB])
        tk.op("dve", lambda h: h.tensor_copy(out=PB[:, :, 0:2], in_=PA[:, :, 0:2]), [B_PA], [B_PB])
        tk.op("dve", lambda h: h.tensor_tensor(out=PA[:, 1, 4:SEQ], in0=PB[:, 1, 4:SEQ], in1=PB[:, 1, 0:SEQ - 4], op=ALU.add),
              [B_PB], [B_PA])
        tk.op("dve", lambda h: h.tensor_copy(out=PA[:, 1, 0:4], in_=PB[:, 1, 0:4]), [B_PB], [B_PA])
        tk.op("dve", lambda h: h.tensor_tensor(out=PB[64:128, 1, 8:SEQ], in0=PA[64:128, 1, 8:SEQ],
                                               in1=PA[64:128, 1, 0:SEQ - 8], op=ALU.add), [B_PA], [B_PB])
        tk.op("dve", lambda h: h.tensor_copy(out=PB[64:128, 1, 0:8], in_=PA[64:128, 1, 0:8]), [B_PA], [B_PB])
        t16 = arena_alloc([128, 16], F32)
        B_t16 = Buf("t16")
        for c in range(2):
            for hf in range(2):
                w = POOL_WINDOWS[2 * c + hf]
                ps_ = slice(hf * 64, (hf + 1) * 64)
                S = PA if hf == 0 else PB
                tk.op("dve", lambda h: h.scalar_tensor_tensor(
                    out=dT[ps_, c, 16:SEQ], in0=S[ps_, c, 16:SEQ], scalar=1.0 / w, in1=uT[ps_, c, 16:SEQ],
                    op0=ALU.mult, op1=ALU.subtract), [B_PA, B_PB] + B_uT, [B_dT])
                tk.op("dve", lambda h: h.tensor_tensor(out=t16[ps_, :], in0=S[ps_, c, 0:16], in1=INVC[ps_, c, :], op=ALU.mult),
                      [B_PA, B_PB, B_mc], [B_t16])
                tk.op("dve", lambda h: h.tensor_tensor(out=dT[ps_, c, 0:16], in0=t16[ps_, :], in1=uT[ps_, c, 0:16],
                                                       op=ALU.subtract), [B_t16] + B_uT, [B_dT])
        pwf = arena_alloc([128, 2, 128], F32)
        B_pwf = Buf("pwf")
        pwb = arena_alloc([128, 2, 128], BF16)
        B_pwb = Buf("pwb")
        tk.op("dve", lambda h: h.memset(pwf, 0.0), [], [B_pwf])
        tk.dma("sp", [(pwf[(gi % 2) * 64:(gi % 2 + 1) * 64, gi // 2, (gi % 2) * 64:(gi % 2 + 1) * 64], pool_w[l, gi])
                      for gi in range(4)], writes=[B_pwf])
        tk.op("dve", lambda h: h.tensor_copy(out=pwb, in_=pwf), [B_pwf], [B_pwb])
        psc_in = arena_alloc([2, 128], F32)
        B_pscin = Buf("pscin")
        pscT = arena_alloc([128, 2], F32)
        B_pscT = Buf("pscT")
        tk.dma("sp", psc_in, pool_scale[l], writes=[B_pscin])
        pt, bpt = ps_get()
        tk.group("pe", [lambda h: h.transpose(out=pt[:, 0:2], in_=psc_in, identity=IDF[0:2, 0:2])], [B_pscin, B_cst], [bpt])
        tk.op("dve", lambda h: h.tensor_copy(out=pscT, in_=pt[:, 0:2]), [bpt], [B_pscT])
        for c in range(2):
            for T, (c0, n) in enumerate(TT):
                pp, bpp = ps_get()
                tk.group("pe", [lambda h: h.matmul(pp[:, 0:n], lhsT=pwb[:, c, :], rhs=dT[:, c, c0:c0 + n], start=True, stop=True)],
                         [B_pwb, B_dT], [bpp])
                tk.op("act", lambda h: h.activation(out=pool_yT[:, c, c0:c0 + n], in_=pp[:, 0:n], func=AF.Copy,
                                                    scale=pscT[:, c:c + 1]), [bpp, B_pscT], [B_py[T]])

        if stop_after == ("mixP", l):
            raise StopBuild()
        barrier()
        arena_off[0] = mark1
        OT = arena_alloc([128, 6, NT], BF16)
        B_OT = [Buf(f"OT{k}") for k in range(6)]
        B_OTs = Buf("OTs")
        mark2 = arena_off[0]
        Dsum = arena_alloc([128, SEQ], F32)
        B_Ds = Buf("Dsum")
        wq = [arena_alloc([128, KC, 3, 128], BF16) for _ in range(1)]
        B_wq = [Buf("wq0")]
        qkT = arena_alloc([128, 2, SEQ], BF16)
        B_qkT = [Buf(f"qkT{s_}") for s_ in range(16)]
        Vg = arena_alloc([128, 16, 128], BF16)
        B_V = [Buf(f"V{s_}") for s_ in range(16)]
        gtmp = arena_alloc([128, 2, 64], F32)
        B_gtmp = Buf("gtmp")
        gains = arena_alloc([128, 2, 2, 64], F32)
        B_gains = Buf("gains")
        tk.dma("sp", gtmp, qk_g[l].partition_broadcast(128), writes=[B_gtmp])
        tk.op("dve", lambda h: h.tensor_scalar(out=gains[:, 0], in0=gtmp[:, 0:1, :].to_broadcast([128, 2, 64]),
                                               scalar1=0.125, scalar2=None, op0=ALU.mult), [B_gtmp], [B_gains])
        tk.op("dve", lambda h: h.tensor_copy(out=gains[:, 1], in_=gtmp[:, 1:2, :].to_broadcast([128, 2, 64])),
              [B_gtmp], [B_gains])
        nhalf = arena_alloc([128, 4], F32)
        B_nhalf = Buf("nhalf")
        tk.op("dve", lambda h: h.memset(nhalf, -0.5), [], [B_nhalf])
        NSL = 4
        sqf = [arena_alloc([128, 256], F32) for _ in range(NSL)]
        B_sqf = [Buf(f"sqf{k}") for k in range(NSL)]
        ssb = [arena_alloc([128, 4], F32) for _ in range(NSL)]
        B_ssb = [Buf(f"ssb{k}") for k in range(NSL)]
        qkn = [arena_alloc([128, 256], F32) for _ in range(NSL)]
        B_qkn = [Buf(f"qkn{k}") for k in range(NSL)]
        qkb = [arena_alloc([128, 256], BF16) for _ in range(NSL)]
        B_qkb = [Buf(f"qkb{k}") for k in range(NSL)]
        vf = [arena_alloc([128, 128], F32) for _ in range(NSL)]
        B_vf = [Buf(f"vf{k}") for k in range(NSL)]
        sbt = [arena_alloc([128, 256], F32) for _ in range(2)]
        B_sbt = [Buf("sbt0"), Buf("sbt1")]
        Pt = [arena_alloc([128, 2, 256], BF16) for _ in range(3)]
        B_Pt = [Buf(f"Pt{k}") for k in range(3)]
        qkn_s = arena_alloc([NS, 256], F32)
        B_qkns = Buf("qkn_s")
        vf_s = arena_alloc([NS, 128], F32)
        B_vfs = Buf("vf_s")
        Kc = arena_alloc([128, NS, 128], BF16)
        B_Kc = Buf("Kc")
        Vc = arena_alloc([128, NS, 128], BF16)
        B_Vc = Buf("Vc")
        qdiag = arena_alloc([NS, NS, 128], BF16)
        B_qdiag = Buf("qdiag")
        prod = arena_alloc([128, 512], F32)
        B_prod = Buf("prod")
        scs = arena_alloc([128, NS, 2], F32)
        B_scs = Buf("scs")
        Ps = arena_alloc([128, NS, 2], BF16)
        B_Ps = Buf("Ps")
        Wt = Vc
        B_Wt = B_Vc
        sfs = arena_alloc([NS, 128], F32)
        B_sfs = Buf("sfs")
        pself = arena_alloc([NS, 2], F32)
        B_pself = Buf("pself")
        numS = arena_alloc([NS, 3, 128], F32)
        B_numS = Buf("numS")
        denS = arena_alloc([NS, 2], F32)
        B_denS = Buf("denS")
        oSb = arena_alloc([NS, 3, 128], BF16)
        B_oSb = Buf("oSb")
        tsl = 0
        sbi = 0
        for pair in range(2):
            for g, (win, dil) in enumerate(GROUPS):
                nblk = SEQ // dil // 128
                pg_i = pair * 3 + g
                ws = 0
                colbase = g * 256 + pair * 128
                tk.dma("pool", [(wq[ws][:, :, wh, :],
                                 w_in[l, :, wh * 768 + colbase:wh * 768 + colbase + 128].rearrange("(k p) c -> p k c", p=128))
                                for wh in range(3)], writes=[B_wq[ws]])
                tk.dma("pool", Kc, cache_k[g][l].rearrange("s (j d) c -> j s d c", d=dil)[:, :, 0, pair * 128:(pair + 1) * 128],
                       writes=[B_Kc])
                tk.dma("pool", Vc, cache_v[g][l].rearrange("s (j d) c -> j s d c", d=dil)[:, :, 0, pair * 128:(pair + 1) * 128],
                       writes=[B_Vc])

                def tile_cols(s_):
                    if g == 0:
                        r, blk = 0, s_
                    elif g == 1:
                        r, blk = s_ // 4, s_ % 4
                    else:
                        r, blk = s_, 0
                    start = r + dil * 128 * blk
                    return r, blk, start

                for s_ in range(17):
                    k2 = tsl % NSL
                    tsl += 1
                    if s_ < 16:
                        r, blk, start = tile_cols(s_)
                        nrow = 128
                        cols = slice(start, start + dil * 127 + 1, dil)
                        hrd = [B_hT[kc][T] for kc in range(KC) for T in range(4)]
                    else:
                        nrow = NS
                        cols = slice(SEQ, NT)
                        hrd = [B_hT[kc][4] for kc in range(KC)]
                    pq, bpq = ps_get()
                    tk.group("pe", [
                        (lambda h, kc=kc: h.matmul(pq[0:nrow, 0:256], lhsT=hT[:, kc, cols], rhs=wq[ws][:, kc, 0:2, :],
                                                   start=(kc == 0), stop=(kc == KC - 1)))
                        for kc in range(KC)] + [
                        (lambda h, kc=kc: h.matmul(pq[0:nrow, 256:384], lhsT=hT[:, kc, cols], rhs=wq[ws][:, kc, 2, :],
                                                   start=(kc == 0), stop=(kc == KC - 1)))
                        for kc in range(KC)], [B_wq[ws]] + hrd, [bpq])
                    tk.op("act", lambda h: h.activation(out=sqf[k2][0:nrow, :], in_=pq[0:nrow, 0:256], func=AF.Square),
                          [bpq], [B_sqf[k2]])
                    tk.op("dve", lambda h: h.tensor_reduce(out=ssb[k2][0:nrow, :],
                                                           in_=sqf[k2][0:nrow, :].rearrange("p (a d) -> p a d", d=64),
                                                           axis=AX.X, op=ALU.add), [B_sqf[k2]], [B_ssb[k2]])
                    tk.op("act", lambda h: h.activation(out=ssb[k2][0:nrow, :], in_=ssb[k2][0:nrow, :], func=AF.Sqrt,
                                                        scale=1.0 / 64, bias=cst[0:nrow, 130:131]), [B_ssb[k2], B_cst], [B_ssb[k2]])
                    tk.op("dve", lambda h: h.reciprocal(out=ssb[k2][0:nrow, :], in_=ssb[k2][0:nrow, :]), [B_ssb[k2]], [B_ssb[k2]])
                    qdst = qkn[k2] if s_ < 16 else qkn_s
                    B_qdst = B_qkn[k2] if s_ < 16 else B_qkns
                    tk.op("dve", lambda h: h.tensor_tensor(
                        out=qdst[0:nrow, :].rearrange("p (a d) -> p a d", d=64),
                        in0=pq[0:nrow, 0:256].rearrange("p (a d) -> p a d", d=64),
                        in1=ssb[k2][0:nrow, :].unsqueeze(2).to_broadcast([nrow, 4, 64]), op=ALU.mult),
                        [bpq, B_ssb[k2]], [B_qdst])
                    tk.op("dve", lambda h: h.tensor_tensor(out=qdst[0:nrow, :], in0=qdst[0:nrow, :],
                                                           in1=gains[0:nrow].rearrange("p a b d -> p (a b d)"), op=ALU.mult),
                          [B_qdst, B_gains], [B_qdst])
                    if s_ < 16:
                        tk.op("act", lambda h: h.copy(out=qkb[k2], in_=qkn[k2]), [B_qkn[k2]], [B_qkb[k2]])
                        ptq, bptq = ps_get()
                        ptb = ptq[:].bitcast(BF16)
                        tk.group("pe", [
                            (lambda h, a=a: h.transpose(out=ptb[:, a * 128:(a + 1) * 128], in_=qkb[k2][:, a * 128:(a + 1) * 128],
                                                        identity=ident_b[:]))
                            for a in range(2)], [B_qkb[k2], B_identb], [bptq])
                        tk.op("act", lambda h: h.copy(out=qkT[:, :, s_ * 128:(s_ + 1) * 128],
                                                      in_=ptb[:, 0:256].rearrange("p (a c) -> p a c", c=128)),
                              [bptq], [B_qkT[s_]])
                        tk.op("act", lambda h: h.copy(out=Vg[:, s_, :], in_=pq[:, 256:384]), [bpq], [B_V[s_]])
                        is_out = (g == 0 and s_ == 15) or (g == 1 and s_ % 4 == 3) or g == 2
                        if is_out:
                            tk.op("dve", lambda h: h.tensor_copy(out=vf[k2], in_=pq[:, 256:384]), [bpq], [B_vf[k2]])
                            kdst = nk_p[g][l].rearrange("(i d) c -> d i c", d=dil)[r, :, pair * 128:(pair + 1) * 128]
                            vdst = nv_p[g][l].rearrange("(i d) c -> d i c", d=dil)[r, :, pair * 128:(pair + 1) * 128]
                            tk.dma("sp", kdst, qkn[k2][:, 128:256], reads=[B_qkn[k2]])
                            tk.dma("sp", vdst, vf[k2][:, :], reads=[B_vf[k2]])
                    else:
                        tk.op("dve", lambda h: h.tensor_copy(out=vf_s, in_=pq[0:NS, 256:384]), [bpq], [B_vfs])
                        tk.dma("sp", nk_s[g][l, :, win - 1, pair * 128:(pair + 1) * 128], qkn_s[:, 128:256], reads=[B_qkns])
                        tk.dma("sp", nv_s[g][l, :, win - 1, pair * 128:(pair + 1) * 128], vf_s[:, :], reads=[B_vfs])

                def s_stage(s_):
                    nonlocal sbi
                    r, blk, start = tile_cols(s_)
                    nq = 2 if blk < nblk - 1 else 1
                    pk = s_ % 3
                    for hh in range(2):
                        hb = slice(hh * 64, (hh + 1) * 64)
                        hidx = g * 4 + pair * 2 + hh
                        pS, bpS = ps_get()
                        tk.group("pe", [lambda h: h.matmul(pS[:, 0:nq * 128], lhsT=qkT[hb, 1, s_ * 128:(s_ + 1) * 128],
                                                           rhs=qkT[hb, 0, s_ * 128:(s_ + nq) * 128], start=True, stop=True)],
                                 [B_qkT[s_]] + ([B_qkT[s_ + 1]] if nq == 2 else []), [bpS])
                        kb = sbi % 2
                        sbi += 1
                        tk.op("dve", lambda h: h.scalar_tensor_tensor(
                            out=sbt[kb][:, 0:nq * 128], in0=DISTG[g][:, 0:nq * 128], scalar=-SLOPES[hidx] * dil,
                            in1=pS[:, 0:nq * 128], op0=ALU.mult, op1=ALU.add), [B_mc, bpS], [B_sbt[kb]])
                        tk.op("act", lambda h: h.activation(out=Pt[pk][:, hh, 0:nq * 128], in_=sbt[kb][:, 0:nq * 128], func=AF.Exp),
                              [B_sbt[kb]], [B_Pt[pk]])

                def pv_stage(s_):
                    r, blk, start = tile_cols(s_)
                    pk = s_ % 3
                    prevP = (s_ - 1) % 3
                    pO, bpO = ps_get()
                    mms = []
                    for hh in range(2):
                        hb = slice(hh * 64, (hh + 1) * 64)
                        for kind in range(2):
                            oc = slice(kind * 128, (kind + 1) * 128)
                            if blk > 0:
                                lhs_prev = Vg[:, s_ - 1, hh * 64:(hh + 1) * 64] if kind == 0 else ones_b[:, 0:64]
                                mms.append(lambda h, lhs_prev=lhs_prev, hb=hb, oc=oc, hh=hh: h.matmul(
                                    pO[hb, oc], lhsT=lhs_prev, rhs=Pt[prevP][:, hh, 128:256], start=True, stop=False))
                            lhs_cur = Vg[:, s_, hh * 64:(hh + 1) * 64] if kind == 0 else ones_b[:, 0:64]
                            mms.append(lambda h, lhs_cur=lhs_cur, hb=hb, oc=oc, hh=hh: h.matmul(
                                pO[hb, oc], lhsT=lhs_cur, rhs=Pt[pk][:, hh, 0:128], start=(blk == 0), stop=True))
                    rds = [B_V[s_], B_Pt[pk], B_onesb]
                    if blk > 0:
                        rds += [B_V[s_ - 1], B_Pt[prevP]]
                    tk.group("pe", mms, rds, [bpO])
                    ocols = slice(start, start + dil * 127 + 1, dil)
                    tk.op("act", lambda h: h.copy(out=OT[:, g * 2 + pair, ocols], in_=pO[:, 0:128]), [bpO], [B_OT[g * 2 + pair]])
                    if g == 0:
                        tk.op("dve", lambda h: h.tensor_copy(out=Dsum[:, ocols], in_=pO[:, 128:256]), [bpO], [B_Ds])
                    else:
                        tk.op("dve", lambda h: h.tensor_tensor(out=Dsum[:, ocols], in0=pO[:, 128:256], in1=Dsum[:, ocols], op=ALU.add),
                              [bpO, B_Ds], [B_Ds])

                for s_ in range(17):
                    if s_ < 16:
                        s_stage(s_)
                    if s_ >= 1:
                        pv_stage(s_ - 1)

                if stop_after == ("mixA2", l):
                    raise StopBuild()
                tk.op("dve", lambda h: h.tensor_tensor(
                    out=qdiag, in0=qkn_s[:, 0:128].unsqueeze(1).to_broadcast([NS, NS, 128]),
                    in1=EYE16.unsqueeze(2).to_broadcast([NS, NS, 128]), op=ALU.mult), [B_qkns, B_mc], [B_qdiag])
                for b4 in range(4):
                    pb, bpb = ps_get()
                    tk.group("pe", [lambda h: h.matmul(pb[:, :], lhsT=ones_b[0:NS, :], rhs=qdiag[:, b4 * 4:(b4 + 1) * 4, :],
                                                       start=True, stop=True)], [B_qdiag, B_onesb], [bpb])
                    tk.op("dve", lambda h: h.tensor_tensor(out=prod, in0=Kc[:, b4 * 4:(b4 + 1) * 4, :].rearrange("p s c -> p (s c)"),
                                                           in1=pb[:, :], op=ALU.mult), [B_Kc, bpb], [B_prod])
                    tk.op("dve", lambda h: h.tensor_reduce(out=scs[:, b4 * 4:(b4 + 1) * 4, :].rearrange("p s a -> p (s a)"),
                                                           in_=prod.rearrange("p (a d) -> p a d", d=64), axis=AX.X, op=ALU.add),
                          [B_prod], [B_scs])
                h0 = g * 4 + pair * 2
                tk.op("dve", lambda h: h.tensor_tensor(out=scs, in0=scs,
                                                       in1=SBIAS[:, h0:h0 + 2].unsqueeze(1).to_broadcast([128, NS, 2]), op=ALU.add),
                      [B_scs, B_mc], [B_scs])
                tk.op("act", lambda h: h.activation(out=Ps, in_=scs, func=AF.Exp), [B_scs], [B_Ps])
                tk.op("dve", lambda h: h.tensor_tensor(
                    out=Wt.rearrange("p s (a d) -> p (s a) d", d=64), in0=Vc.rearrange("p s (a d) -> p (s a) d", d=64),
                    in1=Ps.rearrange("p s a -> p (s a)").unsqueeze(2).to_broadcast([128, NS * 2, 64]), op=ALU.mult),
                    [B_Vc, B_Ps], [B_Vc])
                pN, bpN = ps_get()
                tk.group("pe", [
                    (lambda h, s2=s2: h.matmul(pN[0:NS, 0:128], lhsT=zib[:, 15 - s2:31 - s2], rhs=Wt[:, s2, :],
                                               start=(s2 == 0), stop=(s2 == NS - 1))) for s2 in range(NS)] + [
                    (lambda h, s2=s2: h.matmul(pN[0:NS, 128:130], lhsT=zib[:, 15 - s2:31 - s2], rhs=Ps[:, s2, :],
                                               start=(s2 == 0), stop=(s2 == NS - 1))) for s2 in range(NS)],
                    [B_zib, B_Wt, B_Ps], [bpN])
                tk.op("dve", lambda h: h.tensor_tensor(out=sfs, in0=qkn_s[:, 0:128], in1=qkn_s[:, 128:256], op=ALU.mult),
                      [B_qkns], [B_sfs])
                tk.op("dve", lambda h: h.tensor_reduce(out=pself, in_=sfs.rearrange("p (a d) -> p a d", d=64), axis=AX.X, op=ALU.add),
                      [B_sfs], [B_pself])
                tk.op("act", lambda h: h.activation(out=pself, in_=pself, func=AF.Exp), [B_pself], [B_pself])
                tk.op("dve", lambda h: h.tensor_tensor(
                    out=sfs.rearrange("p (a d) -> p a d", d=64), in0=vf_s.rearrange("p (a d) -> p a d", d=64),
                    in1=pself.unsqueeze(2).to_broadcast([NS, 2, 64]), op=ALU.mult), [B_vfs, B_pself], [B_sfs])
                tk.op("dve", lambda h: h.tensor_tensor(out=numS[:, g, :], in0=pN[0:NS, 0:128], in1=sfs, op=ALU.add),
                      [bpN, B_sfs], [B_numS])
                if g == 0:
                    tk.op("dve", lambda h: h.tensor_tensor(out=denS, in0=pN[0:NS, 128:130], in1=pself, op=ALU.add),
                          [bpN, B_pself], [B_denS])
                else:
                    tk.op("dve", lambda h: h.tensor_tensor(out=pself, in0=pN[0:NS, 128:130], in1=pself, op=ALU.add),
                          [bpN, B_pself], [B_pself])
                    tk.op("dve", lambda h: h.tensor_tensor(out=denS, in0=denS, in1=pself, op=ALU.add),
                          [B_denS, B_pself], [B_denS])
                if stop_after == ("mixA3", l):
                    raise StopBuild()
            tk.op("dve", lambda h: h.reciprocal(out=Dsum, in_=Dsum), [B_Ds], [B_Ds])
            for g in range(3):
                tk.op("dve", lambda h: h.tensor_tensor(out=OT[:, g * 2 + pair, 0:SEQ], in0=OT[:, g * 2 + pair, 0:SEQ], in1=Dsum,
                                                       op=ALU.mult), [B_OT[g * 2 + pair], B_Ds], [B_OT[g * 2 + pair]])
            tk.op("dve", lambda h: h.reciprocal(out=denS, in_=denS), [B_denS], [B_denS])
            tk.op("dve", lambda h: h.tensor_tensor(
                out=oSb.rearrange("p g (a d) -> p g a d", d=64), in0=numS.rearrange("p g (a d) -> p g a d", d=64),
                in1=denS.unsqueeze(1).unsqueeze(3).to_broadcast([NS, 3, 2, 64]), op=ALU.mult), [B_numS, B_denS], [B_oSb])
            ptq, bptq = ps_get()
            ptb = ptq[:].bitcast(BF16)
            tk.group("pe", [
                (lambda h, g=g: h.transpose(out=ptb[:, g * 32:g * 32 + NS], in_=oSb[:, g, :], identity=ident_b[0:NS, 0:NS]))
                for g in range(3)], [B_oSb, B_identb], [bptq])
            for g in range(3):
                tk.op("act", lambda h: h.copy(out=OT[:, g * 2 + pair, SEQ:NT], in_=ptb[:, g * 32:g * 32 + NS]), [bptq], [B_OTs])

        if stop_after == ("mixA", l):
            raise StopBuild()
        barrier()
        arena_off[0] = mark2
        mT = arena_alloc([128, KC, NT], BF16)
        B_mT = [[Buf(f"mT{c}_{t}") for t in range(5)] for c in range(KC)]
        wc = [arena_alloc([128, 24, 256], BF16) for _ in range(2)]
        B_wc = [Buf("wc0"), Buf("wc1")]
        sga = [arena_alloc([128, 512], F32) for _ in range(2)]
        B_sga = [Buf("sga0"), Buf("sga1")]
        sgp = [arena_alloc([128, 512], F32) for _ in range(2)]
        B_sgp = [Buf("sgp0"), Buf("sgp1")]
        tmpb = arena_alloc([128, NS], F32)
        B_tmpb = Buf("tmpb2")
        ti = 0
        for qc in range(4):
            ws = qc % 2
            cs = slice(qc * 256, (qc + 1) * 256)
            tk.dma("pool", [
                (wc[ws][:, 0:8, :], w_in[l, :, 2560 + qc * 256:2560 + (qc + 1) * 256].rearrange("(k p) c -> p k c", p=128)),
                (wc[ws][:, 8:16, :], w_in[l, :, 3584 + qc * 256:3584 + (qc + 1) * 256].rearrange("(k p) c -> p k c", p=128)),
                (wc[ws][:, 16:22, :], w_oa[l, :, cs].rearrange("(k p) c -> p k c", p=128)),
                (wc[ws][:, 22:24, :], w_op[l, :, cs].rearrange("(k p) c -> p k c", p=128))], writes=[B_wc[ws]])
            for cc in range(2):
                c = qc * 2 + cc
                wsl = slice(cc * 128, (cc + 1) * 128)
                for T, (c0, n) in enumerate(TT):
                    hrd = [B_hT[kc][T] for kc in range(KC)]
                    pga, bpga = ps_get()
                    tk.group("pe", [
                        (lambda h, kc=kc: h.matmul(pga[:, 0:n], lhsT=wc[ws][:, kc, wsl], rhs=hT[:, kc, c0:c0 + n],
                                                   start=(kc == 0), stop=(kc == KC - 1))) for kc in range(KC)],
                        [B_wc[ws]] + hrd, [bpga])
                    pgp, bpgp = ps_get()
                    tk.group("pe", [
                        (lambda h, kc=kc: h.matmul(pgp[:, 0:n], lhsT=wc[ws][:, 8 + kc, wsl], rhs=hT[:, kc, c0:c0 + n],
                                                   start=(kc == 0), stop=(kc == KC - 1))) for kc in range(KC)],
                        [B_wc[ws]] + hrd, [bpgp])
                    pa, bpa = ps_get()
                    tk.group("pe", [
                        (lambda h, k6=k6: h.matmul(pa[:, 0:n], lhsT=wc[ws][:, 16 + k6, wsl], rhs=OT[:, k6, c0:c0 + n],
                                                   start=(k6 == 0), stop=(k6 == 5))) for k6 in range(6)],
                        [B_wc[ws]] + (B_OT if T < 4 else [B_OTs]), [bpa])
                    pp, bpp = ps_get()
                    tk.group("pe", [
                        (lambda h, k2=k2: h.matmul(pp[:, 0:n], lhsT=wc[ws][:, 22 + k2, wsl], rhs=pool_yT[:, k2, c0:c0 + n],
                                                   start=(k2 == 0), stop=(k2 == 1))) for k2 in range(2)],
                        [B_wc[ws], B_py[T]], [bpp])
                    k2 = ti % 2
                    ti += 1
                    tk.op("act", lambda h: h.activation(out=sga[k2][:, 0:n], in_=pga[:, 0:n], func=AF.Sigmoid), [bpga], [B_sga[k2]])
                    tk.op("act", lambda h: h.activation(out=sgp[k2][:, 0:n], in_=pgp[:, 0:n], func=AF.Sigmoid), [bpgp], [B_sgp[k2]])
                    tk.op("dve", lambda h: h.tensor_tensor(out=sga[k2][:, 0:n], in0=pa[:, 0:n], in1=sga[k2][:, 0:n], op=ALU.mult),
                          [bpa, B_sga[k2]], [B_sga[k2]])
                    tk.op("dve", lambda h: h.tensor_tensor(out=sgp[k2][:, 0:n], in0=pp[:, 0:n], in1=sgp[k2][:, 0:n], op=ALU.mult),
                          [bpp, B_sgp[k2]], [B_sgp[k2]])
                    tk.op("dve", lambda h: h.tensor_tensor(out=mT[:, c, c0:c0 + n], in0=sga[k2][:, 0:n], in1=sgp[k2][:, 0:n], op=ALU.add),
                          [B_sga[k2], B_sgp[k2]], [B_mT[c][T]])
        for qc in range(4):
            ws = qc % 2
            tk.dma("pool", wc[ws][:, 0:8, :], w_out[l, :, qc * 256:(qc + 1) * 256].rearrange("(k p) c -> p k c", p=128),
                   writes=[B_wc[ws]])
            for cc in range(2):
                c = qc * 2 + cc
                wsl = slice(cc * 128, (cc + 1) * 128)
                for T, (c0, n) in enumerate(TT):
                    pm, bpm = ps_get()
                    tk.group("pe", [
                        (lambda h, kc=kc: h.matmul(pm[:, 0:n], lhsT=wc[ws][:, kc, wsl], rhs=mT[:, kc, c0:c0 + n],
                                                   start=(kc == 0), stop=(kc == KC - 1))) for kc in range(KC)],
                        [B_wc[ws]] + [B_mT[kc][T] for kc in range(KC)], [bpm])
                    resid_update(l, 1, pm, bpm, c, T, c0, n, tmpb, B_tmpb)

    done = False
    for l in range(DEPTH):
        adaln(l)
        ffn(l, 0)
        if stop_after == ("ffn1", l):
            done = True
            break
        try:
            mixer(l)
        except StopBuild:
            done = True
            break
        if stop_after == ("mix", l):
            done = True
            break
        ffn(l, 1)
        if stop_after == ("ffn2", l):
            done = True
            break

    barrier()
    arena_reset()
    yst = [arena_alloc([128, D], F32) for _ in range(2)]
    B_yst = [Buf(f"yst{i}") for i in range(2)]
    for tt in range(17):
        s = tt % 2
        nrow = 128 if tt < 16 else NS
        T = tt // 4 if tt < 16 else 4
        col0 = tt * 128
        for half in range(2):
            pt, bpt = ps_get()
            tk.group("pe", [
                (lambda h, q=q: h.transpose(out=pt[0:nrow, q * 128:(q + 1) * 128],
                                            in_=xT[:, half * 4 + q, col0:col0 + nrow], identity=IDF))
                for q in range(4)], [B_xT[half * 4 + q][T] for q in range(4)] + [B_cst], [bpt])
            if half == 0:
                tk.op("act", lambda h: h.copy(out=yst[s][0:nrow, 0:512], in_=pt[0:nrow, :]), [bpt], [B_yst[s]])
            else:
                tk.op("dve", lambda h: h.tensor_copy(out=yst[s][0:nrow, 512:1024], in_=pt[0:nrow, :]), [bpt], [B_yst[s]])
        dst = y_p[tt * 128:(tt + 1) * 128, :] if tt < 16 else y_s[:, :]
        tk.dma("sp", dst, yst[s][0:nrow, :], reads=[B_yst[s]])

    while bulk:
        o_, i_ = bulk.pop(0)
        tk.dma("sp", o_, i_)
    tk.finish()
    return nc


def make_consts():
    c = np.zeros((128, 256), np.float32)
    c[:, 0:128] = np.eye(128, dtype=np.float32)
    c[:, 130] = EPS
    return c


def make_mconsts():
    m = np.zeros((128, MCW), np.float32)
    k = np.arange(128)[:, None]
    q = np.arange(128)[None, :]
    for g, (win, dil) in enumerate(GROUPS):
        BIG = 70.0 / (SLOPES[g * 4 + 3] * dil)
        m[:, g * 256:g * 256 + 128] = np.where(q >= k, (q - k).astype(np.float32), BIG)
        m[:, g * 256 + 128:(g + 1) * 256] = np.where(q <= k, (128 + q - k).astype(np.float32), BIG)
    for p in range(128):
        for c in range(2):
            w = POOL_WINDOWS[2 * c + p // 64]
            for t in range(16):
                m[p, 768 + c * 16 + t] = 1.0 / min(w, t + 1)
    for g, (win, dil) in enumerate(GROUPS):
        for hh in range(4):
            m[:, 800 + g * 4 + hh] = -SLOPES[g * 4 + hh] * dil * (128 - np.arange(128))
    m[0:16, 816:832] = np.eye(16, dtype=np.float32)
    m[:, 832 + 15] = 1.0
    return m


def shard_inputs(inp):
    consts = make_consts()
    mconsts = make_mconsts()
    maps = []
    for c in range(NCORES):
        sl = slice(NS * c, NS * (c + 1))
        m = {
            "x_p": np.ascontiguousarray(inp["x_prompt"][c]),
            "x_s": np.ascontiguousarray(inp["x_sample"][sl, 0, :]),
            "c_all": np.ascontiguousarray(np.concatenate([inp["c_prompt"][c:c + 1], inp["c_sample"][sl]], axis=0)),
            "state_pool": np.ascontiguousarray(inp["state_pool"][:, sl]).reshape(DEPTH, NS * 15, 256),
            "w_ada": inp["w_ada"],
            "b_ada": np.ascontiguousarray(inp["b_ada"]).reshape(DEPTH, 72, 128),
            "norm_g": np.ascontiguousarray(inp["norm_g"]).reshape(DEPTH, 24, 128),
            "ffn_w1": inp["ffn_w1"],
            "ffn_w2": inp["ffn_w2"],
            "w_in": inp["w_in"],
            "qk_g": np.ascontiguousarray(np.stack([inp["q_norm_g"], inp["k_norm_g"]], axis=1)),
            "w_oa": inp["w_oa"],
            "pool_w": inp["pool_w"],
            "pool_scale": np.ascontiguousarray(inp["pool_scale"]).reshape(DEPTH, 2, 128),
            "w_op": inp["w_op"],
            "w_out": inp["w_out"],
            "consts": consts,
            "mconsts": mconsts,
        }
        for g in range(3):
            W = GROUPS[g][0]
            m[f"ck{g}"] = np.ascontiguousarray(inp[f"cache_k_g{g}"][:, sl]).reshape(DEPTH, NS, W, 256)
            m[f"cv{g}"] = np.ascontiguousarray(inp[f"cache_v_g{g}"][:, sl]).reshape(DEPTH, NS, W, 256)
        maps.append(m)
    return maps


def gather_outputs(results):
    def cat(name, axis, shape_tail=None):
        return np.stack([r[name] for r in results], axis=axis)

    y_p = np.stack([r["y_p"] for r in results], axis=0)
    y_s = np.concatenate([r["y_s"] for r in results], axis=0).reshape(NCORES * NS, 1, D)
    outs = [y_p, y_s]
    for g in range(3):
        W = GROUPS[g][0]
        outs.append(np.stack([r[f"nkp{g}"] for r in results], axis=1).reshape(DEPTH, NCORES, W, 4, HD))
        outs.append(np.stack([r[f"nvp{g}"] for r in results], axis=1).reshape(DEPTH, NCORES, W, 4, HD))
    outs.append(np.stack([r["npool_p"] for r in results], axis=1))
    for g in range(3):
        W = GROUPS[g][0]
        outs.append(np.concatenate([r[f"nks{g}"] for r in results], axis=1).reshape(DEPTH, NCORES * NS, W, 4, HD))
        outs.append(np.concatenate([r[f"nvs{g}"] for r in results], axis=1).reshape(DEPTH, NCORES * NS, W, 4, HD))
    outs.append(np.concatenate([r["npool_s"] for r in results], axis=1))
    return tuple(np.ascontiguousarray(o, dtype=np.float32) for o in outs)


def kernel(**inputs):
    inp = {k: np.asarray(v) for k, v in inputs.items()}
    nc = build_program()
    maps = shard_inputs(inp)
    res = run_bass_kernel_spmd(nc, maps, core_ids=list(range(NCORES)))
    return gather_outputs(res.results)
```
